# Optimizing a Trainium2 kernel written in Bass

```python
import jax
import jax.numpy as jnp
from jax import lax
import numpy as np

D_MODEL = 1024
BATCH = 16
SEQ = 2048
DEPTH = 4
DEC_BATCH = 8
DEC_SEQ = 8192
PAST_LEN = 128

BRANCH_WIDTH = D_MODEL // 2
N_BRANCH = 4
SSD_HEAD_DIM = 64
SSD_HEADS = BRANCH_WIDTH // SSD_HEAD_DIM
SSD_GROUPS = 2
SSD_STATE = 64
SSD_CONV = 5
SSD_CHUNK = 128
SSD_CONV_CH = BRANCH_WIDTH + 2 * SSD_GROUPS * SSD_STATE
POOL_WINDOWS = (2, 4, 8, 16)
POOL_GROUPS = 4
POOL_GROUP_DIM = BRANCH_WIDTH // POOL_GROUPS
SGU_CHUNK = 128
SGU_GROUPS = 4
SGU_GROUP_DIM = BRANCH_WIDTH // SGU_GROUPS
GLA_HEADS = 4
GLA_KEY_DIM = 64
GLA_VAL_DIM = BRANCH_WIDTH // GLA_HEADS
GLA_GATE_RANK = 16
GLA_GATE_NORMALIZER = 16.0
GLA_CHUNK = 64
D_FF = 4 * D_MODEL
RMS_EPS = 1e-6
COL_SIZES = (BRANCH_WIDTH, SSD_CONV_CH, 2 * SSD_HEADS,
             BRANCH_WIDTH,
             BRANCH_WIDTH, BRANCH_WIDTH,
             GLA_HEADS * GLA_KEY_DIM, GLA_HEADS * GLA_KEY_DIM,
             BRANCH_WIDTH, BRANCH_WIDTH, 2 * GLA_GATE_RANK,
             N_BRANCH * D_MODEL)
N_IN = sum(COL_SIZES)

kernel_name = 'hybrid_bidir_ssd_pool_sgu_gla_encoder'


def _column_splits():
    return [int(c) for c in np.cumsum(np.array(COL_SIZES))[:-1]]


def _flip(t):
    return jnp.flip(t, axis=1)


def rms_norm(x, g):
    xf = x.astype(jnp.float32)
    y = xf * lax.rsqrt(jnp.mean(xf * xf, axis=-1, keepdims=True) + RMS_EPS)
    return (y * g.astype(jnp.float32)).astype(x.dtype)


def centred_depthwise_conv(x, w, b):
    pad = SSD_CONV // 2
    y = lax.conv_general_dilated(x, w.astype(x.dtype)[:, None, :], window_strides=(1,),
                                 padding=[(pad, pad)], dimension_numbers=('NWC', 'WIO', 'NWC'),
                                 feature_group_count=x.shape[-1])
    return y + b.astype(x.dtype)


def ssd_scan(x, dt, a, b_in, c_in):
    bsz, seq, n_heads, hd = x.shape
    n_groups = b_in.shape[2]
    hpg = n_heads // n_groups
    nc = seq // SSD_CHUNK
    T = SSD_CHUNK
    x = x.reshape(bsz, nc, T, n_groups, hpg, hd)
    dt = dt.reshape(bsz, nc, T, n_groups, hpg)
    bc = b_in.reshape(bsz, nc, T, n_groups, -1)
    cc = c_in.reshape(bsz, nc, T, n_groups, -1)
    acs = jnp.cumsum(dt * a.reshape(n_groups, hpg), axis=2)
    xdt = x * dt[..., None]
    causal = jnp.arange(T)[:, None] >= jnp.arange(T)[None, :]
    seg = acs[:, :, :, None] - acs[:, :, None, :]
    decay = jnp.exp(jnp.where(causal[:, :, None, None], seg, -jnp.inf))
    cb = jnp.einsum('bctgn,bcsgn->bctsg', cc, bc)
    y_diag = jnp.einsum('bctsgh,bcsghp->bctghp', cb[..., None] * decay, xdt)
    decay_to_end = jnp.exp(acs[:, :, -1:] - acs)
    states = jnp.einsum('bctgn,bctgh,bctghp->bcghpn', bc, decay_to_end, xdt)
    a_last = acs[:, :, -1]
    z = jnp.cumsum(a_last, axis=1)
    ex = (z - a_last)[:, :, None] - z[:, None, :]
    cmask = jnp.arange(nc)[:, None] > jnp.arange(nc)[None, :]
    w_chunk = jnp.exp(jnp.where(cmask[:, :, None, None], ex, -jnp.inf))
    s_in = jnp.einsum('bcdgh,bdghpn->bcghpn', w_chunk, states)
    y_off = jnp.einsum('bctgn,bcghpn,bctgh->bctghp', cc, s_in, jnp.exp(acs))
    return (y_diag + y_off).reshape(bsz, seq, n_heads, hd)


def ssd_mixer(z, xbc, dt_raw, conv_w, conv_b, a_log, dt_bias, d_skip, norm_g):
    bsz, seq, _ = z.shape
    xbc = jax.nn.silu(centred_depthwise_conv(xbc, conv_w, conv_b)).astype(jnp.float32)
    xs, bs, cs = jnp.split(xbc, [BRANCH_WIDTH, BRANCH_WIDTH + SSD_GROUPS * SSD_STATE], axis=-1)
    xs = xs.reshape(bsz, seq, SSD_HEADS, SSD_HEAD_DIM)
    bs = bs.reshape(bsz, seq, SSD_GROUPS, SSD_STATE)
    cs = cs.reshape(bsz, seq, SSD_GROUPS, SSD_STATE)
    dt = jax.nn.softplus(dt_raw.astype(jnp.float32).reshape(bsz, seq, 2, SSD_HEADS)
                         + dt_bias.astype(jnp.float32))
    a = -jnp.exp(a_log.astype(jnp.float32))
    y_f = ssd_scan(xs, dt[:, :, 0], a[0], bs, cs)
    y_b = _flip(ssd_scan(_flip(xs), _flip(dt[:, :, 1]), a[1], _flip(bs), _flip(cs)))
    y = y_f + y_b + xs * d_skip.astype(jnp.float32)[:, None]
    y = y.reshape(bsz, seq, BRANCH_WIDTH) * jax.nn.silu(z.astype(jnp.float32))
    y = rms_norm(y.reshape(bsz, seq, SSD_GROUPS, -1), norm_g.reshape(SSD_GROUPS, -1))
    return y.reshape(bsz, seq, BRANCH_WIDTH).astype(z.dtype)


def pool_mixer(p, pool_w, pool_scale):
    bsz, seq, _ = p.shape
    pf = p.astype(jnp.float32)
    csum = jnp.concatenate([jnp.zeros((bsz, 1, BRANCH_WIDTH), jnp.float32), jnp.cumsum(pf, axis=1)], axis=1)
    pos = jnp.arange(seq)
    outs = []
    for g, w in enumerate(POOL_WINDOWS):
        lo = jnp.clip(pos - w // 2, 0, seq - 1)
        hi = jnp.clip(pos + (w - 1 - w // 2), 0, seq - 1)
        sl = slice(g * POOL_GROUP_DIM, (g + 1) * POOL_GROUP_DIM)
        csg = csum[:, :, sl]
        win_sum = jnp.take(csg, hi + 1, axis=1) - jnp.take(csg, lo, axis=1)
        cnt = (hi - lo + 1).astype(jnp.float32)[None, :, None]
        outs.append(win_sum / cnt - pf[:, :, sl])
    pooled = jnp.stack(outs, axis=2)
    y = jnp.einsum('blgc,gcd->blgd', pooled, pool_w.astype(jnp.float32)).reshape(bsz, seq, BRANCH_WIDTH)
    return (y * pool_scale.astype(jnp.float32)).astype(p.dtype)


def sgu_mixer(u, v, norm_g, w_s, b_s):
    bsz, seq, _ = u.shape
    uf = jax.nn.gelu(u.astype(jnp.float32))
    vf = rms_norm(jax.nn.gelu(v.astype(jnp.float32)), norm_g)
    vf = vf.reshape(bsz, seq // SGU_CHUNK, SGU_CHUNK, SGU_GROUPS, SGU_GROUP_DIM)
    mixed = jnp.einsum('gts,bcsgd->bctgd', w_s.astype(jnp.float32), vf) \
        + b_s.astype(jnp.float32).T[None, None, :, :, None]
    return (uf * mixed.reshape(bsz, seq, BRANCH_WIDTH)).astype(u.dtype)


def gla_scan(q, k, v, g):
    bsz, seq, n_heads, kd = q.shape
    vd = v.shape[-1]
    nc = seq // GLA_CHUNK
    T = GLA_CHUNK
    q = q.reshape(bsz, nc, T, n_heads, kd)
    k = k.reshape(bsz, nc, T, n_heads, kd)
    v = v.reshape(bsz, nc, T, n_heads, vd)
    g = g.reshape(bsz, nc, T, n_heads, kd)
    bcum = jnp.cumsum(g, axis=2)
    b_last = bcum[:, :, -1]
    q_dec = q * jnp.exp(bcum)
    k_inv = k * jnp.exp(-bcum)
    k_end = k * jnp.exp(b_last[:, :, None] - bcum)
    causal = jnp.arange(T)[:, None] >= jnp.arange(T)[None, :]
    att = jnp.where(causal, jnp.einsum('bcthk,bcshk->bchts', q_dec, k_inv), 0.0)
    o_intra = jnp.einsum('bchts,bcshv->bcthv', att, v)
    states = jnp.einsum('bcthk,bcthv->bchkv', k_end, v)

    def step(s, inp):
        st, dec = inp
        return s * dec[..., None] + st, s

    init = jnp.zeros((bsz, n_heads, kd, vd), jnp.float32)
    _, s_in = lax.scan(step, init, (jnp.moveaxis(states, 1, 0), jnp.moveaxis(jnp.exp(b_last), 1, 0)))
    s_in = jnp.moveaxis(s_in, 0, 1)
    o_inter = jnp.einsum('bcthk,bchkv->bcthv', q_dec, s_in)
    return (o_intra + o_inter).reshape(bsz, seq, n_heads, vd)


def gla_mixer(q, k, v, r, g_lr, gate_w2, gate_b, norm_g):
    bsz, seq, _ = q.shape
    f32 = jnp.float32
    qf = q.astype(f32).reshape(bsz, seq, GLA_HEADS, GLA_KEY_DIM) * (GLA_KEY_DIM ** -0.5)
    kf = k.astype(f32).reshape(bsz, seq, GLA_HEADS, GLA_KEY_DIM)
    vf = v.astype(f32).reshape(bsz, seq, GLA_HEADS, GLA_VAL_DIM)
    lr = g_lr.astype(f32).reshape(bsz, seq, 2, GLA_GATE_RANK)
    gk = jax.nn.log_sigmoid(jnp.einsum('blzr,zrk->blzk', lr, gate_w2.astype(f32)) + gate_b.astype(f32))
    gk = (gk / GLA_GATE_NORMALIZER).reshape(bsz, seq, 2, GLA_HEADS, GLA_KEY_DIM)
    o_f = gla_scan(qf, kf, vf, gk[:, :, 0])
    o_b = _flip(gla_scan(_flip(qf), _flip(kf), _flip(vf), _flip(gk[:, :, 1])))
    o = rms_norm(o_f + o_b, norm_g.reshape(GLA_HEADS, GLA_VAL_DIM))
    o = o.reshape(bsz, seq, BRANCH_WIDTH) * jax.nn.silu(r.astype(f32))
    return o.astype(q.dtype)


def encoder_layer(x, norm_mix_pre, w_in, ssd_conv_w, ssd_conv_b, ssd_a_log, ssd_dt_bias, ssd_d, ssd_norm,
                  pool_w, pool_scale, sgu_norm, sgu_w, sgu_b, gla_gate_w2, gla_gate_b, gla_norm,
                  w_branch, w_out, norm_mix_post, norm_ffn_pre, w_ff1, w_ff2, norm_ffn_post):
    bsz, seq, _ = x.shape
    h = rms_norm(x, norm_mix_pre)
    proj = jnp.einsum('bld,dn->bln', h, w_in.astype(h.dtype))
    (ssd_z, ssd_xbc, ssd_dt, pool_in, sgu_u, sgu_v,
     gla_q, gla_k, gla_v, gla_r, gla_glr, gate) = jnp.split(proj, _column_splits(), axis=-1)
    branches = (
        ssd_mixer(ssd_z, ssd_xbc, ssd_dt, ssd_conv_w, ssd_conv_b, ssd_a_log, ssd_dt_bias, ssd_d, ssd_norm),
        pool_mixer(pool_in, pool_w, pool_scale),
        sgu_mixer(sgu_u, sgu_v, sgu_norm, sgu_w, sgu_b),
        gla_mixer(gla_q, gla_k, gla_v, gla_r, gla_glr, gla_gate_w2, gla_gate_b, gla_norm),
    )
    gate = jax.nn.sigmoid(gate.astype(jnp.float32).reshape(bsz, seq, N_BRANCH, D_MODEL)).astype(x.dtype)
    merged = jnp.zeros_like(x)
    for n in range(N_BRANCH):
        merged = merged + gate[:, :, n] * jnp.einsum('blw,wd->bld', branches[n], w_branch[n].astype(x.dtype))
    mix_out = jnp.einsum('bld,de->ble', merged, w_out.astype(x.dtype))
    x = x + rms_norm(mix_out, norm_mix_post)
    h2 = rms_norm(x, norm_ffn_pre)
    hidden = jnp.square(jax.nn.relu(jnp.einsum('bld,df->blf', h2, w_ff1.astype(x.dtype))))
    ff_out = jnp.einsum('blf,fd->bld', hidden, w_ff2.astype(x.dtype))
    return x + rms_norm(ff_out, norm_ffn_post)


def trunk(x, params):
    for layer in range(DEPTH):
        x = encoder_layer(x, *[p[layer] for p in params])
    return x


def setup_inputs(seed: int = 0) -> dict:
    key = jax.random.key(seed)
    ks = jax.random.split(key, 32)
    nrm = jax.random.normal
    L = DEPTH

    def gain(k, n):
        return 1.0 + 0.05 * nrm(k, (L, n), jnp.float32)

    dt0 = jnp.exp(jax.random.uniform(ks[8], (L, 2, SSD_HEADS), jnp.float32, np.log(1e-3), np.log(1e-1)))
    return {
        'x_prompt': nrm(ks[0], (BATCH, SEQ, D_MODEL), jnp.float32),
        'x_sample': nrm(ks[1], (DEC_BATCH, DEC_SEQ, D_MODEL), jnp.float32),
        'norm_mix_pre': gain(ks[2], D_MODEL),
        'w_in': nrm(ks[3], (L, D_MODEL, N_IN), jnp.float32) * D_MODEL ** -0.5,
        'ssd_conv_w': nrm(ks[4], (L, SSD_CONV, SSD_CONV_CH), jnp.float32) * SSD_CONV ** -0.5,
        'ssd_conv_b': 0.02 * nrm(ks[5], (L, SSD_CONV_CH), jnp.float32),
        'ssd_a_log': jnp.log(jax.random.uniform(ks[6], (L, 2, SSD_HEADS), jnp.float32, 1.0, 16.0)),
        'ssd_dt_bias': dt0 + jnp.log(-jnp.expm1(-dt0)),
        'ssd_d': 1.0 + 0.1 * nrm(ks[7], (L, SSD_HEADS), jnp.float32),
        'ssd_norm': gain(ks[9], BRANCH_WIDTH),
        'pool_w': nrm(ks[10], (L, POOL_GROUPS, POOL_GROUP_DIM, POOL_GROUP_DIM), jnp.float32) * POOL_GROUP_DIM ** -0.5,
        'pool_scale': gain(ks[11], BRANCH_WIDTH),
        'sgu_norm': gain(ks[12], BRANCH_WIDTH),
        'sgu_w': nrm(ks[13], (L, SGU_GROUPS, SGU_CHUNK, SGU_CHUNK), jnp.float32) * SGU_CHUNK ** -0.5,
        'sgu_b': 1.0 + 0.01 * nrm(ks[14], (L, SGU_GROUPS, SGU_CHUNK), jnp.float32),
        'gla_gate_w2': nrm(ks[15], (L, 2, GLA_GATE_RANK, GLA_HEADS * GLA_KEY_DIM), jnp.float32) * GLA_GATE_RANK ** -0.5,
        'gla_gate_b': 0.02 * nrm(ks[16], (L, 2, GLA_HEADS * GLA_KEY_DIM), jnp.float32),
        'gla_norm': gain(ks[17], BRANCH_WIDTH),
        'w_branch': nrm(ks[18], (L, N_BRANCH, BRANCH_WIDTH, D_MODEL), jnp.float32) * BRANCH_WIDTH ** -0.5,
        'w_out': nrm(ks[19], (L, D_MODEL, D_MODEL), jnp.float32) * D_MODEL ** -0.5,
        'norm_mix_post': gain(ks[20], D_MODEL),
        'norm_ffn_pre': gain(ks[21], D_MODEL),
        'w_ff1': nrm(ks[22], (L, D_MODEL, D_FF), jnp.float32) * D_MODEL ** -0.5,
        'w_ff2': nrm(ks[23], (L, D_FF, D_MODEL), jnp.float32) * D_FF ** -0.5,
        'norm_ffn_post': gain(ks[24], D_MODEL),
    }


def reference(x_prompt, x_sample, norm_mix_pre, w_in, ssd_conv_w, ssd_conv_b, ssd_a_log, ssd_dt_bias, ssd_d,
              ssd_norm, pool_w, pool_scale, sgu_norm, sgu_w, sgu_b, gla_gate_w2, gla_gate_b, gla_norm,
              w_branch, w_out, norm_mix_post, norm_ffn_pre, w_ff1, w_ff2, norm_ffn_post):
    params = (norm_mix_pre, w_in, ssd_conv_w, ssd_conv_b, ssd_a_log, ssd_dt_bias, ssd_d, ssd_norm,
              pool_w, pool_scale, sgu_norm, sgu_w, sgu_b, gla_gate_w2, gla_gate_b, gla_norm,
              w_branch, w_out, norm_mix_post, norm_ffn_pre, w_ff1, w_ff2, norm_ffn_post)
    y_prompt = trunk(x_prompt, params)
    y_sample = trunk(x_sample, params)
    return (y_prompt, y_sample)
```

```python
import numpy as np
from contextlib import ExitStack
import concourse.bass as bass
import concourse.mybir as mybir
from concourse.bass_utils import run_bass_kernel_spmd

F32 = mybir.dt.float32
BF16 = mybir.dt.bfloat16
AF = mybir.ActivationFunctionType
ALU = mybir.AluOpType

D = 1024
DEPTH = 4
PAD = 8
EPS = 1e-6
NCORES = 8


class _Rec:
    def __init__(self):
        self.call = None

    def __getattr__(self, name):
        def f(*a, **k):
            self.call = (name, a, k)
            return self
        return f


def _record(fn):
    r = _Rec()
    fn(r)
    return r.call


class Sched:
    SEM_MAX = 60000

    def __init__(self, nc, sems):
        self.nc = nc
        self.free_sems = list(sems)
        self.eng_names = ["pe", "act", "dve", "pool", "sp"]
        self.prog = {e: [] for e in self.eng_names}
        self.esem = {}
        self.ecnt = {}
        for e in self.eng_names:
            self.esem[e] = self.free_sems.pop()
            self.ecnt[e] = 0
        self.seen = {e: {} for e in self.eng_names}
        self.last_w = {}
        self.readers = {}
        self.dma_sem = {}
        self.all_sems = {}
        self.n_instr = 0

    def _new_sem(self):
        return self.free_sems.pop()

    def _emit_wait(self, eng, tok):
        if tok is None:
            return
        sem, val, src = tok
        if src == eng and eng == "pe":
            return
        k = id(sem)
        if self.seen[eng].get(k, 0) >= val:
            return
        self.seen[eng][k] = val
        self.prog[eng].append(("wait", sem, val))

    def _deps(self, eng, reads, writes):
        for r in reads:
            self._emit_wait(eng, self.last_w.get(r))
        for w in writes:
            self._emit_wait(eng, self.last_w.get(w))
            for t in self.readers.get(w, ()):
                self._emit_wait(eng, t)

    def _commit(self, tok, reads, writes):
        for r in reads:
            self.readers.setdefault(r, []).append(tok)
        for w in writes:
            self.last_w[w] = tok
            self.readers[w] = []

    def op(self, eng, fn, reads=(), writes=(), inc=True):
        inc = True
        self._deps(eng, reads, writes)
        if self.ecnt[eng] >= self.SEM_MAX:
            self.esem[eng] = self._new_sem()
            self.ecnt[eng] = 0
        if inc:
            self.ecnt[eng] += 1
            tok = (self.esem[eng], self.ecnt[eng], eng)
            self.prog[eng].append(("op", _record(fn), self.esem[eng], 1))
        else:
            tok = (self.esem[eng], self.ecnt[eng] + 1, eng)
            self.prog[eng].append(("op", _record(fn), None, 0))
        self.all_sems[id(tok[0])] = (tok[0], max(tok[1] if inc else 0, self.all_sems.get(id(tok[0]), (None, 0))[1]))
        self._commit(tok, reads, writes)
        self.n_instr += 1
        return tok

    def dma(self, eng, slot, fn, reads=(), writes=()):
        self._deps(eng, reads, writes)
        if slot not in self.dma_sem:
            self.dma_sem[slot] = [self._new_sem(), 0]
        st = self.dma_sem[slot]
        if st[1] + 16 > self.SEM_MAX:
            st[0] = self._new_sem()
            st[1] = 0
        st[1] += 16
        tok = (st[0], st[1], "dma")
        self.all_sems[id(st[0])] = (st[0], st[1])
        self.prog[eng].append(("op", _record(fn), st[0], 16))
        self._commit(tok, reads, writes)
        self.n_instr += 1
        return tok

    def barrier(self, engs=None):
        for e in (engs or self.eng_names):
            for sem, val in list(self.all_sems.values()):
                if val > 0:
                    self._emit_wait(e, (sem, val, "any"))

    def emit(self):
        nc = self.nc
        objs = {"pe": "tensor", "act": "scalar", "dve": "vector", "pool": "gpsimd", "sp": "sync"}
        with nc.Block() as block:
            for e in self.eng_names:
                prog = self.prog[e]

                def body(eobj, prog=prog):
                    for it in prog:
                        if it[0] == "wait":
                            eobj.wait_ge(it[1], it[2])
                        else:
                            name, a, k = it[1]
                            ins = getattr(eobj, name)(*a, **k)
                            if it[2] is not None:
                                ins.then_inc(it[2], it[3])
                getattr(block, objs[e])(body)


WIN_GROUPS = [
    ("z", 0, 512, "tm", 512),
    ("xbc", 512, 768, "tm", 384),
    ("dt", 1280, 16, "tm", 16),
    ("p", 1296, 512, "fm", 512),
    ("u", 1808, 512, "fm", 512),
    ("v", 2320, 512, "tm", 512),
    ("q", 2832, 256, "fm", 256),
    ("k", 3088, 256, "fm", 256),
    ("gv", 3344, 512, "tm", 512),
    ("r", 3856, 512, "fm", 512),
    ("glr", 4368, 32, "fm", 32),
    ("gate", 4400, 4096, "fm", 512),
]
RB_CONVW = 0
RB_CONVB = RB_CONVW + 5 * 768
RB_DTB = RB_CONVB + 768
RB_ALOG = RB_DTB + 16
RB_D = RB_ALOG + 16
RB_SSDN = RB_D + 8
RB_SGUN = RB_SSDN + 512
RB_SGUB = RB_SGUN + 512
RB_LEN = RB_SGUB + 512
PP_G1, PP_GPOST, PP_GF1, PP_GF2, PP_PSC, PP_GLAN = 0, 8, 16, 24, 32, 36
PP_LEN = 40


class Builder:
    def __init__(self, seqs, depth, debug=(), TT=512):
        self.seqs = list(seqs)
        self.depth = depth
        self.TT = TT
        self.T = sum(seqs)
        self.soff = [int(x) for x in np.cumsum([0] + self.seqs[:-1])]
        self.poff = [int(x) for x in (np.cumsum([0] + [L + 2 * PAD for L in self.seqs[:-1]]) + PAD)]
        self.Tp = sum(L + 2 * PAD for L in self.seqs)
        self.debug = set(debug)
        self.nc = bass.Bass("TRN2", target_bir_lowering=False)
        self.es = ExitStack()
        self.psn = 0

    def dram(self, name, shape, dt, kind="Internal"):
        if name in self.debug:
            kind = "ExternalOutput"
        return self.nc.dram_tensor(name, list(shape), dt, kind=kind).ap()

    def sb(self, name, shape, dt=F32):
        return self.es.enter_context(self.nc.sbuf_tensor("sb_" + name, list(shape), dt))

    def arena_reset(self):
        self.S.barrier()
        self.aoff = 0
        self.aphase += 1

    def arena(self, name, free_shape, dt=F32):
        n = int(np.prod(free_shape))
        words = n if dt == F32 else (n + 1) // 2
        a = self.aoff
        self.aoff += words
        assert self.aoff <= self.AWORDS, (name, self.aoff)
        v = self.arena_t[:, a:a + words]
        if dt != F32:
            v = v.bitcast(dt)[:, 0:n]
        if len(free_shape) == 2:
            v = v.rearrange("p (a b) -> p a b", a=free_shape[0])
        elif len(free_shape) == 3:
            v = v.rearrange("p (a b c) -> p a b c", a=free_shape[0], b=free_shape[1])
        return v, f"A{self.aphase}:{name}"

    def ps(self):
        i = self.psn % 8
        self.psn += 1
        return self.psum[i], f"ps{i}"

    def build(self):
        nc, es = self.nc, self.es
        T, Tp, L = self.T, self.Tp, self.depth
        dr = self.dram
        inp = lambda name, shape: nc.dram_tensor(name, list(shape), F32, kind="ExternalInput").ap()
        self.x_in = inp("x", [T, D])
        self.y_out = nc.dram_tensor("y", [T, D], F32, kind="ExternalOutput").ap()
        self.w_in = inp("w_in", [L, D, 8496])
        self.w_branch = inp("w_branch", [L, 4, 512, D])
        self.w_out = inp("w_out", [L, D, D])
        self.w_ff1 = inp("w_ff1", [L, D, 4096])
        self.w_ff2 = inp("w_ff2", [L, 4096, D])
        self.pp_in = inp("pp", [L, 128, PP_LEN])
        self.rb_in = inp("rb", [L, RB_LEN])
        self.sguw_in = inp("sguwT", [L, 128, 512])
        self.poolw_in = inp("poolw", [L, 128, 512])
        self.w2_in = inp("w2blk", [L, 33, 512])
        self.cst_in = inp("cst", [128, 1860])
        self.XT = dr("XT", [8, 128, T], F32)
        self.wsc = {}
        for l in range(L):
            for (g, off, n, ori, nb) in WIN_GROUPS:
                self.wsc[(l, g)] = dr(f"w{l}_{g}", [n // nb, 128, 8 * nb], BF16)
            self.wsc[(l, "br")] = dr(f"w{l}_br", [4, 128, 4 * 1024], BF16)
            self.wsc[(l, "out")] = dr(f"w{l}_out", [2, 128, 8 * 512], BF16)
            self.wsc[(l, "ff1")] = dr(f"w{l}_ff1", [8, 128, 8 * 512], BF16)
            self.wsc[(l, "ff2")] = dr(f"w{l}_ff2", [4, 128, 32 * 256], BF16)
        self.P = {}
        self.P["z"] = dr("P_z", [T, 512], BF16)
        self.P["xbc"] = dr("P_xbc", [Tp, 768], BF16)
        self.P["dt"] = dr("P_dt", [T, 16], F32)
        self.P["p"] = dr("P_p", [4, 128, Tp], BF16)
        self.P["u"] = dr("P_u", [4, 128, T], BF16)
        self.P["v"] = dr("P_v", [T, 512], BF16)
        self.P["q"] = dr("P_q", [2, 128, T], BF16)
        self.P["k"] = dr("P_k", [2, 128, T], BF16)
        self.P["gv"] = dr("P_gv", [T, 512], BF16)
        self.P["r"] = dr("P_r", [4, 128, T], BF16)
        self.P["glr"] = dr("P_glr", [32, T], BF16)
        self.P["gate"] = dr("P_gate", [32, 128, T], BF16)
        self.BR = dr("BR", [4, 4, 128, T], BF16)

        sems = [es.enter_context(nc.semaphore(f"s{i}")) for i in range(100)]
        self.S = Sched(nc, sems)
        self.psum = [es.enter_context(nc.psum_tensor(f"psb{i}", [128, 512], F32)) for i in range(8)]
        self.cst = self.sb("cst", [128, 1860])
        self.cstb = self.sb("cstb", [128, 4 * 128 + 2 * 512], BF16)
        self.AWORDS = 42000
        self.arena_t = self.sb("arena", [128, self.AWORDS], F32)
        self.aoff = 0
        self.aphase = 0
        self.pp = self.sb("pp", [128, PP_LEN])
        self.rb = self.sb("rbt", [128, RB_LEN])
        self.sguw = self.sb("sguw", [128, 512], BF16)
        self.poolw = self.sb("poolw", [128, 512], BF16)
        self.w2b = self.sb("w2b", [33, 512], BF16)
        self.anegt = self.sb("anegt", [128, 16])
        self.zero_t = self.sb("zero_t", [128, 768], BF16)

        S = self.S
        cst = self.cst
        S.dma("sp", "cst", lambda e: e.dma_start(out=cst[:], in_=self.cst_in[:, :]), writes=["cst"])
        self.identf = cst[:, 0:128]
        self.onesf = cst[:, 128:256]
        self.Uf = cst[:, 256:384]
        self.Lf = cst[:, 384:512]
        self.Uneg = cst[:, 512:640]
        self.Lneg = cst[:, 640:768]
        cb = self.cstb
        S.op("dve", lambda e: e.tensor_copy(out=cb[:, 0:128], in_=cst[:, 0:128]), reads=["cst"], writes=["cstb"])
        S.op("dve", lambda e: e.tensor_copy(out=cb[:, 128:256], in_=cst[:, 128:256]), reads=["cst"], writes=["cstb"])
        S.op("dve", lambda e: e.tensor_copy(out=cb[:, 256:512], in_=cst[:, 256:512]), reads=["cst"], writes=["cstb"])
        S.op("dve", lambda e: e.tensor_copy(out=cb[:, 512:1536], in_=cst[:, 768:1792]), reads=["cst"], writes=["cstb"])
        self.identb = cb[:, 0:128]
        self.onesb = cb[:, 128:256]
        self.m01 = {"f": cb[:, 256:384], "b": cb[:, 384:512]}
        self.negm = {"f": cb[:, 512:1024], "b": cb[:, 1024:1536]}
        self.tri = {"f": self.Uf, "b": self.Lf}
        self.trineg = {"f": self.Uneg, "b": self.Lneg}
        zt = self.zero_t
        S.op("dve", lambda e: e.memset(zt[:], 0.0), writes=["zero_t"])
        self.zero_pads()
        self.phase0_x()
        for l in range(L):
            self.prep_weights(l)
        for l in range(L):
            self.layer(l)
        self.phase_final()
        S.barrier(["sp"])
        S.emit()
        return nc

    def zero_pads(self):
        S, zt = self.S, self.zero_t
        for s, Ls in enumerate(self.seqs):
            for (r0) in (self.poff[s] - PAD, self.poff[s] + Ls):
                S.dma("pool", "zp", lambda e, r0=r0: e.dma_start(out=self.P["xbc"][r0:r0 + PAD, :], in_=zt[0:PAD, :]),
                      reads=["zero_t"], writes=[("Pxbc_pad", r0)])
                S.dma("pool", "zp", lambda e, r0=r0: e.dma_start(
                    out=self.P["p"][:, :, r0:r0 + PAD].rearrange("g p t -> p g t"),
                    in_=zt[:, 0:4 * PAD].rearrange("p (g t) -> p g t", g=4)),
                    reads=["zero_t"], writes=[("Pp_pad", r0)])

    def phase0_x(self):
        S = self.S
        self.arena_reset()
        xin = [self.arena(f"xin{i}", [4, D]) for i in range(2)]
        xo = [self.arena(f"xo{i}", [8, 512]) for i in range(2)]
        for ti in range(self.T // 512):
            xi, xir = xin[ti % 2]
            xot, xor_ = xo[ti % 2]
            t0 = ti * 512
            S.dma("sp", f"xin{ti % 2}", lambda e, xi=xi, t0=t0: e.dma_start(
                out=xi, in_=self.x_in[t0:t0 + 512, :].rearrange("(j p) d -> p j d", p=128)), writes=[xir])
            for kc in range(8):
                pt, pr = self.ps()
                for j in range(4):
                    S.op("pe", lambda e, pt=pt, xi=xi, j=j, kc=kc: e.transpose(
                        pt[:, j * 128:(j + 1) * 128], xi[:, j, kc * 128:(kc + 1) * 128], self.identf),
                        reads=[xir, "cst"], writes=[pr], inc=(j == 3))
                eng = "dve" if kc % 2 == 0 else "act"
                if eng == "dve":
                    S.op("dve", lambda e, pt=pt, xot=xot, kc=kc: e.tensor_copy(out=xot[:, kc, :], in_=pt[:]),
                         reads=[pr], writes=[(xor_, kc)])
                else:
                    S.op("act", lambda e, pt=pt, xot=xot, kc=kc: e.copy(out=xot[:, kc, :], in_=pt[:]),
                         reads=[pr], writes=[(xor_, kc)])
            S.dma("pool", f"xo{ti % 2}", lambda e, xot=xot, t0=t0: e.dma_start(
                out=self.XT[:, :, t0:t0 + 512].rearrange("k p t -> p k t"), in_=xot),
                reads=[(xor_, kc) for kc in range(8)], writes=[("XT", ti)])

    def prep_weights(self, l):
        S = self.S
        self.arena_reset()
        st32 = [self.arena(f"ws32_{i}", [8192]) for i in range(2)]
        st16 = [self.arena(f"ws16_{i}", [8192], BF16) for i in range(2)]
        cnt = [0]

        def block(src2d, kc_n, c0, nb, dst_blk):
            i = cnt[0] % 2
            cnt[0] += 1
            s32, r32 = st32[i]
            s16, r16 = st16[i]
            n = kc_n * nb
            S.dma("sp", f"ws32_{i}", lambda e: e.dma_start(
                out=s32[:, 0:n].rearrange("p (k n) -> p k n", k=kc_n),
                in_=src2d[:, c0:c0 + nb].rearrange("(k p) n -> p k n", p=128)), writes=[r32])
            eng = ["dve", "act"][cnt[0] % 2]
            if eng == "act":
                S.op("act", lambda e: e.copy(out=s16[:, 0:n], in_=s32[:, 0:n]), reads=[r32], writes=[r16])
            else:
                S.op("dve", lambda e: e.tensor_copy(out=s16[:, 0:n], in_=s32[:, 0:n]), reads=[r32], writes=[r16])
            S.dma("pool", f"ws16_{i}", lambda e: e.dma_start(out=dst_blk, in_=s16[:, 0:n]), reads=[r16],
                  writes=[("wsc", l)])

        for (g, off, n, ori, nb) in WIN_GROUPS:
            for j in range(n // nb):
                block(self.w_in[l], 8, off + j * nb, nb, self.wsc[(l, g)][j])
        for n_ in range(4):
            block(self.w_branch[l, n_], 4, 0, 1024, self.wsc[(l, "br")][n_])
        for j in range(2):
            block(self.w_out[l], 8, j * 512, 512, self.wsc[(l, "out")][j])
        for j in range(8):
            block(self.w_ff1[l], 8, j * 512, 512, self.wsc[(l, "ff1")][j])
        for j in range(4):
            block(self.w_ff2[l], 32, j * 256, 256, self.wsc[(l, "ff2")][j])

    def layer(self, l):
        self.load_layer_params(l)
        self.phase1(l)
        if "stop1" in self.debug:
            return
        self.phase2(l)
        if "stop2" in self.debug:
            return
        self.phase3(l)

    def load_layer_params(self, l):
        S = self.S
        self.arena_reset()
        S.dma("sp", "pp", lambda e: e.dma_start(out=self.pp[:], in_=self.pp_in[l]), writes=["pp"])
        S.dma("sp", "rb", lambda e: e.dma_start(out=self.rb[:], in_=self.rb_in[l:l + 1, :].partition_broadcast(128)),
              writes=["rb"])
        t32, r32 = self.arena("lp32", [512])
        for (src, dst, name, rows) in ((self.sguw_in, self.sguw, "sguw", 128), (self.poolw_in, self.poolw, "poolw", 128),
                                       (self.w2_in, self.w2b, "w2b", 33)):
            S.dma("sp", "lp32", lambda e, src=src, rows=rows: e.dma_start(out=t32[0:rows, :], in_=src[l]), writes=[r32])
            S.op("dve", lambda e, dst=dst, rows=rows: e.tensor_copy(out=dst[0:rows, :], in_=t32[0:rows, :]),
                 reads=[r32], writes=[name])
        S.op("act", lambda e: e.activation(out=self.anegt[:], in_=self.rb[:, RB_ALOG:RB_ALOG + 16], func=AF.Exp),
             reads=["rb"], writes=["anegt"])
        S.op("dve", lambda e: e.tensor_scalar(out=self.anegt[:], in0=self.anegt[:], scalar1=-1.0, scalar2=None,
                                              op0=ALU.mult), reads=["anegt"], writes=["anegt"])

    def rstd_from_sq(self, sq, sqr, nk, rstd, rstdr, dim):
        S = self.S
        pt, pr = self.ps()
        for kc in range(nk):
            S.op("pe", lambda e, kc=kc: e.matmul(pt[:], lhsT=self.onesb, rhs=sq[:, kc, :], start=(kc == 0),
                                                 stop=(kc == nk - 1)),
                 reads=[(sqr, kc), "cstb"], writes=[pr], inc=(kc == nk - 1))
        S.op("dve", lambda e: e.tensor_scalar(out=rstd, in0=pt[:], scalar1=1.0 / dim, scalar2=EPS, op0=ALU.mult,
                                              op1=ALU.add), reads=[pr], writes=[rstdr])
        S.op("act", lambda e: e.activation(out=rstd, in_=rstd, func=AF.Sqrt), reads=[rstdr], writes=[rstdr])
        S.op("dve", lambda e: e.reciprocal(out=rstd, in_=rstd), reads=[rstdr], writes=[rstdr])

    def phase1(self, l):
        S = self.S
        self.arena_reset()
        TT = 512
        xT = [self.arena(f"xT{i}", [8, TT]) for i in range(2)]
        sq, sqr = self.arena("sq", [8, TT], BF16)
        hT, hTr = self.arena("hT", [8, TT], BF16)
        rstd, rstdr = self.arena("rstd", [TT])
        wb = [self.arena(f"wb{i}", [8192], BF16) for i in range(3)]
        stg = [self.arena(f"stg{i}", [512], BF16) for i in range(4)]
        stg32 = [self.arena(f"stgf{i}", [16]) for i in range(2)]
        wcnt = [0]
        scnt = [0]
        pp = self.pp
        def pad_off(t0):
            for s in range(len(self.seqs)):
                if self.soff[s] <= t0 < self.soff[s] + self.seqs[s]:
                    return self.poff[s] + (t0 - self.soff[s])
            raise AssertionError
        evac_funcs = {"u": AF.Gelu_apprx_tanh, "v": AF.Gelu_apprx_tanh, "r": AF.Silu, "z": AF.Silu, "gate": AF.Sigmoid}
        for ti in range(self.T // TT):
            t0 = ti * TT
            tp0 = pad_off(t0)
            xt, xr = xT[ti % 2]
            S.dma("sp", f"xT{ti % 2}", lambda e, xt=xt, t0=t0: e.dma_start(
                out=xt, in_=self.XT[:, :, t0:t0 + TT].rearrange("k p t -> p k t")), reads=[("XT", ti)], writes=[xr])
            for kc in range(8):
                S.op("act", lambda e, kc=kc, xt=xt: e.activation(out=sq[:, kc, :], in_=xt[:, kc, :], func=AF.Square),
                     reads=[xr], writes=[(sqr, kc)])
            self.rstd_from_sq(sq, sqr, 8, rstd, rstdr, D)
            for kc in range(8):
                S.op("dve", lambda e, kc=kc, xt=xt: e.scalar_tensor_tensor(
                    out=hT[:, kc, :], in0=xt[:, kc, :], scalar=pp[:, PP_G1 + kc:PP_G1 + kc + 1], in1=rstd,
                    op0=ALU.mult, op1=ALU.mult), reads=[xr, rstdr, "pp"], writes=[(hTr, kc)])
            hreads = [(hTr, kc) for kc in range(8)]
            for (g, off, n, ori, nb) in WIN_GROUPS:
                for j in range(n // nb):
                    w, wr = wb[wcnt[0] % 3]
                    slot = f"wb{wcnt[0] % 3}"
                    wcnt[0] += 1
                    S.dma("sp", slot, lambda e, w=w, g=g, j=j, nb=nb: e.dma_start(out=w[:, 0:8 * nb], in_=self.wsc[(l, g)][j]),
                          reads=[("wsc", l)], writes=[wr])
                    wv = w[:, 0:8 * nb].rearrange("p (k n) -> p k n", k=8)
                    if ori == "fm":
                        for cc in range(max(1, nb // 128)):
                            m = min(128, nb)
                            pt, pr = self.ps()
                            for kc in range(8):
                                S.op("pe", lambda e, pt=pt, wv=wv, cc=cc, kc=kc, m=m: e.matmul(
                                    pt[0:m, :], lhsT=wv[:, kc, cc * 128:cc * 128 + m], rhs=hT[:, kc, :],
                                    start=(kc == 0), stop=(kc == 7)), reads=[wr] + hreads, writes=[pr], inc=(kc == 7))
                            st, sr = stg[scnt[0] % 4]
                            sslot = f"stg{scnt[0] % 4}"
                            scnt[0] += 1
                            cidx = j * (nb // 128) + cc if nb >= 128 else 0
                            fn = evac_funcs.get(g)
                            if fn is not None:
                                S.op("act", lambda e, st=st, pt=pt, fn=fn, m=m: e.activation(out=st[0:m, :], in_=pt[0:m, :], func=fn),
                                     reads=[pr], writes=[sr])
                            elif g == "q":
                                S.op("act", lambda e, st=st, pt=pt, m=m: e.mul(out=st[0:m, :], in_=pt[0:m, :], mul=0.125),
                                     reads=[pr], writes=[sr])
                            else:
                                S.op("dve", lambda e, st=st, pt=pt, m=m: e.tensor_copy(out=st[0:m, :], in_=pt[0:m, :]),
                                     reads=[pr], writes=[sr])
                            if g == "p":
                                dst = self.P["p"][cidx, :, tp0:tp0 + TT]
                            elif g == "glr":
                                dst = self.P["glr"][:, t0:t0 + TT]
                            else:
                                dst = self.P[g][cidx, :, t0:t0 + TT]
                            S.dma("pool", sslot, lambda e, dst=dst, st=st, m=m: e.dma_start(out=dst, in_=st[0:m, :]),
                                  reads=[sr], writes=[("P" + g, ti)])
                    else:
                        for i in range(TT // 128):
                            pt, pr = self.ps()
                            for kc in range(8):
                                S.op("pe", lambda e, pt=pt, wv=wv, i=i, kc=kc, nb=nb: e.matmul(
                                    pt[:, 0:nb], lhsT=hT[:, kc, i * 128:(i + 1) * 128], rhs=wv[:, kc, :],
                                    start=(kc == 0), stop=(kc == 7)), reads=[wr] + hreads, writes=[pr], inc=(kc == 7))
                            if g == "dt":
                                st, sr = stg32[scnt[0] % 2]
                                sslot = f"stgf{scnt[0] % 2}"
                            else:
                                st, sr = stg[scnt[0] % 4]
                                sslot = f"stg{scnt[0] % 4}"
                            scnt[0] += 1
                            fn = evac_funcs.get(g)
                            if fn is not None:
                                S.op("act", lambda e, st=st, pt=pt, fn=fn, nb=nb: e.activation(out=st[:, 0:nb], in_=pt[:, 0:nb], func=fn),
                                     reads=[pr], writes=[sr])
                            else:
                                S.op("dve", lambda e, st=st, pt=pt, nb=nb: e.tensor_copy(out=st[:, 0:nb], in_=pt[:, 0:nb]),
                                     reads=[pr], writes=[sr])
                            if g == "xbc":
                                dst = self.P["xbc"][tp0 + i * 128:tp0 + (i + 1) * 128, j * nb:(j + 1) * nb]
                            else:
                                dst = self.P[g][t0 + i * 128:t0 + (i + 1) * 128, j * nb:(j + 1) * nb]
                            S.dma("pool", sslot, lambda e, dst=dst, st=st, nb=nb: e.dma_start(out=dst, in_=st[:, 0:nb]),
                                  reads=[sr], writes=[("P" + g, ti)])


    def phase2(self, l):
        S = self.S
        self.arena_reset()
        ar = self.arena
        rb, pp = self.rb, self.pp
        NCMAX = max(self.seqs) // 128
        Sst_f, Sst_fr = ar("Sst_f", [NCMAX, 256], BF16)
        Sgl_f, Sgl_fr = ar("Sgl_f", [NCMAX, 256], BF16)
        Sssd, Sssdr = ar("Sssd", [256])
        Sgla, Sglar = ar("Sgla", [2, 128])
        Sin_b, Sin_br = ar("Sin_b", [256], BF16)
        Gin_b, Gin_br = ar("Gin_b", [2, 128], BF16)
        xs0 = [ar(f"xs_{k}", [768], BF16) for k in range(5)]
        xs = [xs0, xs0]
        acc, accr = ar("acc", [768])
        ctmp, ctmpr = ar("ctmp", [768])
        xa, xar = ar("xa", [768], BF16)
        dtr = [ar(f"dtr{i}", [16]) for i in range(2)]
        dtt, dttr = ar("dtt", [16])
        dta, dtar = ar("dta", [16])
        ndta, ndtar = ar("ndta", [16])
        sc, scr = ar("sc", [32])
        d1, d1r = ar("d1", [16])
        dte, dter = ar("dte", [16])
        wst, wstr = ar("wst", [16])
        etot, etotr = ar("etot", [16])
        etsel, etselr = ar("etsel", [4])
        eacs, eacsr = ar("eacs", [16])
        xw, xwr = ar("xw", [512], BF16)
        xdt = {X: ar(f"xdt{X}", [512], BF16) for X in "fb"}
        dtaU = {X: ar(f"dtaU{X}", [8, 128]) for X in "fb"}
        BCT, BCTr = ar("BCT", [2, 128], BF16)
        CBT, CBTr = ar("CBT", [2, 128], BF16)
        Cblk, Cblkr = ar("Cblk", [2, 128], BF16)
        Sblk = {X: ar(f"Sblk{X}", [2, 256], BF16) for X in "fb"}
        Qblk, Qblkr = ar("Qblk", [4, 2, 128], BF16)
        Et = [ar(f"E{i}", [4, 128], BF16) for i in range(2)]
        Mt = {(X, g): ar(f"M{X}{g}", [4, 128], BF16) for X in "fb" for g in range(2)}
        y1, y1r = ar("y1", [512])
        y2, y2r = ar("y2", [512])
        zt = [ar(f"zt{i}", [512], BF16) for i in range(2)]
        junk, junkr = ar("junk", [512])
        ss2, ss2r = ar("ss2", [2])
        ya, yar = ar("ya", [512], BF16)
        brs = [ar(f"brs{i}", [4, 128], BF16) for i in range(4)]
        glr = [ar(f"glr{i}", [128], BF16) for i in range(2)]
        e1, e1r = ar("e1", [512])
        sp_, spr = ar("sp", [512])
        ebt, ebtr = ar("ebt", [4])
        ek, ekr = ar("ek", [4, 128])
        eq, eqr = ar("eq", [4, 128])
        kT = [ar(f"kT{i}", [2, 128], BF16) for i in range(2)]
        qT = [ar(f"qT{i}", [2, 128], BF16) for i in range(2)]
        vtm = [ar(f"vtm{i}", [512], BF16) for i in range(2)]
        rT = [ar(f"rT{i}", [4, 128], BF16) for i in range(2)]
        kinv, kinvr = ar("kinv", [4, 128], BF16)
        kend, kendr = ar("kend", [4, 128], BF16)
        qdec, qdecr = ar("qdec", [4, 128], BF16)
        kendtm, kendtmr = ar("kendtm", [4, 128], BF16)
        attm = {X: ar(f"attm{X}", [4, 128], BF16) for X in "fb"}
        osq, osqr = ar("osq", [512], BF16)
        orstd, orstdr = ar("orstd", [512])
        o1, o1r = ar("o1", [512])
        rg, rgr = ar("rg", [4, 128])
        vt = [ar(f"vt{i}", [512], BF16) for i in range(2)]
        uT = [ar(f"uT{i}", [4, 128], BF16) for i in range(2)]
        ss1, ss1r = ar("ss1", [2])
        vf, vfr = ar("vf", [512], BF16)
        sg1, sg1r = ar("sg1", [4, 128])
        ptl = [ar(f"ptl{i}", [4, 144], BF16) for i in range(2)]
        a1, a1r = ar("a1", [3, 144])
        a2, a2r = ar("a2", [2, 144])
        a3, a3r = ar("a3", [144])
        wsum, wsumr = ar("wsum", [4, 128])
        pooled, pooledr = ar("pooled", [4, 128], BF16)
        ldc = [0]
        brc = [0]
        V = lambda fn, r=(), w=(): S.op("dve", fn, reads=r, writes=w)
        A = lambda fn, r=(), w=(): S.op("act", fn, reads=r, writes=w)
        PE = lambda fn, r=(), w=(), inc=True: S.op("pe", fn, reads=r, writes=w, inc=inc)
        cw = lambda k: rb[:, RB_CONVW + k * 768:RB_CONVW + (k + 1) * 768]
        for i in range(2):
            g_, gr_ = glr[i]
            V(lambda e, g_=g_: e.memset(g_[32:33, :], 1.0), w=[gr_ + "one"])

        def bc(ap2, n):
            return ap2.unsqueeze(2).to_broadcast([ap2.shape[0], ap2.shape[1], n])

        def store_branch(n, src, srcr, tok0):
            S.dma("pool", "brs" + srcr, lambda e: e.dma_start(
                out=self.BR[n, :, :, tok0:tok0 + 128].rearrange("w p t -> p w t"), in_=src), reads=[srcr], writes=[("BR", n, tok0)])

        def next_brs():
            b = brs[brc[0] % 4]
            brc[0] += 1
            return b

        def ssd_common(tp, tk, full):
            i = ldc[0] % 2
            ldc[0] += 1
            for k in range(5):
                x_, xr_ = xs[i][k]
                S.dma("sp", f"xs{i}_{k}", lambda e, x_=x_, k=k: e.dma_start(out=x_, in_=self.P["xbc"][tp + k - 2:tp + k - 2 + 128, :]), writes=[xr_])
            d_, dr_ = dtr[i]
            S.dma("sp", f"dtr{i}", lambda e: e.dma_start(out=d_, in_=self.P["dt"][tk:tk + 128, :]), writes=[dr_])
            V(lambda e: e.tensor_tensor(out=acc, in0=xs[i][0][0], in1=cw(0), op=ALU.mult), [xs[i][0][1], "rb"], [accr])
            for k in range(1, 5):
                V(lambda e, k=k: e.tensor_tensor(out=ctmp, in0=xs[i][k][0], in1=cw(k), op=ALU.mult), [xs[i][k][1], "rb"], [ctmpr])
                V(lambda e: e.tensor_tensor(out=acc, in0=acc, in1=ctmp, op=ALU.add), [accr, ctmpr], [accr])
            V(lambda e: e.tensor_tensor(out=acc, in0=acc, in1=rb[:, RB_CONVB:RB_CONVB + 768], op=ALU.add), [accr, "rb"], [accr])
            A(lambda e: e.activation(out=xa, in_=acc, func=AF.Silu), [accr], [xar])
            if "s1" in self.debug:
                return
            V(lambda e: e.tensor_tensor(out=dtt, in0=d_, in1=rb[:, RB_DTB:RB_DTB + 16], op=ALU.add), [dr_, "rb"], [dttr])
            A(lambda e: e.activation(out=dtt, in_=dtt, func=AF.Exp), [dttr], [dttr])
            A(lambda e: e.activation(out=dtt, in_=dtt, func=AF.Ln, bias=1.0), [dttr], [dttr])
            V(lambda e: e.tensor_tensor(out=dta, in0=dtt, in1=self.anegt[:], op=ALU.mult), [dttr, "anegt"], [dtar])
            if "s2" in self.debug:
                return
            pt, pr = self.ps()
            PE(lambda e: e.matmul(pt[:, 0:8], lhsT=self.Uf, rhs=dta[:, 0:8], start=True, stop=True), [dtar, "cst"], [pr], inc=False)
            PE(lambda e: e.matmul(pt[:, 8:16], lhsT=self.Lf, rhs=dta[:, 8:16], start=True, stop=True), [dtar, "cst"], [pr], inc=False)
            PE(lambda e: e.matmul(pt[:, 16:32], lhsT=self.onesf, rhs=dta[:, 0:16], start=True, stop=True), [dtar, "cst"], [pr])
            V(lambda e: e.tensor_copy(out=sc, in_=pt[:, 0:32]), [pr], [scr])
            V(lambda e: e.tensor_tensor(out=d1, in0=sc[:, 16:32], in1=sc[:, 0:16], op=ALU.subtract), [scr], [d1r])
            A(lambda e: e.activation(out=dte, in_=d1, func=AF.Exp), [d1r], [dter])
            A(lambda e: e.activation(out=etot, in_=sc[:, 16:32], func=AF.Exp), [scr], [etotr])
            V(lambda e: e.tensor_tensor(out=wst, in0=dtt, in1=dte, op=ALU.mult), [dttr, dter], [wstr])

        def ssd_state_update(X, c, store):
            if "s1" in self.debug or "s2" in self.debug or "s3" in self.debug:
                return
            xo = 0 if X == "f" else 8
            V(lambda e: e.tensor_tensor(out=xw.rearrange("p (h d) -> p h d", h=8), in0=xa[:, 0:512].rearrange("p (h d) -> p h d", h=8),
                                        in1=bc(wst[:, xo:xo + 8], 64), op=ALU.mult), [xar, wstr], [xwr])
            pt, pr = self.ps()
            PE(lambda e: e.matmul(pt[:], lhsT=xa[:, 512:640], rhs=xw, start=True, stop=True), [xar, xwr], [pr])
            if store:
                V(lambda e: e.tensor_copy(out=Sst_f[:, c, :], in_=Sssd), [Sssdr], [(Sst_fr, c)])
            V(lambda e: e.tensor_scalar(out=etsel, in0=etot[:, xo:xo + 4], scalar1=self.cst[:, 1856:1857], scalar2=None, op0=ALU.mult), [etotr, "cst"], [etselr])
            V(lambda e: e.scalar_tensor_tensor(out=etsel, in0=etot[:, xo + 4:xo + 8], scalar=self.cst[:, 1857:1858], in1=etsel, op0=ALU.mult, op1=ALU.add),
              [etotr, etselr, "cst"], [etselr])
            V(lambda e: e.tensor_tensor(out=Sssd.rearrange("p (h d) -> p h d", h=4), in0=Sssd.rearrange("p (h d) -> p h d", h=4), in1=bc(etsel, 64), op=ALU.mult),
              [Sssdr, etselr], [Sssdr])
            for g in range(2):
                V(lambda e, g=g, pt=pt: e.scalar_tensor_tensor(out=Sssd, in0=pt[:, g * 256:(g + 1) * 256], scalar=self.cst[:, 1856 + g:1857 + g], in1=Sssd,
                                                               op0=ALU.mult, op1=ALU.add), [Sssdr, pr, "cst"], [Sssdr])

        def gla_common(tk, dirs):
            i = ldc[0] % 2
            g_, gr_ = glr[i]
            S.dma("sp", f"glr{i}", lambda e: e.dma_start(out=g_[0:32, :], in_=self.P["glr"][:, tk:tk + 128]), writes=[gr_])
            k_, kr_ = kT[i]
            S.dma("sp", f"kT{i}", lambda e: e.dma_start(out=k_, in_=self.P["k"][:, :, tk:tk + 128].rearrange("h p t -> p h t")), writes=[kr_])
            v_, vr_ = vtm[i]
            S.dma("sp", f"vtm{i}", lambda e: e.dma_start(out=v_, in_=self.P["gv"][tk:tk + 128, :]), writes=[vr_])
            pg, pgr = self.ps()
            PE(lambda e: e.matmul(pg[:], lhsT=g_[0:33, :], rhs=self.w2b[0:33, :], start=True, stop=True), [gr_, gr_ + "one", "w2b"], [pgr])
            A(lambda e: e.activation(out=e1, in_=pg[:], func=AF.Exp, scale=-1.0), [pgr], [e1r])
            A(lambda e: e.activation(out=sp_, in_=e1, func=AF.Ln, bias=1.0), [e1r], [spr])
            pb, pbr = self.ps()
            pb3 = pb[:].rearrange("p (a t) -> p a t", a=4)
            combos = [(X, hp) for X in dirs for hp in range(2)]
            for n_, (X, hp) in enumerate(combos):
                xi = 0 if X == "f" else 1
                PE(lambda e, X=X, hp=hp, xi=xi: e.matmul(pb3[:, xi * 2 + hp, :], lhsT=sp_[:, xi * 256 + hp * 128:xi * 256 + (hp + 1) * 128],
                                                         rhs=self.trineg[X], start=True, stop=True), [spr, "cst"], [pbr], inc=(n_ == len(combos) - 1))
            for X in dirs:
                xi = 0 if X == "f" else 1
                col = 127 if X == "f" else 0
                A(lambda e, xi=xi, col=col: e.activation(out=ebt[:, xi * 2:xi * 2 + 2], in_=pb3[:, xi * 2:xi * 2 + 2, col], func=AF.Exp), [pbr], [ebtr + X])
            lo, hi = (0, 2) if dirs == "f" else (0, 4)
            A(lambda e: e.activation(out=ek[:, lo:hi, :], in_=pb3[:, lo:hi, :], func=AF.Exp, scale=-1.0), [pbr], [ekr])
            if dirs == "fb":
                A(lambda e: e.activation(out=eq, in_=pb3, func=AF.Exp), [pbr], [eqr])
            nx = hi // 2
            V(lambda e: e.tensor_tensor(out=kinv[:, lo:hi, :].rearrange("p (x h) t -> p x h t", x=nx),
                                        in0=ek[:, lo:hi, :].rearrange("p (x h) t -> p x h t", x=nx),
                                        in1=k_.unsqueeze(1).to_broadcast([128, nx, 2, 128]), op=ALU.mult), [ekr, kr_], [kinvr])
            V(lambda e: e.tensor_tensor(out=kend[:, lo:hi, :], in0=kinv[:, lo:hi, :], in1=bc(ebt[:, lo:hi], 128), op=ALU.mult),
              [kinvr] + [ebtr + X for X in dirs], [kendr])
            ptb, ptr_ = self.ps()
            ptb3 = ptb[:].bitcast(BF16)[:, 0:512].rearrange("p (a t) -> p a t", a=4)
            for a_ in range(lo, hi):
                PE(lambda e, a_=a_: e.transpose(ptb3[:, a_, :], kend[:, a_, :], self.identb), [kendr, "cstb"], [ptr_], inc=(a_ == hi - 1))
            V(lambda e: e.tensor_copy(out=kendtm[:, lo:hi, :], in_=ptb3[:, lo:hi, :]), [ptr_], [kendtmr])
            return i

        def gla_state_update(X, c, i, store):
            xi = 0 if X == "f" else 1
            v_, vr_ = vtm[i]
            pt, pr = self.ps()
            for hp in range(2):
                PE(lambda e, hp=hp: e.matmul(pt[:, hp * 256:(hp + 1) * 256], lhsT=kendtm[:, xi * 2 + hp, :], rhs=v_[:, hp * 256:(hp + 1) * 256],
                                             start=True, stop=True), [kendtmr, vr_], [pr], inc=(hp == 1))
            if store:
                V(lambda e: e.tensor_copy(out=Sgl_f[:, c, :].rearrange("p (h v) -> p h v", h=2), in_=Sgla), [Sglar], [(Sgl_fr, c)])
            V(lambda e: e.tensor_tensor(out=Sgla, in0=Sgla, in1=bc(ebt[:, xi * 2:xi * 2 + 2], 128), op=ALU.mult), [Sglar, ebtr + X], [Sglar])
            p4 = pt[:].rearrange("p (a b v) -> p a b v", a=2, b=2)
            for hh in range(2):
                V(lambda e, hh=hh: e.scalar_tensor_tensor(out=Sgla, in0=p4[:, :, hh, :], scalar=self.cst[:, 1856 + hh:1857 + hh], in1=Sgla,
                                                          op0=ALU.mult, op1=ALU.add), [Sglar, pr, "cst"], [Sglar])

        for s, Ls in enumerate(self.seqs):
            nch = Ls // 128
            V(lambda e: e.memset(Sssd, 0.0), w=[Sssdr])
            V(lambda e: e.memset(Sgla, 0.0), w=[Sglar])
            for c in range(nch):
                tp = self.poff[s] + c * 128
                tk = self.soff[s] + c * 128
                if "noAssd" not in self.debug:
                    ssd_common(tp, tk, False)
                    ssd_state_update("f", c, True)
                if "noAgla" not in self.debug:
                    i = gla_common(tk, "f")
                    gla_state_update("f", c, i, True)
            V(lambda e: e.memset(Sssd, 0.0), w=[Sssdr])
            V(lambda e: e.memset(Sgla, 0.0), w=[Sglar])
            for c in range(nch - 1, -1, -1):
                tp = self.poff[s] + c * 128
                tk = self.soff[s] + c * 128
                if "noBssd" not in self.debug:
                    ssd_common(tp, tk, True)
                    li = (ldc[0] - 1) % 2
                    z_, zr_ = zt[li]
                    S.dma("sp", f"zt{li}", lambda e, z_=z_, tk=tk: e.dma_start(out=z_, in_=self.P["z"][tk:tk + 128, :]), writes=[zr_])
                    A(lambda e: e.activation(out=eacs, in_=sc[:, 0:16], func=AF.Exp), [scr], [eacsr])
                    V(lambda e: e.tensor_scalar(out=ndta, in0=dta, scalar1=-1.0, scalar2=None, op0=ALU.mult), [dtar], [ndtar])
                    for X in "fb":
                        xo = 0 if X == "f" else 8
                        xd, xdr = xdt[X]
                        V(lambda e, xd=xd, xo=xo: e.tensor_tensor(out=xd.rearrange("p (h d) -> p h d", h=8), in0=xa[:, 0:512].rearrange("p (h d) -> p h d", h=8),
                                                                  in1=bc(dtt[:, xo:xo + 8], 64), op=ALU.mult), [xar, dttr], [xdr])
                        du, dur = dtaU[X]
                        V(lambda e, du=du, xo=xo, X=X: e.tensor_tensor(out=du, in0=self.tri[X].unsqueeze(1).to_broadcast([128, 8, 128]),
                                                                       in1=bc(dta[:, xo:xo + 8], 128), op=ALU.mult), [dtar, "cst"], [dur])
                    if "b1" in self.debug:
                        continue
                    ptb, ptr_ = self.ps()
                    ptb3 = ptb[:].bitcast(BF16)[:, 0:256].rearrange("p (a t) -> p a t", a=2)
                    PE(lambda e: e.transpose(ptb3[:, 0, :], xa[:, 512:640], self.identb), [xar, "cstb"], [ptr_], inc=False)
                    PE(lambda e: e.transpose(ptb3[:, 1, :], xa[:, 640:768], self.identb), [xar, "cstb"], [ptr_])
                    V(lambda e: e.tensor_copy(out=BCT, in_=ptb3), [ptr_], [BCTr])
                    pcb, pcbr = self.ps()
                    for g in range(2):
                        V(lambda e, g=g: e.tensor_scalar(out=Cblk[:, g, :], in0=BCT[:, 1, :], scalar1=self.cst[:, 1856 + g:1857 + g], scalar2=None, op0=ALU.mult),
                          [BCTr, "cst"], [Cblkr])
                    PE(lambda e: e.matmul(pcb[:, 0:256], lhsT=BCT[:, 0, :], rhs=Cblk.rearrange("p a t -> p (a t)"), start=True, stop=True), [BCTr, Cblkr], [pcbr])
                    V(lambda e: e.tensor_copy(out=CBT, in_=pcb[:, 0:256].rearrange("p (a t) -> p a t", a=2)), [pcbr], [CBTr])
                    if "b2" in self.debug:
                        continue
                    ei = 0
                    for X in "fb":
                        xo = 0 if X == "f" else 8
                        du, dur = dtaU[X]
                        for g in range(2):
                            pseg, psegr = self.ps()
                            pseg3 = pseg[:].rearrange("p (h t) -> p h t", h=4)
                            PE(lambda e, g=g, du=du, pseg3=pseg3: e.matmul(pseg3, lhsT=self.onesf, rhs=du[:, g * 4:(g + 1) * 4, :], start=True, stop=False),
                               [dur, "cst"], [psegr], inc=False)
                            PE(lambda e, g=g, X=X, xo=xo, pseg3=pseg3: e.matmul(pseg3, lhsT=self.tri[X], rhs=bc(ndta[:, xo + g * 4:xo + g * 4 + 4], 128),
                                                                               start=False, stop=False), [ndtar, "cst"], [psegr], inc=False)
                            PE(lambda e, X=X, pseg=pseg: e.matmul(pseg[:], lhsT=self.identb, rhs=self.negm[X], start=False, stop=True), ["cstb"], [psegr])
                            E_, Er_ = Et[ei % 2]
                            ei += 1
                            A(lambda e, E_=E_, pseg3=pseg3: e.activation(out=E_, in_=pseg3, func=AF.Exp), [psegr], [Er_])
                            M_, Mr_ = Mt[(X, g)]
                            V(lambda e, M_=M_, E_=E_, g=g: e.tensor_tensor(out=M_, in0=E_, in1=CBT[:, g, :].unsqueeze(1).to_broadcast([128, 4, 128]), op=ALU.mult),
                              [Er_, CBTr], [Mr_])
                    pyd, pydr = self.ps()
                    for h in range(8):
                        g, h4 = h // 4, h % 4
                        for X in "fb":
                            PE(lambda e, h=h, g=g, h4=h4, X=X: e.matmul(pyd[:, h * 64:(h + 1) * 64], lhsT=Mt[(X, g)][0][:, h4, :], rhs=xdt[X][0][:, h * 64:(h + 1) * 64],
                                                                        start=(X == "f"), stop=(X == "b")), [Mt[(X, g)][1], xdt[X][1]], [pydr], inc=(h == 7 and X == "b"))
                    for g in range(2):
                        V(lambda e, g=g: e.tensor_scalar(out=Sblk["f"][0][:, g, :], in0=Sst_f[:, c, :], scalar1=self.cst[:, 1856 + g:1857 + g], scalar2=None, op0=ALU.mult),
                          [(Sst_fr, c), "cst"], [Sblk["f"][1]])
                        V(lambda e, g=g: e.tensor_scalar(out=Sblk["b"][0][:, g, :], in0=Sssd, scalar1=self.cst[:, 1856 + g:1857 + g], scalar2=None, op0=ALU.mult),
                          [Sssdr, "cst"], [Sblk["b"][1]])
                    if "b4" in self.debug:
                        continue
                    pyo = {}
                    for X in "fb":
                        p_, pr_ = self.ps()
                        pyo[X] = (p_, pr_)
                        PE(lambda e, p_=p_, X=X: e.matmul(p_[:], lhsT=BCT[:, 1, :], rhs=Sblk[X][0].rearrange("p a t -> p (a t)"), start=True, stop=True),
                           [BCTr, Sblk[X][1]], [pr_])
                    h8 = lambda ap: ap.rearrange("p (h d) -> p h d", h=8)
                    V(lambda e: e.tensor_tensor(out=h8(y1), in0=h8(pyo["f"][0][:]), in1=bc(eacs[:, 0:8], 64), op=ALU.mult), [pyo["f"][1], eacsr], [y1r])
                    V(lambda e: e.tensor_tensor(out=h8(y2), in0=h8(pyo["b"][0][:]), in1=bc(eacs[:, 8:16], 64), op=ALU.mult), [pyo["b"][1], eacsr], [y2r])
                    V(lambda e: e.tensor_tensor(out=y1, in0=y1, in1=y2, op=ALU.add), [y1r, y2r], [y1r])
                    V(lambda e: e.tensor_tensor(out=h8(y2), in0=h8(xa[:, 0:512]), in1=bc(rb[:, RB_D:RB_D + 8], 64), op=ALU.mult), [xar, "rb"], [y2r])
                    V(lambda e: e.tensor_tensor(out=y1, in0=y1, in1=y2, op=ALU.add), [y1r, y2r], [y1r])
                    V(lambda e: e.tensor_tensor(out=y1, in0=y1, in1=pyd[:], op=ALU.add), [y1r, pydr], [y1r])
                    V(lambda e: e.tensor_tensor(out=y1, in0=y1, in1=z_, op=ALU.mult), [y1r, zr_], [y1r])
                    if "b6" in self.debug:
                        continue
                    for g in range(2):
                        A(lambda e, g=g: e.activation(out=junk[:, g * 256:(g + 1) * 256], in_=y1[:, g * 256:(g + 1) * 256], func=AF.Square,
                                                      accum_out=ss2[:, g:g + 1]), [y1r], [junkr, ss2r])
                    V(lambda e: e.tensor_scalar(out=ss2, in0=ss2, scalar1=1.0 / 256, scalar2=EPS, op0=ALU.mult, op1=ALU.add), [ss2r], [ss2r])
                    A(lambda e: e.activation(out=ss2, in_=ss2, func=AF.Sqrt), [ss2r], [ss2r])
                    V(lambda e: e.reciprocal(out=ss2, in_=ss2), [ss2r], [ss2r])
                    for g in range(2):
                        V(lambda e, g=g: e.scalar_tensor_tensor(out=ya[:, g * 256:(g + 1) * 256], in0=y1[:, g * 256:(g + 1) * 256], scalar=ss2[:, g:g + 1],
                                                                in1=rb[:, RB_SSDN + g * 256:RB_SSDN + (g + 1) * 256], op0=ALU.mult, op1=ALU.mult),
                          [y1r, ss2r, "rb"], [yar])
                    ptb, ptr_ = self.ps()
                    ptb3 = ptb[:].bitcast(BF16)[:, 0:512].rearrange("p (a t) -> p a t", a=4)
                    for a_ in range(4):
                        PE(lambda e, a_=a_, ptb3=ptb3: e.transpose(ptb3[:, a_, :], ya[:, a_ * 128:(a_ + 1) * 128], self.identb), [yar, "cstb"], [ptr_], inc=(a_ == 3))
                    b_, br_ = next_brs()
                    V(lambda e, b_=b_, ptb3=ptb3: e.tensor_copy(out=b_, in_=ptb3), [ptr_], [br_])
                    store_branch(0, b_, br_, tk)
                    if "b7" not in self.debug:
                        ssd_state_update("b", c, False)
                i = 0
                if "noBgla" not in self.debug:
                    i = gla_common(tk, "fb")
                    q_, qr_ = qT[i]
                    S.dma("sp", f"qT{i}", lambda e, q_=q_, tk=tk: e.dma_start(out=q_, in_=self.P["q"][:, :, tk:tk + 128].rearrange("h p t -> p h t")), writes=[qr_])
                    r_, rr_ = rT[i]
                    S.dma("sp", f"rT{i}", lambda e, r_=r_, tk=tk: e.dma_start(out=r_, in_=self.P["r"][:, :, tk:tk + 128].rearrange("h p t -> p h t")), writes=[rr_])
                    v_, vr_ = vtm[i]
                    V(lambda e, q_=q_: e.tensor_tensor(out=qdec.rearrange("p (x h) t -> p x h t", x=2), in0=eq.rearrange("p (x h) t -> p x h t", x=2),
                                                       in1=q_.unsqueeze(1).to_broadcast([128, 2, 2, 128]), op=ALU.mult), [eqr, qr_], [qdecr])
                    for hh in range(2):
                        V(lambda e, hh=hh: e.tensor_scalar(out=Qblk[:, :, hh, :], in0=qdec, scalar1=self.cst[:, 1856 + hh:1857 + hh], scalar2=None, op0=ALU.mult),
                          [qdecr, "cst"], [Qblkr])
                    for X in "fb":
                        xi = 0 if X == "f" else 1
                        pa, par = self.ps()
                        pa3 = pa[:].rearrange("p (h t) -> p h t", h=4)
                        for hp in range(2):
                            PE(lambda e, hp=hp, xi=xi, pa3=pa3: e.matmul(pa3[:, hp * 2:(hp + 1) * 2, :], lhsT=kinv[:, xi * 2 + hp, :], rhs=Qblk[:, xi * 2 + hp, :, :],
                                                                         start=True, stop=True), [kinvr, Qblkr], [par], inc=(hp == 1))
                        am, amr = attm[X]
                        V(lambda e, am=am, pa3=pa3, X=X: e.tensor_tensor(out=am, in0=pa3, in1=self.m01[X].unsqueeze(1).to_broadcast([128, 4, 128]), op=ALU.mult),
                          [par, "cstb"], [amr])
                    V(lambda e: e.tensor_copy(out=Gin_b, in_=Sgla), [Sglar], [Gin_br])
                    po, por = self.ps()
                    po3 = po[:].rearrange("p (h t) -> p h t", h=4)
                    for h in range(4):
                        hp, hh = h // 2, h % 2
                        rs = slice(hh * 64, (hh + 1) * 64)
                        PE(lambda e, h=h: e.matmul(po3[:, h, :], lhsT=v_[:, h * 128:(h + 1) * 128], rhs=attm["f"][0][:, h, :], start=True, stop=False),
                           [vr_, attm["f"][1]], [por], inc=False)
                        PE(lambda e, h=h: e.matmul(po3[:, h, :], lhsT=v_[:, h * 128:(h + 1) * 128], rhs=attm["b"][0][:, h, :], start=False, stop=False),
                           [vr_, attm["b"][1]], [por], inc=False)
                        PE(lambda e, h=h, hp=hp, hh=hh: e.matmul(po3[:, h, :], lhsT=Sgl_f[:, c, hp * 128:(hp + 1) * 128], rhs=Qblk[:, hp, hh, :], start=False, stop=False),
                           [(Sgl_fr, c), Qblkr], [por], inc=False)
                        PE(lambda e, h=h, hp=hp, hh=hh: e.matmul(po3[:, h, :], lhsT=Gin_b[:, hp, :], rhs=Qblk[:, 2 + hp, hh, :], start=False, stop=True),
                           [Gin_br, Qblkr], [por], inc=(h == 3))
                    A(lambda e: e.activation(out=osq, in_=po[:], func=AF.Square), [por], [osqr])
                    pss, pssr = self.ps()
                    PE(lambda e: e.matmul(pss[:], lhsT=self.onesb, rhs=osq, start=True, stop=True), [osqr, "cstb"], [pssr])
                    V(lambda e: e.tensor_scalar(out=orstd, in0=pss[:], scalar1=1.0 / 128, scalar2=EPS, op0=ALU.mult, op1=ALU.add), [pssr], [orstdr])
                    A(lambda e: e.activation(out=orstd, in_=orstd, func=AF.Sqrt), [orstdr], [orstdr])
                    V(lambda e: e.reciprocal(out=orstd, in_=orstd), [orstdr], [orstdr])
                    V(lambda e: e.tensor_tensor(out=o1, in0=po[:], in1=orstd, op=ALU.mult), [por, orstdr], [o1r])
                    V(lambda e, r_=r_: e.tensor_tensor(out=rg, in0=r_, in1=bc(pp[:, PP_GLAN:PP_GLAN + 4], 128), op=ALU.mult), [rr_, "pp"], [rgr])
                    b_, br_ = next_brs()
                    V(lambda e, b_=b_: e.tensor_tensor(out=b_, in0=o1.rearrange("p (h t) -> p h t", h=4), in1=rg, op=ALU.mult), [o1r, rgr], [br_])
                    store_branch(3, b_, br_, tk)
                    gla_state_update("b", c, i, False)
                if "noBsgu" not in self.debug:
                    vv, vvr = vt[i]
                    S.dma("sp", f"vt{i}", lambda e, vv=vv, tk=tk: e.dma_start(out=vv, in_=self.P["v"][tk:tk + 128, :]), writes=[vvr])
                    uu, uur = uT[i]
                    S.dma("sp", f"uT{i}", lambda e, uu=uu, tk=tk: e.dma_start(out=uu, in_=self.P["u"][:, :, tk:tk + 128].rearrange("h p t -> p h t")), writes=[uur])
                    A(lambda e, vv=vv: e.activation(out=junk, in_=vv, func=AF.Square, accum_out=ss1[:, 0:1]), [vvr], [junkr, ss1r])
                    V(lambda e: e.tensor_scalar(out=ss1[:, 0:1], in0=ss1[:, 0:1], scalar1=1.0 / 512, scalar2=EPS, op0=ALU.mult, op1=ALU.add), [ss1r], [ss1r])
                    A(lambda e: e.activation(out=ss1[:, 0:1], in_=ss1[:, 0:1], func=AF.Sqrt), [ss1r], [ss1r])
                    V(lambda e: e.reciprocal(out=ss1[:, 0:1], in_=ss1[:, 0:1]), [ss1r], [ss1r])
                    V(lambda e, vv=vv: e.scalar_tensor_tensor(out=vf, in0=vv, scalar=ss1[:, 0:1], in1=rb[:, RB_SGUN:RB_SGUN + 512], op0=ALU.mult, op1=ALU.mult),
                      [vvr, ss1r, "rb"], [vfr])
                    psg, psgr = self.ps()
                    for g in range(4):
                        PE(lambda e, g=g, psg=psg: e.matmul(psg[:, g * 128:(g + 1) * 128], lhsT=vf[:, g * 128:(g + 1) * 128], rhs=self.sguw[:, g * 128:(g + 1) * 128],
                                                            start=True, stop=True), [vfr, "sguw"], [psgr], inc=(g == 3))
                    V(lambda e, psg=psg: e.tensor_tensor(out=sg1.rearrange("p h t -> p (h t)"), in0=psg[:], in1=rb[:, RB_SGUB:RB_SGUB + 512], op=ALU.add), [psgr, "rb"], [sg1r])
                    b_, br_ = next_brs()
                    V(lambda e, b_=b_, uu=uu: e.tensor_tensor(out=b_, in0=sg1, in1=uu, op=ALU.mult), [sg1r, uur], [br_])
                    store_branch(2, b_, br_, tk)
                if "noBpool" not in self.debug:
                    pl, plr = ptl[i]
                    S.dma("sp", f"ptl{i}", lambda e, pl=pl, tp=tp: e.dma_start(out=pl, in_=self.P["p"][:, :, tp - 8:tp + 136].rearrange("g p t -> p g t")), writes=[plr])
                    V(lambda e, pl=pl: e.tensor_tensor(out=wsum[:, 0, :], in0=pl[:, 0, 7:135], in1=pl[:, 0, 8:136], op=ALU.add), [plr], [wsumr + "0"])
                    V(lambda e, pl=pl: e.tensor_tensor(out=a1[:, :, 1:144], in0=pl[:, 1:4, 0:143], in1=pl[:, 1:4, 1:144], op=ALU.add), [plr], [a1r])
                    V(lambda e: e.tensor_tensor(out=wsum[:, 1, :], in0=a1[:, 0, 7:135], in1=a1[:, 0, 9:137], op=ALU.add), [a1r], [wsumr + "1"])
                    V(lambda e: e.tensor_tensor(out=a2[:, :, 2:143], in0=a1[:, 1:3, 1:142], in1=a1[:, 1:3, 3:144], op=ALU.add), [a1r], [a2r])
                    V(lambda e: e.tensor_tensor(out=wsum[:, 2, :], in0=a2[:, 0, 6:134], in1=a2[:, 0, 10:138], op=ALU.add), [a2r], [wsumr + "2"])
                    V(lambda e: e.tensor_tensor(out=a3[:, 4:141], in0=a2[:, 1, 2:139], in1=a2[:, 1, 6:143], op=ALU.add), [a2r], [a3r])
                    V(lambda e: e.tensor_tensor(out=wsum[:, 3, :], in0=a3[:, 4:132], in1=a3[:, 12:140], op=ALU.add), [a3r], [wsumr + "3"])
                    wregs = [wsumr + str(g) for g in range(4)]
                    corr = self.cst[:, 1792:1856].rearrange("p (e g t) -> p e g t", e=2, g=4)
                    if c == 0:
                        V(lambda e: e.tensor_tensor(out=wsum[:, :, 0:8], in0=wsum[:, :, 0:8], in1=corr[:, 0], op=ALU.mult), wregs + ["cst"], wregs)
                    if c == nch - 1:
                        V(lambda e: e.tensor_tensor(out=wsum[:, :, 120:128], in0=wsum[:, :, 120:128], in1=corr[:, 1], op=ALU.mult), wregs + ["cst"], wregs)
                    for g, w_ in enumerate((2, 4, 8, 16)):
                        V(lambda e, g=g, w_=w_, pl=pl: e.scalar_tensor_tensor(out=pooled[:, g, :], in0=wsum[:, g, :], scalar=1.0 / w_, in1=pl[:, g, 8:136],
                                                                             op0=ALU.mult, op1=ALU.subtract), [wsumr + str(g), plr], [pooledr + str(g)])
                    ppl, pplr = self.ps()
                    for g in range(4):
                        PE(lambda e, g=g, ppl=ppl: e.matmul(ppl[:, g * 128:(g + 1) * 128], lhsT=self.poolw[:, g * 128:(g + 1) * 128], rhs=pooled[:, g, :],
                                                            start=True, stop=True), [pooledr + str(g), "poolw"], [pplr], inc=(g == 3))
                    b_, br_ = next_brs()
                    for g in range(4):
                        V(lambda e, g=g, b_=b_, ppl=ppl: e.tensor_scalar(out=b_[:, g, :], in0=ppl[:, g * 128:(g + 1) * 128], scalar1=pp[:, PP_PSC + g:PP_PSC + g + 1],
                                                                         scalar2=None, op0=ALU.mult), [pplr, "pp"], [br_])
                    S.dma("pool", "brs" + br_, lambda e, b_=b_, tk=tk: e.dma_start(
                        out=self.BR[1, :, :, tk:tk + 128].rearrange("w p t -> p w t"), in_=b_), reads=[br_], writes=[("BR", 1, tk)])

    def phase3(self, l):
        S = self.S
        self.arena_reset()
        TT = 512
        pp = self.pp
        xT = [self.arena(f"xT{i}", [8, TT]) for i in range(2)]
        a16, a16r = self.arena("a16", [8, TT], BF16)
        mo, mor = self.arena("mo", [8, TT])
        sq, sqr = self.arena("sq", [8, TT], BF16)
        rstd, rstdr = self.arena("rstd", [TT])
        hid, hidr = self.arena("hid", [32, TT], BF16)
        wb = [self.arena(f"wb{i}", [8192], BF16) for i in range(2)]
        brt = [self.arena(f"brt{i}", [4, TT], BF16) for i in range(2)]
        gt = [self.arena(f"gt{i}", [TT], BF16) for i in range(4)]
        tmp = [self.arena(f"tmp{i}", [TT]) for i in range(2)]
        rl = [self.arena(f"rl{i}", [TT], BF16) for i in range(2)]
        wc = [0]
        gc = [0]
        tc = [0]

        def wload(key, j, n):
            w, wr = wb[wc[0] % 2]
            slot = f"wb{wc[0] % 2}"
            wc[0] += 1
            S.dma("sp", slot, lambda e: e.dma_start(out=w[:, 0:n], in_=self.wsc[(l, key)][j]), writes=[wr])
            return w, wr

        def post_norm_residual(xt, xr, gcol):
            self.rstd_from_sq(sq, sqr, 8, rstd, rstdr, D)
            for ec in range(8):
                t_, tr_ = tmp[tc[0] % 2]
                tc[0] += 1
                S.op("dve", lambda e, ec=ec, t_=t_: e.scalar_tensor_tensor(
                    out=t_, in0=mo[:, ec, :], scalar=pp[:, gcol + ec:gcol + ec + 1], in1=rstd, op0=ALU.mult, op1=ALU.mult),
                    reads=[(mor, ec), rstdr, "pp"], writes=[tr_])
                S.op("dve", lambda e, ec=ec, t_=t_: e.tensor_tensor(out=xt[:, ec, :], in0=xt[:, ec, :], in1=t_, op=ALU.add),
                     reads=[tr_, (xr, ec)], writes=[(xr, ec)])

        for ti in range(self.T // TT):
            t0 = ti * TT
            xt, xr = xT[ti % 2]
            S.dma("sp", f"xT{ti % 2}", lambda e, xt=xt, t0=t0: e.dma_start(
                out=xt, in_=self.XT[:, :, t0:t0 + TT].rearrange("k p t -> p k t")),
                writes=[(xr, k) for k in range(8)])
            for n in range(4):
                w, wr = wload("br", n, 4096)
                wv = w[:, 0:4096].rearrange("p (k n) -> p k n", k=4)
                bt, btr = brt[n % 2]
                S.dma("sp", f"brt{n % 2}", lambda e, bt=bt, n=n, t0=t0: e.dma_start(
                    out=bt, in_=self.BR[n, :, :, t0:t0 + TT].rearrange("w p t -> p w t")), writes=[btr])
                for dc in range(8):
                    g_, gr_ = gt[gc[0] % 4]
                    gslot = f"gt{gc[0] % 4}"
                    gc[0] += 1
                    S.dma("sp", gslot, lambda e, g_=g_, n=n, dc=dc, t0=t0: e.dma_start(
                        out=g_, in_=self.P["gate"][n * 8 + dc, :, t0:t0 + TT]), writes=[gr_])
                    pt, pr = self.ps()
                    for k in range(4):
                        S.op("pe", lambda e, pt=pt, wv=wv, k=k, dc=dc, bt=bt: e.matmul(
                            pt[:], lhsT=wv[:, k, dc * 128:(dc + 1) * 128], rhs=bt[:, k, :], start=(k == 0), stop=(k == 3)),
                            reads=[wr, btr], writes=[pr], inc=(k == 3))
                    if n == 0:
                        S.op("dve", lambda e, pt=pt, g_=g_, dc=dc: e.tensor_tensor(out=mo[:, dc, :], in0=pt[:], in1=g_, op=ALU.mult),
                             reads=[pr, gr_], writes=[(mor, dc)])
                    else:
                        t_, tr_ = tmp[tc[0] % 2]
                        tc[0] += 1
                        S.op("dve", lambda e, pt=pt, g_=g_, t_=t_: e.tensor_tensor(out=t_, in0=pt[:], in1=g_, op=ALU.mult),
                             reads=[pr, gr_], writes=[tr_])
                        if n < 3:
                            S.op("dve", lambda e, t_=t_, dc=dc: e.tensor_tensor(out=mo[:, dc, :], in0=mo[:, dc, :], in1=t_, op=ALU.add),
                                 reads=[tr_, (mor, dc)], writes=[(mor, dc)])
                        else:
                            S.op("dve", lambda e, t_=t_, dc=dc: e.tensor_tensor(out=a16[:, dc, :], in0=mo[:, dc, :], in1=t_, op=ALU.add),
                                 reads=[tr_, (mor, dc)], writes=[(a16r, dc)])
            areads = [(a16r, k) for k in range(8)]
            for j in range(2):
                w, wr = wload("out", j, 4096)
                wv = w[:, 0:4096].rearrange("p (k n) -> p k n", k=8)
                for cc in range(4):
                    ec = j * 4 + cc
                    pt, pr = self.ps()
                    for k in range(8):
                        S.op("pe", lambda e, pt=pt, wv=wv, k=k, cc=cc: e.matmul(
                            pt[:], lhsT=wv[:, k, cc * 128:(cc + 1) * 128], rhs=a16[:, k, :], start=(k == 0), stop=(k == 7)),
                            reads=[wr] + areads, writes=[pr], inc=(k == 7))
                    S.op("dve", lambda e, pt=pt, ec=ec: e.tensor_copy(out=mo[:, ec, :], in_=pt[:]), reads=[pr], writes=[(mor, ec)])
                    S.op("act", lambda e, ec=ec: e.activation(out=sq[:, ec, :], in_=mo[:, ec, :], func=AF.Square),
                         reads=[(mor, ec)], writes=[(sqr, ec)])
            post_norm_residual(xt, xr, PP_GPOST)
            for kc in range(8):
                S.op("act", lambda e, kc=kc, xt=xt: e.activation(out=sq[:, kc, :], in_=xt[:, kc, :], func=AF.Square),
                     reads=[(xr, kc)], writes=[(sqr, kc)])
            self.rstd_from_sq(sq, sqr, 8, rstd, rstdr, D)
            for kc in range(8):
                S.op("dve", lambda e, kc=kc, xt=xt: e.scalar_tensor_tensor(
                    out=a16[:, kc, :], in0=xt[:, kc, :], scalar=pp[:, PP_GF1 + kc:PP_GF1 + kc + 1], in1=rstd,
                    op0=ALU.mult, op1=ALU.mult), reads=[(xr, kc), rstdr, "pp"], writes=[(a16r, kc)])
            for j in range(8):
                w, wr = wload("ff1", j, 4096)
                wv = w[:, 0:4096].rearrange("p (k n) -> p k n", k=8)
                for cc in range(4):
                    fc = j * 4 + cc
                    pt, pr = self.ps()
                    for k in range(8):
                        S.op("pe", lambda e, pt=pt, wv=wv, k=k, cc=cc: e.matmul(
                            pt[:], lhsT=wv[:, k, cc * 128:(cc + 1) * 128], rhs=a16[:, k, :], start=(k == 0), stop=(k == 7)),
                            reads=[wr] + areads, writes=[pr], inc=(k == 7))
                    r_, rr_ = rl[fc % 2]
                    S.op("act", lambda e, pt=pt, r_=r_: e.activation(out=r_, in_=pt[:], func=AF.Relu), reads=[pr], writes=[rr_])
                    S.op("dve", lambda e, r_=r_, fc=fc: e.tensor_tensor(out=hid[:, fc, :], in0=r_, in1=r_, op=ALU.mult),
                         reads=[rr_], writes=[(hidr, fc)])
            hreads = [(hidr, f) for f in range(32)]
            for j in range(4):
                w, wr = wload("ff2", j, 8192)
                wv = w[:, 0:8192].rearrange("p (k n) -> p k n", k=32)
                for cc in range(2):
                    ec = j * 2 + cc
                    pt, pr = self.ps()
                    for k in range(32):
                        S.op("pe", lambda e, pt=pt, wv=wv, k=k, cc=cc: e.matmul(
                            pt[:], lhsT=wv[:, k, cc * 128:(cc + 1) * 128], rhs=hid[:, k, :], start=(k == 0), stop=(k == 31)),
                            reads=[wr] + hreads, writes=[pr], inc=(k == 31))
                    S.op("dve", lambda e, pt=pt, ec=ec: e.tensor_copy(out=mo[:, ec, :], in_=pt[:]), reads=[pr], writes=[(mor, ec)])
                    S.op("act", lambda e, ec=ec: e.activation(out=sq[:, ec, :], in_=mo[:, ec, :], func=AF.Square),
                         reads=[(mor, ec)], writes=[(sqr, ec)])
            post_norm_residual(xt, xr, PP_GF2)
            S.dma("pool", f"xTs{ti % 2}", lambda e, xt=xt, t0=t0: e.dma_start(
                out=self.XT[:, :, t0:t0 + TT].rearrange("k p t -> p k t"), in_=xt),
                reads=[(xr, k) for k in range(8)], writes=[("XT", ti)])

    def phase_final(self):
        S = self.S
        self.arena_reset()
        xin = [self.arena(f"fx{i}", [8, 512]) for i in range(2)]
        xo = [self.arena(f"fo{i}", [4, D]) for i in range(2)]
        for ti in range(self.T // 512):
            xi, xir = xin[ti % 2]
            xot, xor_ = xo[ti % 2]
            t0 = ti * 512
            S.dma("sp", f"fx{ti % 2}", lambda e, xi=xi, t0=t0: e.dma_start(
                out=xi, in_=self.XT[:, :, t0:t0 + 512].rearrange("k p t -> p k t")), reads=[("XT", ti)], writes=[xir])
            for j in range(4):
                for half in range(2):
                    pt, pr = self.ps()
                    for q in range(4):
                        kc = half * 4 + q
                        S.op("pe", lambda e, pt=pt, xi=xi, j=j, kc=kc, q=q: e.transpose(
                            pt[:, q * 128:(q + 1) * 128], xi[:, kc, j * 128:(j + 1) * 128], self.identf),
                            reads=[xir, "cst"], writes=[pr], inc=(q == 3))
                    if half == 0:
                        S.op("dve", lambda e, pt=pt, xot=xot, j=j: e.tensor_copy(out=xot[:, j, 0:512], in_=pt[:]),
                             reads=[pr], writes=[(xor_, j, 0)])
                    else:
                        S.op("act", lambda e, pt=pt, xot=xot, j=j: e.copy(out=xot[:, j, 512:1024], in_=pt[:]),
                             reads=[pr], writes=[(xor_, j, 1)])
            S.dma("pool", f"fo{ti % 2}", lambda e, xot=xot, t0=t0: e.dma_start(
                out=self.y_out[t0:t0 + 512, :].rearrange("(j p) d -> p j d", p=128), in_=xot),
                reads=[(xor_, j, h) for j in range(4) for h in range(2)], writes=[("Y", ti)])


def make_consts():
    r = np.arange(128)
    ident = np.eye(128, dtype=np.float32)
    ones = np.ones((128, 128), np.float32)
    U = (r[:, None] <= r[None, :]).astype(np.float32)
    Lm = (r[:, None] >= r[None, :]).astype(np.float32)
    Uneg = U * (-1.0 / 16.0)
    Lneg = Lm * (-1.0 / 16.0)
    m01f, m01b = U.copy(), Lm.copy()
    negf = np.tile((1.0 - U) * -30000.0, (1, 4)).astype(np.float32)
    negb = np.tile((1.0 - Lm) * -30000.0, (1, 4)).astype(np.float32)
    corr = np.ones((2, 4, 8), np.float32)
    for g, w in enumerate((2, 4, 8, 16)):
        for t in range(8):
            if t < w // 2:
                corr[0, g, t] = w / (t + w // 2)
            tq = 7 - t
            if tq < w // 2 - 1:
                corr[1, g, t] = w / (tq + 1 + w // 2)
    corr = np.broadcast_to(corr.reshape(1, 64), (128, 64))
    half = np.stack([(r < 64), (r >= 64)], axis=1).astype(np.float32)
    cst = np.concatenate([ident, ones, U, Lm, Uneg, Lneg, negf, negb, corr, half], axis=1)
    pad = np.zeros((128, 1860 - cst.shape[1]), np.float32)
    return np.ascontiguousarray(np.concatenate([cst, pad], axis=1))


def pack_params(inputs, depth):
    L = depth
    f = lambda k: np.asarray(inputs[k], np.float32)
    pp = np.zeros((L, 128, PP_LEN), np.float32)
    for l in range(L):
        pp[l, :, PP_G1:PP_G1 + 8] = f("norm_mix_pre")[l].reshape(8, 128).T
        pp[l, :, PP_GPOST:PP_GPOST + 8] = f("norm_mix_post")[l].reshape(8, 128).T
        pp[l, :, PP_GF1:PP_GF1 + 8] = f("norm_ffn_pre")[l].reshape(8, 128).T
        pp[l, :, PP_GF2:PP_GF2 + 8] = f("norm_ffn_post")[l].reshape(8, 128).T
        pp[l, :, PP_PSC:PP_PSC + 4] = f("pool_scale")[l].reshape(4, 128).T
        pp[l, :, PP_GLAN:PP_GLAN + 4] = f("gla_norm")[l].reshape(4, 128).T
    rb = np.zeros((L, RB_LEN), np.float32)
    for l in range(L):
        rb[l, RB_CONVW:RB_CONVW + 3840] = f("ssd_conv_w")[l].reshape(-1)
        rb[l, RB_CONVB:RB_CONVB + 768] = f("ssd_conv_b")[l]
        rb[l, RB_DTB:RB_DTB + 16] = f("ssd_dt_bias")[l].reshape(-1)
        rb[l, RB_ALOG:RB_ALOG + 16] = f("ssd_a_log")[l].reshape(-1)
        rb[l, RB_D:RB_D + 8] = f("ssd_d")[l]
        rb[l, RB_SSDN:RB_SSDN + 512] = f("ssd_norm")[l]
        rb[l, RB_SGUN:RB_SGUN + 512] = f("sgu_norm")[l]
        rb[l, RB_SGUB:RB_SGUB + 512] = f("sgu_b")[l].reshape(-1)
    sguwT = np.ascontiguousarray(f("sgu_w")[:L].transpose(0, 3, 1, 2).reshape(L, 128, 512))
    poolw = np.ascontiguousarray(f("pool_w")[:L].transpose(0, 2, 1, 3).reshape(L, 128, 512))
    w2 = np.zeros((L, 33, 512), np.float32)
    w2[:, 0:16, 0:256] = f("gla_gate_w2")[:L, 0]
    w2[:, 16:32, 256:512] = f("gla_gate_w2")[:L, 1]
    w2[:, 32, :] = f("gla_gate_b")[:L].reshape(L, 512)
    return pp, rb, sguwT, poolw, w2


_CACHE = {}


def run(inputs, seqs_per_core, x_cores, depth, debug=()):
    key = (tuple(seqs_per_core), depth, tuple(debug))
    if key not in _CACHE:
        _CACHE[key] = Builder(seqs_per_core, depth, debug).build()
    nc = _CACHE[key]
    pp, rb, sguwT, poolw, w2 = pack_params(inputs, depth)
    f = lambda k: np.ascontiguousarray(np.asarray(inputs[k], np.float32)[:depth])
    shared = {"w_in": f("w_in"), "w_branch": f("w_branch"), "w_out": f("w_out"), "w_ff1": f("w_ff1"), "w_ff2": f("w_ff2"),
              "pp": pp, "rb": rb, "sguwT": sguwT, "poolw": poolw, "w2blk": w2, "cst": make_consts()}
    in_maps = [dict(shared, x=np.ascontiguousarray(xc)) for xc in x_cores]
    res = run_bass_kernel_spmd(nc, in_maps, core_ids=list(range(len(x_cores))))
    return res.results


def kernel(**inputs):
    xp = np.asarray(inputs["x_prompt"], np.float32)
    xs = np.asarray(inputs["x_sample"], np.float32)
    x_cores = []
    for c in range(NCORES):
        x_cores.append(np.concatenate([xp[2 * c], xp[2 * c + 1], xs[c]], axis=0))
    res = run(inputs, [2048, 2048, 8192], x_cores, DEPTH)
    yp = np.zeros_like(xp)
    ys = np.zeros_like(xs)
    for c in range(NCORES):
        y = res[c]["y"]
        yp[2 * c] = y[0:2048]
        yp[2 * c + 1] = y[2048:4096]
        ys[c] = y[4096:]
    return (yp, ys)
```

```python
import numpy as np
from contextlib import ExitStack
import concourse.bass as bass
import concourse.mybir as mybir
from concourse.bass_utils import run_bass_kernel_spmd

F32 = mybir.dt.float32
BF16 = mybir.dt.bfloat16
AF = mybir.ActivationFunctionType
ALU = mybir.AluOpType

D = 1024
DEPTH = 4
PAD = 8
EPS = 1e-6
NCORES = 8


class _Rec:
    def __init__(self):
        self.call = None

    def __getattr__(self, name):
        def f(*a, **k):
            self.call = (name, a, k)
            return self
        return f


def _record(fn):
    r = _Rec()
    fn(r)
    return r.call


class Sched:
    SEM_MAX = 60000

    def __init__(self, nc, sems):
        self.nc = nc
        self.free_sems = list(sems)
        self.eng_names = ["pe", "act", "dve", "pool", "sp"]
        self.prog = {e: [] for e in self.eng_names}
        self.esem = {}
        self.ecnt = {}
        for e in self.eng_names:
            self.esem[e] = self.free_sems.pop()
            self.ecnt[e] = 0
        self.seen = {e: {} for e in self.eng_names}
        self.last_w = {}
        self.readers = {}
        self.dma_sem = {}
        self.all_sems = {}
        self.n_instr = 0
        self.pend_r = {e: [] for e in self.eng_names}
        self.pend_w = {e: [] for e in self.eng_names}
        self.yield_hook = None

    def _new_sem(self):
        return self.free_sems.pop()

    def _emit_wait(self, eng, tok):
        if tok is None:
            return
        sem, val, src = tok
        if src == eng and eng == "pe":
            return
        k = id(sem)
        if self.seen[eng].get(k, 0) >= val:
            return
        self.seen[eng][k] = val
        self.prog[eng].append(("wait", sem, val))

    def _deps(self, eng, reads, writes):
        for r in reads:
            self._emit_wait(eng, self.last_w.get(r))
        for w in writes:
            self._emit_wait(eng, self.last_w.get(w))
            for t in self.readers.get(w, ()):
                self._emit_wait(eng, t)

    def _commit(self, tok, reads, writes):
        for r in reads:
            self.readers.setdefault(r, []).append(tok)
        for w in writes:
            self.last_w[w] = tok
            self.readers[w] = []

    def op(self, eng, fn, reads=(), writes=(), inc=True):
        self._deps(eng, reads, writes)
        if not inc:
            self.pend_r[eng].extend(reads)
            self.pend_w[eng].extend(writes)
            self.prog[eng].append(("op", _record(fn), None, 0))
            self.n_instr += 1
            return None
        if self.ecnt[eng] >= self.SEM_MAX:
            self.esem[eng] = self._new_sem()
            self.ecnt[eng] = 0
        self.ecnt[eng] += 1
        tok = (self.esem[eng], self.ecnt[eng], eng)
        self.prog[eng].append(("op", _record(fn), self.esem[eng], 1))
        self.all_sems[id(tok[0])] = (tok[0], tok[1])
        reads = list(reads) + self.pend_r[eng]
        writes = list(writes) + self.pend_w[eng]
        self.pend_r[eng] = []
        self.pend_w[eng] = []
        self._commit(tok, reads, writes)
        self.n_instr += 1
        if self.yield_hook is not None:
            self.yield_hook()
        return tok

    def dma(self, eng, slot, fn, reads=(), writes=()):
        self._deps(eng, reads, writes)
        if slot not in self.dma_sem:
            self.dma_sem[slot] = [self._new_sem(), 0]
        st = self.dma_sem[slot]
        if st[1] + 16 > self.SEM_MAX:
            st[0] = self._new_sem()
            st[1] = 0
        st[1] += 16
        tok = (st[0], st[1], "dma")
        self.all_sems[id(st[0])] = (st[0], st[1])
        self.prog[eng].append(("op", _record(fn), st[0], 16))
        self._commit(tok, reads, writes)
        self.n_instr += 1
        return tok

    def barrier(self, engs=None):
        for e in (engs or self.eng_names):
            for sem, val in list(self.all_sems.values()):
                if val > 0:
                    self._emit_wait(e, (sem, val, "any"))

    def emit(self):
        nc = self.nc
        objs = {"pe": "tensor", "act": "scalar", "dve": "vector", "pool": "gpsimd", "sp": "sync"}
        with nc.Block() as block:
            for e in self.eng_names:
                prog = self.prog[e]

                def body(eobj, prog=prog):
                    for it in prog:
                        if it[0] == "wait":
                            eobj.wait_ge(it[1], it[2])
                        else:
                            name, a, k = it[1]
                            ins = getattr(eobj, name)(*a, **k)
                            if it[2] is not None:
                                ins.then_inc(it[2], it[3])
                getattr(block, objs[e])(body)


def interleave(S, fns):
    import threading
    n = len(fns)
    if n == 1:
        fns[0]()
        return
    sems = [threading.Semaphore(0) for _ in range(n)]
    fin = threading.Semaphore(0)
    done = [False] * n
    errs = []
    idx = {}

    def nxt(i):
        for k in range(1, n + 1):
            j = (i + k) % n
            if not done[j]:
                return j
        return None

    def worker(i):
        sems[i].acquire()
        idx[threading.get_ident()] = i
        try:
            fns[i]()
        except BaseException as ex:
            errs.append(ex)
        finally:
            done[i] = True
            j = nxt(i)
            if j is None:
                fin.release()
            else:
                sems[j].release()

    def hook():
        i = idx.get(threading.get_ident())
        if i is None:
            return
        j = nxt(i)
        if j is not None and j != i:
            sems[j].release()
            sems[i].acquire()

    ths = [threading.Thread(target=worker, args=(i,)) for i in range(n)]
    for t in ths:
        t.start()
    S.yield_hook = hook
    sems[0].release()
    fin.acquire()
    S.yield_hook = None
    for t in ths:
        t.join()
    if errs:
        raise errs[0]


WIN_GROUPS = [
    ("z", 0, 512, "tm", 512),
    ("xbc", 512, 768, "tm", 384),
    ("dt", 1280, 16, "tm", 16),
    ("p", 1296, 512, "fm", 512),
    ("u", 1808, 512, "fm", 512),
    ("v", 2320, 512, "tm", 512),
    ("q", 2832, 256, "fm", 256),
    ("k", 3088, 256, "fm", 256),
    ("gv", 3344, 512, "tm", 512),
    ("r", 3856, 512, "fm", 512),
    ("glr", 4368, 32, "fm", 32),
    ("gate", 4400, 4096, "fm", 512),
]
RB_CONVW = 0
RB_CONVB = RB_CONVW + 5 * 768
RB_DTB = RB_CONVB + 768
RB_ALOG = RB_DTB + 16
RB_D = RB_ALOG + 16
RB_SSDN = RB_D + 8
RB_SGUN = RB_SSDN + 512
RB_SGUB = RB_SGUN + 512
RB_LEN = RB_SGUB + 512
PP_G1, PP_GPOST, PP_GF1, PP_GF2, PP_PSC, PP_GLAN = 0, 8, 16, 24, 32, 36
PP_LEN = 40


class Builder:
    def __init__(self, seqs, depth, debug=(), TT=512):
        self.seqs = list(seqs)
        self.depth = depth
        self.TT = TT
        self.T = sum(seqs)
        self.soff = [int(x) for x in np.cumsum([0] + self.seqs[:-1])]
        self.poff = [int(x) for x in (np.cumsum([0] + [L + 2 * PAD for L in self.seqs[:-1]]) + PAD)]
        self.Tp = sum(L + 2 * PAD for L in self.seqs)
        self.debug = set(debug)
        self.nc = bass.Bass("TRN2", target_bir_lowering=False)
        self.es = ExitStack()
        self.psn = 0

    def dram(self, name, shape, dt, kind="Internal"):
        if name in self.debug:
            kind = "ExternalOutput"
        return self.nc.dram_tensor(name, list(shape), dt, kind=kind).ap()

    def sb(self, name, shape, dt=F32):
        return self.es.enter_context(self.nc.sbuf_tensor("sb_" + name, list(shape), dt))

    def arena_reset(self):
        self.S.barrier()
        self.aoff = 0
        self.aphase += 1

    def arena(self, name, free_shape, dt=F32):
        n = int(np.prod(free_shape))
        words = n if dt == F32 else (n + 1) // 2
        a = self.aoff
        self.aoff += words
        assert self.aoff <= self.AWORDS, (name, self.aoff)
        v = self.arena_t[:, a:a + words]
        if dt != F32:
            v = v.bitcast(dt)[:, 0:n]
        if len(free_shape) == 2:
            v = v.rearrange("p (a b) -> p a b", a=free_shape[0])
        elif len(free_shape) == 3:
            v = v.rearrange("p (a b c) -> p a b c", a=free_shape[0], b=free_shape[1])
        return v, f"A{self.aphase}:{name}"

    def ps(self):
        i = self.psn % 8
        self.psn += 1
        return self.psum[i], f"ps{i}"

    def build(self):
        nc, es = self.nc, self.es
        T, Tp, L = self.T, self.Tp, self.depth
        dr = self.dram
        inp = lambda name, shape: nc.dram_tensor(name, list(shape), F32, kind="ExternalInput").ap()
        self.x_in = inp("x", [T, D])
        self.y_out = nc.dram_tensor("y", [T, D], F32, kind="ExternalOutput").ap()
        self.w_in = inp("w_in", [L, D, 8496])
        self.w_branch = inp("w_branch", [L, 4, 512, D])
        self.w_out = inp("w_out", [L, D, D])
        self.w_ff1 = inp("w_ff1", [L, D, 4096])
        self.w_ff2 = inp("w_ff2", [L, 4096, D])
        self.pp_in = inp("pp", [L, 128, PP_LEN])
        self.rb_in = inp("rb", [L, RB_LEN])
        self.sguw_in = inp("sguwT", [L, 128, 512])
        self.poolw_in = inp("poolw", [L, 128, 512])
        self.w2_in = inp("w2blk", [L, 33, 512])
        self.cst_in = inp("cst", [128, 1860])
        self.XT = dr("XT", [8, 128, T], F32)
        self.wsc = {}
        for l in range(L):
            for (g, off, n, ori, nb) in WIN_GROUPS:
                self.wsc[(l, g)] = dr(f"w{l}_{g}", [n // nb, 128, 8 * nb], BF16)
            self.wsc[(l, "br")] = dr(f"w{l}_br", [4, 128, 4 * 1024], BF16)
            self.wsc[(l, "out")] = dr(f"w{l}_out", [2, 128, 8 * 512], BF16)
            self.wsc[(l, "ff1")] = dr(f"w{l}_ff1", [8, 128, 8 * 512], BF16)
            self.wsc[(l, "ff2")] = dr(f"w{l}_ff2", [4, 128, 32 * 256], BF16)
        self.P = {}
        self.P["z"] = dr("P_z", [T, 512], BF16)
        self.P["xbc"] = dr("P_xbc", [Tp, 768], BF16)
        self.P["dt"] = dr("P_dt", [T, 16], F32)
        self.P["p"] = dr("P_p", [4, 128, Tp], BF16)
        self.P["u"] = dr("P_u", [4, 128, T], BF16)
        self.P["v"] = dr("P_v", [T, 512], BF16)
        self.P["q"] = dr("P_q", [2, 128, T], BF16)
        self.P["k"] = dr("P_k", [2, 128, T], BF16)
        self.P["gv"] = dr("P_gv", [T, 512], BF16)
        self.P["r"] = dr("P_r", [4, 128, T], BF16)
        self.P["glr"] = dr("P_glr", [32, T], BF16)
        self.P["gate"] = dr("P_gate", [32, 128, T], BF16)
        self.BR = dr("BR", [4, 4, 128, T], BF16)

        sems = [es.enter_context(nc.semaphore(f"s{i}")) for i in range(100)]
        self.S = Sched(nc, sems)
        self.psum = [es.enter_context(nc.psum_tensor(f"psb{i}", [128, 512], F32)) for i in range(8)]
        self.cst = self.sb("cst", [128, 1860])
        self.cstb = self.sb("cstb", [128, 4 * 128 + 2 * 512], BF16)
        self.AWORDS = 42000
        self.arena_t = self.sb("arena", [128, self.AWORDS], F32)
        self.aoff = 0
        self.aphase = 0
        self.pp = self.sb("pp", [128, PP_LEN])
        self.rb = self.sb("rbt", [128, RB_LEN])
        self.sguw = self.sb("sguw", [128, 512], BF16)
        self.poolw = self.sb("poolw", [128, 512], BF16)
        self.w2b = self.sb("w2b", [33, 512], BF16)
        self.anegt = self.sb("anegt", [128, 16])
        self.zero_t = self.sb("zero_t", [128, 768], BF16)

        S = self.S
        cst = self.cst
        S.dma("sp", "cst", lambda e: e.dma_start(out=cst[:], in_=self.cst_in[:, :]), writes=["cst"])
        self.identf = cst[:, 0:128]
        self.onesf = cst[:, 128:256]
        self.Uf = cst[:, 256:384]
        self.Lf = cst[:, 384:512]
        self.Uneg = cst[:, 512:640]
        self.Lneg = cst[:, 640:768]
        cb = self.cstb
        S.op("dve", lambda e: e.tensor_copy(out=cb[:, 0:128], in_=cst[:, 0:128]), reads=["cst"], writes=["cstb"])
        S.op("dve", lambda e: e.tensor_copy(out=cb[:, 128:256], in_=cst[:, 128:256]), reads=["cst"], writes=["cstb"])
        S.op("dve", lambda e: e.tensor_copy(out=cb[:, 256:512], in_=cst[:, 256:512]), reads=["cst"], writes=["cstb"])
        S.op("dve", lambda e: e.tensor_copy(out=cb[:, 512:1536], in_=cst[:, 768:1792]), reads=["cst"], writes=["cstb"])
        self.identb = cb[:, 0:128]
        self.onesb = cb[:, 128:256]
        self.m01 = {"f": cb[:, 256:384], "b": cb[:, 384:512]}
        self.negm = {"f": cb[:, 512:1024], "b": cb[:, 1024:1536]}
        self.tri = {"f": self.Uf, "b": self.Lf}
        self.trineg = {"f": self.Uneg, "b": self.Lneg}
        zt = self.zero_t
        S.op("dve", lambda e: e.memset(zt[:], 0.0), writes=["zero_t"])
        self.zero_pads()
        self.phase0_x()
        for l in range(L):
            self.prep_weights(l)
        for l in range(L):
            self.layer(l)
        self.phase_final()
        S.barrier(["sp"])
        S.emit()
        return nc

    def zero_pads(self):
        S, zt = self.S, self.zero_t
        for s, Ls in enumerate(self.seqs):
            for (r0) in (self.poff[s] - PAD, self.poff[s] + Ls):
                S.dma("pool", "zp", lambda e, r0=r0: e.dma_start(out=self.P["xbc"][r0:r0 + PAD, :], in_=zt[0:PAD, :]),
                      reads=["zero_t"], writes=[("Pxbc_pad", r0)])
                S.dma("pool", "zp", lambda e, r0=r0: e.dma_start(
                    out=self.P["p"][:, :, r0:r0 + PAD].rearrange("g p t -> p g t"),
                    in_=zt[:, 0:4 * PAD].rearrange("p (g t) -> p g t", g=4)),
                    reads=["zero_t"], writes=[("Pp_pad", r0)])

    def phase0_x(self):
        S = self.S
        self.arena_reset()
        xin = [self.arena(f"xin{i}", [4, D]) for i in range(2)]
        xo = [self.arena(f"xo{i}", [8, 512]) for i in range(2)]
        for ti in range(self.T // 512):
            xi, xir = xin[ti % 2]
            xot, xor_ = xo[ti % 2]
            t0 = ti * 512
            S.dma("sp", f"xin{ti % 2}", lambda e, xi=xi, t0=t0: e.dma_start(
                out=xi, in_=self.x_in[t0:t0 + 512, :].rearrange("(j p) d -> p j d", p=128)), writes=[xir])
            for kc in range(8):
                pt, pr = self.ps()
                for j in range(4):
                    S.op("pe", lambda e, pt=pt, xi=xi, j=j, kc=kc: e.transpose(
                        pt[:, j * 128:(j + 1) * 128], xi[:, j, kc * 128:(kc + 1) * 128], self.identf),
                        reads=[xir, "cst"], writes=[pr], inc=(j == 3))
                eng = "dve" if kc % 2 == 0 else "act"
                if eng == "dve":
                    S.op("dve", lambda e, pt=pt, xot=xot, kc=kc: e.tensor_copy(out=xot[:, kc, :], in_=pt[:]),
                         reads=[pr], writes=[(xor_, kc)])
                else:
                    S.op("act", lambda e, pt=pt, xot=xot, kc=kc: e.copy(out=xot[:, kc, :], in_=pt[:]),
                         reads=[pr], writes=[(xor_, kc)])
            S.dma("pool", f"xo{ti % 2}", lambda e, xot=xot, t0=t0: e.dma_start(
                out=self.XT[:, :, t0:t0 + 512].rearrange("k p t -> p k t"), in_=xot),
                reads=[(xor_, kc) for kc in range(8)], writes=[("XT", ti)])

    def prep_weights(self, l):
        S = self.S
        self.arena_reset()
        st32 = [self.arena(f"ws32_{i}", [8192]) for i in range(2)]
        st16 = [self.arena(f"ws16_{i}", [8192], BF16) for i in range(2)]
        cnt = [0]

        def block(src2d, kc_n, c0, nb, dst_blk):
            i = cnt[0] % 2
            cnt[0] += 1
            s32, r32 = st32[i]
            s16, r16 = st16[i]
            n = kc_n * nb
            S.dma("sp", f"ws32_{i}", lambda e: e.dma_start(
                out=s32[:, 0:n].rearrange("p (k n) -> p k n", k=kc_n),
                in_=src2d[:, c0:c0 + nb].rearrange("(k p) n -> p k n", p=128)), writes=[r32])
            eng = ["dve", "act"][cnt[0] % 2]
            if eng == "act":
                S.op("act", lambda e: e.copy(out=s16[:, 0:n], in_=s32[:, 0:n]), reads=[r32], writes=[r16])
            else:
                S.op("dve", lambda e: e.tensor_copy(out=s16[:, 0:n], in_=s32[:, 0:n]), reads=[r32], writes=[r16])
            S.dma("pool", f"ws16_{i}", lambda e: e.dma_start(out=dst_blk, in_=s16[:, 0:n]), reads=[r16],
                  writes=[("wsc", l)])

        for (g, off, n, ori, nb) in WIN_GROUPS:
            for j in range(n // nb):
                block(self.w_in[l], 8, off + j * nb, nb, self.wsc[(l, g)][j])
        for n_ in range(4):
            block(self.w_branch[l, n_], 4, 0, 1024, self.wsc[(l, "br")][n_])
        for j in range(2):
            block(self.w_out[l], 8, j * 512, 512, self.wsc[(l, "out")][j])
        for j in range(8):
            block(self.w_ff1[l], 8, j * 512, 512, self.wsc[(l, "ff1")][j])
        for j in range(4):
            block(self.w_ff2[l], 32, j * 256, 256, self.wsc[(l, "ff2")][j])

    def layer(self, l):
        self.load_layer_params(l)
        self.phase1(l)
        if "stop1" in self.debug:
            return
        self.phase2(l)
        if "stop2" in self.debug:
            return
        self.phase3(l)

    def load_layer_params(self, l):
        S = self.S
        self.arena_reset()
        S.dma("sp", "pp", lambda e: e.dma_start(out=self.pp[:], in_=self.pp_in[l]), writes=["pp"])
        S.dma("sp", "rb", lambda e: e.dma_start(out=self.rb[:], in_=self.rb_in[l:l + 1, :].partition_broadcast(128)),
              writes=["rb"])
        t32, r32 = self.arena("lp32", [512])
        for (src, dst, name, rows) in ((self.sguw_in, self.sguw, "sguw", 128), (self.poolw_in, self.poolw, "poolw", 128),
                                       (self.w2_in, self.w2b, "w2b", 33)):
            S.dma("sp", "lp32", lambda e, src=src, rows=rows: e.dma_start(out=t32[0:rows, :], in_=src[l]), writes=[r32])
            S.op("dve", lambda e, dst=dst, rows=rows: e.tensor_copy(out=dst[0:rows, :], in_=t32[0:rows, :]),
                 reads=[r32], writes=[name])
        S.op("act", lambda e: e.activation(out=self.anegt[:], in_=self.rb[:, RB_ALOG:RB_ALOG + 16], func=AF.Exp),
             reads=["rb"], writes=["anegt"])
        S.op("dve", lambda e: e.tensor_scalar(out=self.anegt[:], in0=self.anegt[:], scalar1=-1.0, scalar2=None,
                                              op0=ALU.mult), reads=["anegt"], writes=["anegt"])

    def rstd_from_sq(self, sq, sqr, nk, rstd, rstdr, dim):
        S = self.S
        pt, pr = self.ps()
        for kc in range(nk):
            S.op("pe", lambda e, kc=kc: e.matmul(pt[:], lhsT=self.onesb, rhs=sq[:, kc, :], start=(kc == 0),
                                                 stop=(kc == nk - 1)),
                 reads=[(sqr, k_) for k_ in range(nk)] + ["cstb"], writes=[pr], inc=(kc == nk - 1))
        S.op("dve", lambda e: e.tensor_scalar(out=rstd, in0=pt[:], scalar1=1.0 / dim, scalar2=EPS, op0=ALU.mult,
                                              op1=ALU.add), reads=[pr], writes=[rstdr])
        S.op("act", lambda e: e.activation(out=rstd, in_=rstd, func=AF.Sqrt), reads=[rstdr], writes=[rstdr])
        S.op("dve", lambda e: e.reciprocal(out=rstd, in_=rstd), reads=[rstdr], writes=[rstdr])

    def phase1(self, l):
        S = self.S
        self.arena_reset()
        TT = 512
        xT = [self.arena(f"xT{i}", [8, TT]) for i in range(2)]
        sq, sqr = self.arena("sq", [8, TT], BF16)
        hT, hTr = self.arena("hT", [8, TT], BF16)
        rstd, rstdr = self.arena("rstd", [TT])
        wb = [self.arena(f"wb{i}", [8192], BF16) for i in range(3)]
        stg = [self.arena(f"stg{i}", [512], BF16) for i in range(4)]
        stg32 = [self.arena(f"stgf{i}", [16]) for i in range(2)]
        wcnt = [0]
        scnt = [0]
        pp = self.pp
        def pad_off(t0):
            for s in range(len(self.seqs)):
                if self.soff[s] <= t0 < self.soff[s] + self.seqs[s]:
                    return self.poff[s] + (t0 - self.soff[s])
            raise AssertionError
        evac_funcs = {"u": AF.Gelu_apprx_tanh, "v": AF.Gelu_apprx_tanh, "r": AF.Silu, "z": AF.Silu, "gate": AF.Sigmoid}
        for ti in range(self.T // TT):
            t0 = ti * TT
            tp0 = pad_off(t0)
            xt, xr = xT[ti % 2]
            S.dma("sp", f"xT{ti % 2}", lambda e, xt=xt, t0=t0: e.dma_start(
                out=xt, in_=self.XT[:, :, t0:t0 + TT].rearrange("k p t -> p k t")), reads=[("XT", ti)], writes=[xr])
            for kc in range(8):
                S.op("act", lambda e, kc=kc, xt=xt: e.activation(out=sq[:, kc, :], in_=xt[:, kc, :], func=AF.Square),
                     reads=[xr], writes=[(sqr, kc)])
            self.rstd_from_sq(sq, sqr, 8, rstd, rstdr, D)
            for kc in range(8):
                S.op("dve", lambda e, kc=kc, xt=xt: e.scalar_tensor_tensor(
                    out=hT[:, kc, :], in0=xt[:, kc, :], scalar=pp[:, PP_G1 + kc:PP_G1 + kc + 1], in1=rstd,
                    op0=ALU.mult, op1=ALU.mult), reads=[xr, rstdr, "pp"], writes=[(hTr, kc)])
            hreads = [(hTr, kc) for kc in range(8)]
            for (g, off, n, ori, nb) in WIN_GROUPS:
                for j in range(n // nb):
                    w, wr = wb[wcnt[0] % 3]
                    slot = f"wb{wcnt[0] % 3}"
                    wcnt[0] += 1
                    S.dma("sp", slot, lambda e, w=w, g=g, j=j, nb=nb: e.dma_start(out=w[:, 0:8 * nb], in_=self.wsc[(l, g)][j]),
                          reads=[("wsc", l)], writes=[wr])
                    wv = w[:, 0:8 * nb].rearrange("p (k n) -> p k n", k=8)
                    if ori == "fm":
                        for cc in range(max(1, nb // 128)):
                            m = min(128, nb)
                            pt, pr = self.ps()
                            for kc in range(8):
                                S.op("pe", lambda e, pt=pt, wv=wv, cc=cc, kc=kc, m=m: e.matmul(
                                    pt[0:m, :], lhsT=wv[:, kc, cc * 128:cc * 128 + m], rhs=hT[:, kc, :],
                                    start=(kc == 0), stop=(kc == 7)), reads=[wr] + hreads, writes=[pr], inc=(kc == 7))
                            st, sr = stg[scnt[0] % 4]
                            sslot = f"stg{scnt[0] % 4}"
                            scnt[0] += 1
                            cidx = j * (nb // 128) + cc if nb >= 128 else 0
                            fn = evac_funcs.get(g)
                            if fn is not None:
                                S.op("act", lambda e, st=st, pt=pt, fn=fn, m=m: e.activation(out=st[0:m, :], in_=pt[0:m, :], func=fn),
                                     reads=[pr], writes=[sr])
                            elif g == "q":
                                S.op("act", lambda e, st=st, pt=pt, m=m: e.mul(out=st[0:m, :], in_=pt[0:m, :], mul=0.125),
                                     reads=[pr], writes=[sr])
                            else:
                                S.op("dve", lambda e, st=st, pt=pt, m=m: e.tensor_copy(out=st[0:m, :], in_=pt[0:m, :]),
                                     reads=[pr], writes=[sr])
                            if g == "p":
                                dst = self.P["p"][cidx, :, tp0:tp0 + TT]
                            elif g == "glr":
                                dst = self.P["glr"][:, t0:t0 + TT]
                            else:
                                dst = self.P[g][cidx, :, t0:t0 + TT]
                            S.dma("pool", sslot, lambda e, dst=dst, st=st, m=m: e.dma_start(out=dst, in_=st[0:m, :]),
                                  reads=[sr], writes=[("P" + g, ti)])
                    else:
                        for i in range(TT // 128):
                            pt, pr = self.ps()
                            for kc in range(8):
                                S.op("pe", lambda e, pt=pt, wv=wv, i=i, kc=kc, nb=nb: e.matmul(
                                    pt[:, 0:nb], lhsT=hT[:, kc, i * 128:(i + 1) * 128], rhs=wv[:, kc, :],
                                    start=(kc == 0), stop=(kc == 7)), reads=[wr] + hreads, writes=[pr], inc=(kc == 7))
                            if g == "dt":
                                st, sr = stg32[scnt[0] % 2]
                                sslot = f"stgf{scnt[0] % 2}"
                            else:
                                st, sr = stg[scnt[0] % 4]
                                sslot = f"stg{scnt[0] % 4}"
                            scnt[0] += 1
                            fn = evac_funcs.get(g)
                            if fn is not None:
                                S.op("act", lambda e, st=st, pt=pt, fn=fn, nb=nb: e.activation(out=st[:, 0:nb], in_=pt[:, 0:nb], func=fn),
                                     reads=[pr], writes=[sr])
                            else:
                                S.op("dve", lambda e, st=st, pt=pt, nb=nb: e.tensor_copy(out=st[:, 0:nb], in_=pt[:, 0:nb]),
                                     reads=[pr], writes=[sr])
                            if g == "xbc":
                                dst = self.P["xbc"][tp0 + i * 128:tp0 + (i + 1) * 128, j * nb:(j + 1) * nb]
                            else:
                                dst = self.P[g][t0 + i * 128:t0 + (i + 1) * 128, j * nb:(j + 1) * nb]
                            S.dma("pool", sslot, lambda e, dst=dst, st=st, nb=nb: e.dma_start(out=dst, in_=st[:, 0:nb]),
                                  reads=[sr], writes=[("P" + g, ti)])


    def phase2(self, l):
        S = self.S
        self.arena_reset()
        ar = self.arena
        rb, pp = self.rb, self.pp
        NCMAX = max(self.seqs) // 128
        Sst_f, Sst_fr = ar("Sst_f", [NCMAX, 256], BF16)
        Sgl_f, Sgl_fr = ar("Sgl_f", [NCMAX, 256], BF16)
        Sssd, Sssdr = ar("Sssd", [256])
        Sgla, Sglar = ar("Sgla", [2, 128])
        Sin_b, Sin_br = ar("Sin_b", [256], BF16)
        Gin_b, Gin_br = ar("Gin_b", [2, 128], BF16)
        xs0 = [ar(f"xs_{k}", [768], BF16) for k in range(5)]
        xs = [xs0, xs0]
        acc, accr = ar("acc", [768])
        ctmp, ctmpr = ar("ctmp", [768])
        xa, xar = ar("xa", [768], BF16)
        dtr = [ar(f"dtr{i}", [16]) for i in range(2)]
        dtt, dttr = ar("dtt", [16])
        dta, dtar = ar("dta", [16])
        ndta, ndtar = ar("ndta", [16])
        sc, scr = ar("sc", [32])
        d1, d1r = ar("d1", [16])
        dte, dter = ar("dte", [16])
        wst, wstr = ar("wst", [16])
        etot, etotr = ar("etot", [16])
        etsel, etselr = ar("etsel", [4])
        eacs, eacsr = ar("eacs", [16])
        xw, xwr = ar("xw", [512], BF16)
        xdt = {X: ar(f"xdt{X}", [512], BF16) for X in "fb"}
        dtaU = {X: ar(f"dtaU{X}", [8, 128]) for X in "fb"}
        BCT, BCTr = ar("BCT", [2, 128], BF16)
        CBT, CBTr = ar("CBT", [2, 128], BF16)
        Cblk, Cblkr = ar("Cblk", [2, 128], BF16)
        Sblk = {X: ar(f"Sblk{X}", [2, 256], BF16) for X in "fb"}
        Qblk, Qblkr = ar("Qblk", [4, 2, 128], BF16)
        Et = [ar(f"E{i}", [4, 128], BF16) for i in range(2)]
        Mt = {(X, g): ar(f"M{X}{g}", [4, 128], BF16) for X in "fb" for g in range(2)}
        y1, y1r = ar("y1", [512])
        y2, y2r = ar("y2", [512])
        zt = [ar(f"zt{i}", [512], BF16) for i in range(2)]
        junk, junkr = ar("junk", [512])
        ss2, ss2r = ar("ss2", [2])
        ya, yar = ar("ya", [512], BF16)
        brs = [ar(f"brs{i}", [4, 128], BF16) for i in range(4)]
        glr = [ar(f"glr{i}", [128], BF16) for i in range(2)]
        e1, e1r = ar("e1", [512])
        sp_, spr = ar("sp", [512])
        ebt, ebtr = ar("ebt", [4])
        ek, ekr = ar("ek", [4, 128])
        eq, eqr = ar("eq", [4, 128])
        kT = [ar(f"kT{i}", [2, 128], BF16) for i in range(2)]
        qT = [ar(f"qT{i}", [2, 128], BF16) for i in range(2)]
        vtm = [ar(f"vtm{i}", [512], BF16) for i in range(2)]
        rT = [ar(f"rT{i}", [4, 128], BF16) for i in range(2)]
        kinv, kinvr = ar("kinv", [4, 128], BF16)
        kend, kendr = ar("kend", [4, 128], BF16)
        qdec, qdecr = ar("qdec", [4, 128], BF16)
        kendtm, kendtmr = ar("kendtm", [4, 128], BF16)
        attm = {X: ar(f"attm{X}", [4, 128], BF16) for X in "fb"}
        osq, osqr = ar("osq", [512], BF16)
        orstd, orstdr = ar("orstd", [512])
        o1, o1r = ar("o1", [512])
        rg, rgr = ar("rg", [4, 128])
        vt = [ar(f"vt{i}", [512], BF16) for i in range(2)]
        uT = [ar(f"uT{i}", [4, 128], BF16) for i in range(2)]
        ss1, ss1r = ar("ss1", [2])
        vf, vfr = ar("vf", [512], BF16)
        sg1, sg1r = ar("sg1", [4, 128])
        ptl = [ar(f"ptl{i}", [4, 144], BF16) for i in range(2)]
        a1, a1r = ar("a1", [3, 144])
        a2, a2r = ar("a2", [2, 144])
        a3, a3r = ar("a3", [144])
        wsum, wsumr = ar("wsum", [4, 128])
        pooled, pooledr = ar("pooled", [4, 128], BF16)
        ldc = [0]
        brc = [0]
        V = lambda fn, r=(), w=(): S.op("dve", fn, reads=r, writes=w)
        A = lambda fn, r=(), w=(): S.op("act", fn, reads=r, writes=w)
        PE = lambda fn, r=(), w=(), inc=True: S.op("pe", fn, reads=r, writes=w, inc=inc)
        cw = lambda k: rb[:, RB_CONVW + k * 768:RB_CONVW + (k + 1) * 768]
        for i in range(2):
            g_, gr_ = glr[i]
            V(lambda e, g_=g_: e.memset(g_[32:33, :], 1.0), w=[gr_ + "one"])

        def bc(ap2, n):
            return ap2.unsqueeze(2).to_broadcast([ap2.shape[0], ap2.shape[1], n])

        def store_branch(n, src, srcr, tok0):
            S.dma("pool", "brs" + srcr, lambda e: e.dma_start(
                out=self.BR[n, :, :, tok0:tok0 + 128].rearrange("w p t -> p w t"), in_=src), reads=[srcr], writes=[("BR", n, tok0)])

        def next_brs():
            b = brs[brc[0] % 4]
            brc[0] += 1
            return b

        def ssd_common(tp, tk, i):
            for k in range(5):
                x_, xr_ = xs[i][k]
                S.dma("sp", f"xs{i}_{k}", lambda e, x_=x_, k=k: e.dma_start(out=x_, in_=self.P["xbc"][tp + k - 2:tp + k - 2 + 128, :]), writes=[xr_])
            d_, dr_ = dtr[i]
            S.dma("sp", f"dtr{i}", lambda e: e.dma_start(out=d_, in_=self.P["dt"][tk:tk + 128, :]), writes=[dr_])
            V(lambda e: e.tensor_tensor(out=acc, in0=xs[i][0][0], in1=cw(0), op=ALU.mult), [xs[i][0][1], "rb"], [accr])
            for k in range(1, 5):
                V(lambda e, k=k: e.tensor_tensor(out=ctmp, in0=xs[i][k][0], in1=cw(k), op=ALU.mult), [xs[i][k][1], "rb"], [ctmpr])
                V(lambda e: e.tensor_tensor(out=acc, in0=acc, in1=ctmp, op=ALU.add), [accr, ctmpr], [accr])
            V(lambda e: e.tensor_tensor(out=acc, in0=acc, in1=rb[:, RB_CONVB:RB_CONVB + 768], op=ALU.add), [accr, "rb"], [accr])
            A(lambda e: e.activation(out=xa, in_=acc, func=AF.Silu), [accr], [xar])
            if "s1" in self.debug:
                return
            V(lambda e: e.tensor_tensor(out=dtt, in0=d_, in1=rb[:, RB_DTB:RB_DTB + 16], op=ALU.add), [dr_, "rb"], [dttr])
            A(lambda e: e.activation(out=dtt, in_=dtt, func=AF.Exp), [dttr], [dttr])
            A(lambda e: e.activation(out=dtt, in_=dtt, func=AF.Ln, bias=1.0), [dttr], [dttr])
            V(lambda e: e.tensor_tensor(out=dta, in0=dtt, in1=self.anegt[:], op=ALU.mult), [dttr, "anegt"], [dtar])
            if "s2" in self.debug:
                return
            pt, pr = self.ps()
            PE(lambda e: e.matmul(pt[:, 0:8], lhsT=self.Uf, rhs=dta[:, 0:8], start=True, stop=True), [dtar, "cst"], [pr], inc=False)
            PE(lambda e: e.matmul(pt[:, 8:16], lhsT=self.Lf, rhs=dta[:, 8:16], start=True, stop=True), [dtar, "cst"], [pr], inc=False)
            PE(lambda e: e.matmul(pt[:, 16:32], lhsT=self.onesf, rhs=dta[:, 0:16], start=True, stop=True), [dtar, "cst"], [pr])
            V(lambda e: e.tensor_copy(out=sc, in_=pt[:, 0:32]), [pr], [scr])
            V(lambda e: e.tensor_tensor(out=d1, in0=sc[:, 16:32], in1=sc[:, 0:16], op=ALU.subtract), [scr], [d1r])
            A(lambda e: e.activation(out=dte, in_=d1, func=AF.Exp), [d1r], [dter])
            A(lambda e: e.activation(out=etot, in_=sc[:, 16:32], func=AF.Exp), [scr], [etotr])
            V(lambda e: e.tensor_tensor(out=wst, in0=dtt, in1=dte, op=ALU.mult), [dttr, dter], [wstr])

        def ssd_state_update(X, c, store):
            if "s1" in self.debug or "s2" in self.debug or "s3" in self.debug:
                return
            xo = 0 if X == "f" else 8
            V(lambda e: e.tensor_tensor(out=xw.rearrange("p (h d) -> p h d", h=8), in0=xa[:, 0:512].rearrange("p (h d) -> p h d", h=8),
                                        in1=bc(wst[:, xo:xo + 8], 64), op=ALU.mult), [xar, wstr], [xwr])
            pt, pr = self.ps()
            PE(lambda e: e.matmul(pt[:], lhsT=xa[:, 512:640], rhs=xw, start=True, stop=True), [xar, xwr], [pr])
            if store:
                V(lambda e: e.tensor_copy(out=Sst_f[:, c, :], in_=Sssd), [Sssdr], [(Sst_fr, c)])
            V(lambda e: e.tensor_scalar(out=etsel, in0=etot[:, xo:xo + 4], scalar1=self.cst[:, 1856:1857], scalar2=None, op0=ALU.mult), [etotr, "cst"], [etselr])
            V(lambda e: e.scalar_tensor_tensor(out=etsel, in0=etot[:, xo + 4:xo + 8], scalar=self.cst[:, 1857:1858], in1=etsel, op0=ALU.mult, op1=ALU.add),
              [etotr, etselr, "cst"], [etselr])
            V(lambda e: e.tensor_tensor(out=Sssd.rearrange("p (h d) -> p h d", h=4), in0=Sssd.rearrange("p (h d) -> p h d", h=4), in1=bc(etsel, 64), op=ALU.mult),
              [Sssdr, etselr], [Sssdr])
            for g in range(2):
                V(lambda e, g=g, pt=pt: e.scalar_tensor_tensor(out=Sssd, in0=pt[:, g * 256:(g + 1) * 256], scalar=self.cst[:, 1856 + g:1857 + g], in1=Sssd,
                                                               op0=ALU.mult, op1=ALU.add), [Sssdr, pr, "cst"], [Sssdr])

        def gla_common(tk, dirs, i):
            g_, gr_ = glr[i]
            S.dma("sp", f"glr{i}", lambda e: e.dma_start(out=g_[0:32, :], in_=self.P["glr"][:, tk:tk + 128]), writes=[gr_])
            k_, kr_ = kT[i]
            S.dma("sp", f"kT{i}", lambda e: e.dma_start(out=k_, in_=self.P["k"][:, :, tk:tk + 128].rearrange("h p t -> p h t")), writes=[kr_])
            v_, vr_ = vtm[i]
            S.dma("sp", f"vtm{i}", lambda e: e.dma_start(out=v_, in_=self.P["gv"][tk:tk + 128, :]), writes=[vr_])
            pg, pgr = self.ps()
            PE(lambda e: e.matmul(pg[:], lhsT=g_[0:33, :], rhs=self.w2b[0:33, :], start=True, stop=True), [gr_, gr_ + "one", "w2b"], [pgr])
            A(lambda e: e.activation(out=e1, in_=pg[:], func=AF.Exp, scale=-1.0), [pgr], [e1r])
            A(lambda e: e.activation(out=sp_, in_=e1, func=AF.Ln, bias=1.0), [e1r], [spr])
            pb, pbr = self.ps()
            pb3 = pb[:].rearrange("p (a t) -> p a t", a=4)
            combos = [(X, hp) for X in dirs for hp in range(2)]
            for n_, (X, hp) in enumerate(combos):
                xi = 0 if X == "f" else 1
                PE(lambda e, X=X, hp=hp, xi=xi: e.matmul(pb3[:, xi * 2 + hp, :], lhsT=sp_[:, xi * 256 + hp * 128:xi * 256 + (hp + 1) * 128],
                                                         rhs=self.trineg[X], start=True, stop=True), [spr, "cst"], [pbr], inc=(n_ == len(combos) - 1))
            for X in dirs:
                xi = 0 if X == "f" else 1
                col = 127 if X == "f" else 0
                A(lambda e, xi=xi, col=col: e.activation(out=ebt[:, xi * 2:xi * 2 + 2], in_=pb3[:, xi * 2:xi * 2 + 2, col], func=AF.Exp), [pbr], [ebtr + X])
            lo, hi = (0, 2) if dirs == "f" else (0, 4)
            A(lambda e: e.activation(out=ek[:, lo:hi, :], in_=pb3[:, lo:hi, :], func=AF.Exp, scale=-1.0), [pbr], [ekr])
            if dirs == "fb":
                A(lambda e: e.activation(out=eq, in_=pb3, func=AF.Exp), [pbr], [eqr])
            nx = hi // 2
            V(lambda e: e.tensor_tensor(out=kinv[:, lo:hi, :].rearrange("p (x h) t -> p x h t", x=nx),
                                        in0=ek[:, lo:hi, :].rearrange("p (x h) t -> p x h t", x=nx),
                                        in1=k_.unsqueeze(1).to_broadcast([128, nx, 2, 128]), op=ALU.mult), [ekr, kr_], [kinvr])
            V(lambda e: e.tensor_tensor(out=kend[:, lo:hi, :], in0=kinv[:, lo:hi, :], in1=bc(ebt[:, lo:hi], 128), op=ALU.mult),
              [kinvr] + [ebtr + X for X in dirs], [kendr])
            ptb, ptr_ = self.ps()
            ptb3 = ptb[:].bitcast(BF16)[:, 0:512].rearrange("p (a t) -> p a t", a=4)
            for a_ in range(lo, hi):
                PE(lambda e, a_=a_: e.transpose(ptb3[:, a_, :], kend[:, a_, :], self.identb), [kendr, "cstb"], [ptr_], inc=(a_ == hi - 1))
            V(lambda e: e.tensor_copy(out=kendtm[:, lo:hi, :], in_=ptb3[:, lo:hi, :]), [ptr_], [kendtmr])
            return i

        def gla_state_update(X, c, i, store):
            xi = 0 if X == "f" else 1
            v_, vr_ = vtm[i]
            pt, pr = self.ps()
            for hp in range(2):
                PE(lambda e, hp=hp: e.matmul(pt[:, hp * 256:(hp + 1) * 256], lhsT=kendtm[:, xi * 2 + hp, :], rhs=v_[:, hp * 256:(hp + 1) * 256],
                                             start=True, stop=True), [kendtmr, vr_], [pr], inc=(hp == 1))
            if store:
                V(lambda e: e.tensor_copy(out=Sgl_f[:, c, :].rearrange("p (h v) -> p h v", h=2), in_=Sgla), [Sglar], [(Sgl_fr, c)])
            V(lambda e: e.tensor_tensor(out=Sgla, in0=Sgla, in1=bc(ebt[:, xi * 2:xi * 2 + 2], 128), op=ALU.mult), [Sglar, ebtr + X], [Sglar])
            p4 = pt[:].rearrange("p (a b v) -> p a b v", a=2, b=2)
            for hh in range(2):
                V(lambda e, hh=hh: e.scalar_tensor_tensor(out=Sgla, in0=p4[:, :, hh, :], scalar=self.cst[:, 1856 + hh:1857 + hh], in1=Sgla,
                                                          op0=ALU.mult, op1=ALU.add), [Sglar, pr, "cst"], [Sglar])

        for s, Ls in enumerate(self.seqs):
            nch = Ls // 128
            V(lambda e: e.memset(Sssd, 0.0), w=[Sssdr])
            V(lambda e: e.memset(Sgla, 0.0), w=[Sglar])
            for c in range(nch):
                tp = self.poff[s] + c * 128
                tk = self.soff[s] + c * 128
                i = c % 2

                def fA_ssd():
                    ssd_common(tp, tk, i)
                    ssd_state_update("f", c, True)

                def fA_gla():
                    gla_common(tk, "f", i)
                    gla_state_update("f", c, i, True)
                interleave(S, [fA_ssd, fA_gla])
            V(lambda e: e.memset(Sssd, 0.0), w=[Sssdr])
            V(lambda e: e.memset(Sgla, 0.0), w=[Sglar])
            for c in range(nch - 1, -1, -1):
                tp = self.poff[s] + c * 128
                tk = self.soff[s] + c * 128
                i = c % 2

                def fB_ssd():
                    ssd_common(tp, tk, i)
                    li = i
                    z_, zr_ = zt[li]
                    S.dma("sp", f"zt{li}", lambda e, z_=z_, tk=tk: e.dma_start(out=z_, in_=self.P["z"][tk:tk + 128, :]), writes=[zr_])
                    A(lambda e: e.activation(out=eacs, in_=sc[:, 0:16], func=AF.Exp), [scr], [eacsr])
                    V(lambda e: e.tensor_scalar(out=ndta, in0=dta, scalar1=-1.0, scalar2=None, op0=ALU.mult), [dtar], [ndtar])
                    for X in "fb":
                        xo = 0 if X == "f" else 8
                        xd, xdr = xdt[X]
                        V(lambda e, xd=xd, xo=xo: e.tensor_tensor(out=xd.rearrange("p (h d) -> p h d", h=8), in0=xa[:, 0:512].rearrange("p (h d) -> p h d", h=8),
                                                                  in1=bc(dtt[:, xo:xo + 8], 64), op=ALU.mult), [xar, dttr], [xdr])
                        du, dur = dtaU[X]
                        V(lambda e, du=du, xo=xo, X=X: e.tensor_tensor(out=du, in0=self.tri[X].unsqueeze(1).to_broadcast([128, 8, 128]),
                                                                       in1=bc(dta[:, xo:xo + 8], 128), op=ALU.mult), [dtar, "cst"], [dur])
                    ptb, ptr_ = self.ps()
                    ptb3 = ptb[:].bitcast(BF16)[:, 0:256].rearrange("p (a t) -> p a t", a=2)
                    PE(lambda e: e.transpose(ptb3[:, 0, :], xa[:, 512:640], self.identb), [xar, "cstb"], [ptr_], inc=False)
                    PE(lambda e: e.transpose(ptb3[:, 1, :], xa[:, 640:768], self.identb), [xar, "cstb"], [ptr_])
                    V(lambda e: e.tensor_copy(out=BCT, in_=ptb3), [ptr_], [BCTr])
                    pcb, pcbr = self.ps()
                    for g in range(2):
                        V(lambda e, g=g: e.tensor_scalar(out=Cblk[:, g, :], in0=BCT[:, 1, :], scalar1=self.cst[:, 1856 + g:1857 + g], scalar2=None, op0=ALU.mult),
                          [BCTr, "cst"], [Cblkr])
                    PE(lambda e: e.matmul(pcb[:, 0:256], lhsT=BCT[:, 0, :], rhs=Cblk.rearrange("p a t -> p (a t)"), start=True, stop=True), [BCTr, Cblkr], [pcbr])
                    V(lambda e: e.tensor_copy(out=CBT, in_=pcb[:, 0:256].rearrange("p (a t) -> p a t", a=2)), [pcbr], [CBTr])
                    ei = 0
                    for X in "fb":
                        xo = 0 if X == "f" else 8
                        du, dur = dtaU[X]
                        for g in range(2):
                            pseg, psegr = self.ps()
                            pseg3 = pseg[:].rearrange("p (h t) -> p h t", h=4)
                            PE(lambda e, g=g, du=du, pseg3=pseg3: e.matmul(pseg3, lhsT=self.onesf, rhs=du[:, g * 4:(g + 1) * 4, :], start=True, stop=False),
                               [dur, "cst"], [psegr], inc=False)
                            PE(lambda e, g=g, X=X, xo=xo, pseg3=pseg3: e.matmul(pseg3, lhsT=self.tri[X], rhs=bc(ndta[:, xo + g * 4:xo + g * 4 + 4], 128),
                                                                               start=False, stop=False), [ndtar, "cst"], [psegr], inc=False)
                            PE(lambda e, X=X, pseg=pseg: e.matmul(pseg[:], lhsT=self.identb, rhs=self.negm[X], start=False, stop=True), ["cstb"], [psegr])
                            E_, Er_ = Et[ei % 2]
                            ei += 1
                            A(lambda e, E_=E_, pseg3=pseg3: e.activation(out=E_, in_=pseg3, func=AF.Exp), [psegr], [Er_])
                            M_, Mr_ = Mt[(X, g)]
                            V(lambda e, M_=M_, E_=E_, g=g: e.tensor_tensor(out=M_, in0=E_, in1=CBT[:, g, :].unsqueeze(1).to_broadcast([128, 4, 128]), op=ALU.mult),
                              [Er_, CBTr], [Mr_])
                    pyd, pydr = self.ps()
                    for h in range(8):
                        g, h4 = h // 4, h % 4
                        for X in "fb":
                            PE(lambda e, h=h, g=g, h4=h4, X=X: e.matmul(pyd[:, h * 64:(h + 1) * 64], lhsT=Mt[(X, g)][0][:, h4, :], rhs=xdt[X][0][:, h * 64:(h + 1) * 64],
                                                                        start=(X == "f"), stop=(X == "b")), [Mt[(X, g)][1], xdt[X][1]], [pydr], inc=(h == 7 and X == "b"))
                    for g in range(2):
                        V(lambda e, g=g: e.tensor_scalar(out=Sblk["f"][0][:, g, :], in0=Sst_f[:, c, :], scalar1=self.cst[:, 1856 + g:1857 + g], scalar2=None, op0=ALU.mult),
                          [(Sst_fr, c), "cst"], [Sblk["f"][1]])
                        V(lambda e, g=g: e.tensor_scalar(out=Sblk["b"][0][:, g, :], in0=Sssd, scalar1=self.cst[:, 1856 + g:1857 + g], scalar2=None, op0=ALU.mult),
                          [Sssdr, "cst"], [Sblk["b"][1]])
                    pyo = {}
                    for X in "fb":
                        p_, pr_ = self.ps()
                        pyo[X] = (p_, pr_)
                        PE(lambda e, p_=p_, X=X: e.matmul(p_[:], lhsT=BCT[:, 1, :], rhs=Sblk[X][0].rearrange("p a t -> p (a t)"), start=True, stop=True),
                           [BCTr, Sblk[X][1]], [pr_])
                    h8 = lambda ap: ap.rearrange("p (h d) -> p h d", h=8)
                    V(lambda e: e.tensor_tensor(out=h8(y1), in0=h8(pyo["f"][0][:]), in1=bc(eacs[:, 0:8], 64), op=ALU.mult), [pyo["f"][1], eacsr], [y1r])
                    V(lambda e: e.tensor_tensor(out=h8(y2), in0=h8(pyo["b"][0][:]), in1=bc(eacs[:, 8:16], 64), op=ALU.mult), [pyo["b"][1], eacsr], [y2r])
                    V(lambda e: e.tensor_tensor(out=y1, in0=y1, in1=y2, op=ALU.add), [y1r, y2r], [y1r])
                    V(lambda e: e.tensor_tensor(out=h8(y2), in0=h8(xa[:, 0:512]), in1=bc(rb[:, RB_D:RB_D + 8], 64), op=ALU.mult), [xar, "rb"], [y2r])
                    V(lambda e: e.tensor_tensor(out=y1, in0=y1, in1=y2, op=ALU.add), [y1r, y2r], [y1r])
                    V(lambda e: e.tensor_tensor(out=y1, in0=y1, in1=pyd[:], op=ALU.add), [y1r, pydr], [y1r])
                    V(lambda e: e.tensor_tensor(out=y1, in0=y1, in1=z_, op=ALU.mult), [y1r, zr_], [y1r])
                    for g in range(2):
                        A(lambda e, g=g: e.activation(out=junk[:, g * 256:(g + 1) * 256], in_=y1[:, g * 256:(g + 1) * 256], func=AF.Square,
                                                      accum_out=ss2[:, g:g + 1]), [y1r], [junkr, ss2r])
                    V(lambda e: e.tensor_scalar(out=ss2, in0=ss2, scalar1=1.0 / 256, scalar2=EPS, op0=ALU.mult, op1=ALU.add), [ss2r], [ss2r])
                    A(lambda e: e.activation(out=ss2, in_=ss2, func=AF.Sqrt), [ss2r], [ss2r])
                    V(lambda e: e.reciprocal(out=ss2, in_=ss2), [ss2r], [ss2r])
                    for g in range(2):
                        V(lambda e, g=g: e.scalar_tensor_tensor(out=ya[:, g * 256:(g + 1) * 256], in0=y1[:, g * 256:(g + 1) * 256], scalar=ss2[:, g:g + 1],
                                                                in1=rb[:, RB_SSDN + g * 256:RB_SSDN + (g + 1) * 256], op0=ALU.mult, op1=ALU.mult),
                          [y1r, ss2r, "rb"], [yar])
                    ptb, ptr_ = self.ps()
                    ptb3 = ptb[:].bitcast(BF16)[:, 0:512].rearrange("p (a t) -> p a t", a=4)
                    for a_ in range(4):
                        PE(lambda e, a_=a_, ptb3=ptb3: e.transpose(ptb3[:, a_, :], ya[:, a_ * 128:(a_ + 1) * 128], self.identb), [yar, "cstb"], [ptr_], inc=(a_ == 3))
                    b_, br_ = next_brs()
                    V(lambda e, b_=b_, ptb3=ptb3: e.tensor_copy(out=b_, in_=ptb3), [ptr_], [br_])
                    store_branch(0, b_, br_, tk)
                    ssd_state_update("b", c, False)

                def fB_gla():
                    gla_common(tk, "fb", i)
                    q_, qr_ = qT[i]
                    S.dma("sp", f"qT{i}", lambda e, q_=q_, tk=tk: e.dma_start(out=q_, in_=self.P["q"][:, :, tk:tk + 128].rearrange("h p t -> p h t")), writes=[qr_])
                    r_, rr_ = rT[i]
                    S.dma("sp", f"rT{i}", lambda e, r_=r_, tk=tk: e.dma_start(out=r_, in_=self.P["r"][:, :, tk:tk + 128].rearrange("h p t -> p h t")), writes=[rr_])
                    v_, vr_ = vtm[i]
                    V(lambda e, q_=q_: e.tensor_tensor(out=qdec.rearrange("p (x h) t -> p x h t", x=2), in0=eq.rearrange("p (x h) t -> p x h t", x=2),
                                                       in1=q_.unsqueeze(1).to_broadcast([128, 2, 2, 128]), op=ALU.mult), [eqr, qr_], [qdecr])
                    for hh in range(2):
                        V(lambda e, hh=hh: e.tensor_scalar(out=Qblk[:, :, hh, :], in0=qdec, scalar1=self.cst[:, 1856 + hh:1857 + hh], scalar2=None, op0=ALU.mult),
                          [qdecr, "cst"], [Qblkr])
                    for X in "fb":
                        xi = 0 if X == "f" else 1
                        pa, par = self.ps()
                        pa3 = pa[:].rearrange("p (h t) -> p h t", h=4)
                        for hp in range(2):
                            PE(lambda e, hp=hp, xi=xi, pa3=pa3: e.matmul(pa3[:, hp * 2:(hp + 1) * 2, :], lhsT=kinv[:, xi * 2 + hp, :], rhs=Qblk[:, xi * 2 + hp, :, :],
                                                                         start=True, stop=True), [kinvr, Qblkr], [par], inc=(hp == 1))
                        am, amr = attm[X]
                        V(lambda e, am=am, pa3=pa3, X=X: e.tensor_tensor(out=am, in0=pa3, in1=self.m01[X].unsqueeze(1).to_broadcast([128, 4, 128]), op=ALU.mult),
                          [par, "cstb"], [amr])
                    V(lambda e: e.tensor_copy(out=Gin_b, in_=Sgla), [Sglar], [Gin_br])
                    po, por = self.ps()
                    po3 = po[:].rearrange("p (h t) -> p h t", h=4)
                    for h in range(4):
                        hp, hh = h // 2, h % 2
                        rs = slice(hh * 64, (hh + 1) * 64)
                        PE(lambda e, h=h: e.matmul(po3[:, h, :], lhsT=v_[:, h * 128:(h + 1) * 128], rhs=attm["f"][0][:, h, :], start=True, stop=False),
                           [vr_, attm["f"][1]], [por], inc=False)
                        PE(lambda e, h=h: e.matmul(po3[:, h, :], lhsT=v_[:, h * 128:(h + 1) * 128], rhs=attm["b"][0][:, h, :], start=False, stop=False),
                           [vr_, attm["b"][1]], [por], inc=False)
                        PE(lambda e, h=h, hp=hp, hh=hh: e.matmul(po3[:, h, :], lhsT=Sgl_f[:, c, hp * 128:(hp + 1) * 128], rhs=Qblk[:, hp, hh, :], start=False, stop=False),
                           [(Sgl_fr, c), Qblkr], [por], inc=False)
                        PE(lambda e, h=h, hp=hp, hh=hh: e.matmul(po3[:, h, :], lhsT=Gin_b[:, hp, :], rhs=Qblk[:, 2 + hp, hh, :], start=False, stop=True),
                           [Gin_br, Qblkr], [por], inc=(h == 3))
                    A(lambda e: e.activation(out=osq, in_=po[:], func=AF.Square), [por], [osqr])
                    pss, pssr = self.ps()
                    PE(lambda e: e.matmul(pss[:], lhsT=self.onesb, rhs=osq, start=True, stop=True), [osqr, "cstb"], [pssr])
                    V(lambda e: e.tensor_scalar(out=orstd, in0=pss[:], scalar1=1.0 / 128, scalar2=EPS, op0=ALU.mult, op1=ALU.add), [pssr], [orstdr])
                    A(lambda e: e.activation(out=orstd, in_=orstd, func=AF.Sqrt), [orstdr], [orstdr])
                    V(lambda e: e.reciprocal(out=orstd, in_=orstd), [orstdr], [orstdr])
                    V(lambda e: e.tensor_tensor(out=o1, in0=po[:], in1=orstd, op=ALU.mult), [por, orstdr], [o1r])
                    V(lambda e, r_=r_: e.tensor_tensor(out=rg, in0=r_, in1=bc(pp[:, PP_GLAN:PP_GLAN + 4], 128), op=ALU.mult), [rr_, "pp"], [rgr])
                    b_, br_ = next_brs()
                    V(lambda e, b_=b_: e.tensor_tensor(out=b_, in0=o1.rearrange("p (h t) -> p h t", h=4), in1=rg, op=ALU.mult), [o1r, rgr], [br_])
                    store_branch(3, b_, br_, tk)
                    gla_state_update("b", c, i, False)

                def fB_sgu():
                    vv, vvr = vt[i]
                    S.dma("sp", f"vt{i}", lambda e, vv=vv, tk=tk: e.dma_start(out=vv, in_=self.P["v"][tk:tk + 128, :]), writes=[vvr])
                    uu, uur = uT[i]
                    S.dma("sp", f"uT{i}", lambda e, uu=uu, tk=tk: e.dma_start(out=uu, in_=self.P["u"][:, :, tk:tk + 128].rearrange("h p t -> p h t")), writes=[uur])
                    A(lambda e, vv=vv: e.activation(out=junk, in_=vv, func=AF.Square, accum_out=ss1[:, 0:1]), [vvr], [junkr, ss1r])
                    V(lambda e: e.tensor_scalar(out=ss1[:, 0:1], in0=ss1[:, 0:1], scalar1=1.0 / 512, scalar2=EPS, op0=ALU.mult, op1=ALU.add), [ss1r], [ss1r])
                    A(lambda e: e.activation(out=ss1[:, 0:1], in_=ss1[:, 0:1], func=AF.Sqrt), [ss1r], [ss1r])
                    V(lambda e: e.reciprocal(out=ss1[:, 0:1], in_=ss1[:, 0:1]), [ss1r], [ss1r])
                    V(lambda e, vv=vv: e.scalar_tensor_tensor(out=vf, in0=vv, scalar=ss1[:, 0:1], in1=rb[:, RB_SGUN:RB_SGUN + 512], op0=ALU.mult, op1=ALU.mult),
                      [vvr, ss1r, "rb"], [vfr])
                    psg, psgr = self.ps()
                    for g in range(4):
                        PE(lambda e, g=g, psg=psg: e.matmul(psg[:, g * 128:(g + 1) * 128], lhsT=vf[:, g * 128:(g + 1) * 128], rhs=self.sguw[:, g * 128:(g + 1) * 128],
                                                            start=True, stop=True), [vfr, "sguw"], [psgr], inc=(g == 3))
                    V(lambda e, psg=psg: e.tensor_tensor(out=sg1.rearrange("p h t -> p (h t)"), in0=psg[:], in1=rb[:, RB_SGUB:RB_SGUB + 512], op=ALU.add), [psgr, "rb"], [sg1r])
                    b_, br_ = next_brs()
                    V(lambda e, b_=b_, uu=uu: e.tensor_tensor(out=b_, in0=sg1, in1=uu, op=ALU.mult), [sg1r, uur], [br_])
                    store_branch(2, b_, br_, tk)

                def fB_pool():
                    pl, plr = ptl[i]
                    S.dma("sp", f"ptl{i}", lambda e, pl=pl, tp=tp: e.dma_start(out=pl, in_=self.P["p"][:, :, tp - 8:tp + 136].rearrange("g p t -> p g t")), writes=[plr])
                    V(lambda e, pl=pl: e.tensor_tensor(out=wsum[:, 0, :], in0=pl[:, 0, 7:135], in1=pl[:, 0, 8:136], op=ALU.add), [plr], [wsumr + "0"])
                    V(lambda e, pl=pl: e.tensor_tensor(out=a1[:, :, 1:144], in0=pl[:, 1:4, 0:143], in1=pl[:, 1:4, 1:144], op=ALU.add), [plr], [a1r])
                    V(lambda e: e.tensor_tensor(out=wsum[:, 1, :], in0=a1[:, 0, 7:135], in1=a1[:, 0, 9:137], op=ALU.add), [a1r], [wsumr + "1"])
                    V(lambda e: e.tensor_tensor(out=a2[:, :, 2:143], in0=a1[:, 1:3, 1:142], in1=a1[:, 1:3, 3:144], op=ALU.add), [a1r], [a2r])
                    V(lambda e: e.tensor_tensor(out=wsum[:, 2, :], in0=a2[:, 0, 6:134], in1=a2[:, 0, 10:138], op=ALU.add), [a2r], [wsumr + "2"])
                    V(lambda e: e.tensor_tensor(out=a3[:, 4:141], in0=a2[:, 1, 2:139], in1=a2[:, 1, 6:143], op=ALU.add), [a2r], [a3r])
                    V(lambda e: e.tensor_tensor(out=wsum[:, 3, :], in0=a3[:, 4:132], in1=a3[:, 12:140], op=ALU.add), [a3r], [wsumr + "3"])
                    wregs = [wsumr + str(g) for g in range(4)]
                    corr = self.cst[:, 1792:1856].rearrange("p (e g t) -> p e g t", e=2, g=4)
                    if c == 0:
                        V(lambda e: e.tensor_tensor(out=wsum[:, :, 0:8], in0=wsum[:, :, 0:8], in1=corr[:, 0], op=ALU.mult), wregs + ["cst"], wregs)
                    if c == nch - 1:
                        V(lambda e: e.tensor_tensor(out=wsum[:, :, 120:128], in0=wsum[:, :, 120:128], in1=corr[:, 1], op=ALU.mult), wregs + ["cst"], wregs)
                    for g, w_ in enumerate((2, 4, 8, 16)):
                        V(lambda e, g=g, w_=w_, pl=pl: e.scalar_tensor_tensor(out=pooled[:, g, :], in0=wsum[:, g, :], scalar=1.0 / w_, in1=pl[:, g, 8:136],
                                                                             op0=ALU.mult, op1=ALU.subtract), [wsumr + str(g), plr], [pooledr + str(g)])
                    ppl, pplr = self.ps()
                    for g in range(4):
                        PE(lambda e, g=g, ppl=ppl: e.matmul(ppl[:, g * 128:(g + 1) * 128], lhsT=self.poolw[:, g * 128:(g + 1) * 128], rhs=pooled[:, g, :],
                                                            start=True, stop=True), [pooledr + str(g), "poolw"], [pplr], inc=(g == 3))
                    b_, br_ = next_brs()
                    for g in range(4):
                        V(lambda e, g=g, b_=b_, ppl=ppl: e.tensor_scalar(out=b_[:, g, :], in0=ppl[:, g * 128:(g + 1) * 128], scalar1=pp[:, PP_PSC + g:PP_PSC + g + 1],
                                                                         scalar2=None, op0=ALU.mult), [pplr, "pp"], [br_])
                    S.dma("pool", "brs" + br_, lambda e, b_=b_, tk=tk: e.dma_start(
                        out=self.BR[1, :, :, tk:tk + 128].rearrange("w p t -> p w t"), in_=b_), reads=[br_], writes=[("BR", 1, tk)])
                interleave(S, [fB_ssd, fB_gla, fB_sgu, fB_pool])

    def phase3(self, l):
        S = self.S
        self.arena_reset()
        TT = 512
        pp = self.pp
        xT = [self.arena(f"xT{i}", [8, TT]) for i in range(2)]
        a16, a16r = self.arena("a16", [8, TT], BF16)
        mo, mor = self.arena("mo", [8, TT])
        sq, sqr = self.arena("sq", [8, TT], BF16)
        rstd, rstdr = self.arena("rstd", [TT])
        hid, hidr = self.arena("hid", [32, TT], BF16)
        wb = [self.arena(f"wb{i}", [8192], BF16) for i in range(3)]
        brt = [self.arena(f"brt{i}", [4, TT], BF16) for i in range(2)]
        gt = [self.arena(f"gt{i}", [TT], BF16) for i in range(4)]
        tmp = [self.arena(f"tmp{i}", [TT]) for i in range(2)]
        rl = [self.arena(f"rl{i}", [TT], BF16) for i in range(2)]
        wc = [0]
        gc = [0]
        tc = [0]

        def wload(key, j, n):
            w, wr = wb[wc[0] % 3]
            slot = f"wb{wc[0] % 3}"
            wc[0] += 1
            S.dma("sp", slot, lambda e: e.dma_start(out=w[:, 0:n], in_=self.wsc[(l, key)][j]), writes=[wr])
            return w, wr

        def post_norm_residual(xt, xr, gcol):
            self.rstd_from_sq(sq, sqr, 8, rstd, rstdr, D)
            for ec in range(8):
                t_, tr_ = tmp[tc[0] % 2]
                tc[0] += 1
                S.op("dve", lambda e, ec=ec, t_=t_: e.scalar_tensor_tensor(
                    out=t_, in0=mo[:, ec, :], scalar=pp[:, gcol + ec:gcol + ec + 1], in1=rstd, op0=ALU.mult, op1=ALU.mult),
                    reads=[(mor, ec), rstdr, "pp"], writes=[tr_])
                S.op("dve", lambda e, ec=ec, t_=t_: e.tensor_tensor(out=xt[:, ec, :], in0=xt[:, ec, :], in1=t_, op=ALU.add),
                     reads=[tr_, (xr, ec)], writes=[(xr, ec)])

        for ti in range(self.T // TT):
            t0 = ti * TT
            xt, xr = xT[ti % 2]
            S.dma("sp", f"xT{ti % 2}", lambda e, xt=xt, t0=t0: e.dma_start(
                out=xt, in_=self.XT[:, :, t0:t0 + TT].rearrange("k p t -> p k t")),
                writes=[(xr, k) for k in range(8)])
            for n in range(4):
                w, wr = wload("br", n, 4096)
                wv = w[:, 0:4096].rearrange("p (k n) -> p k n", k=4)
                bt, btr = brt[n % 2]
                S.dma("sp", f"brt{n % 2}", lambda e, bt=bt, n=n, t0=t0: e.dma_start(
                    out=bt, in_=self.BR[n, :, :, t0:t0 + TT].rearrange("w p t -> p w t")), writes=[btr])
                for dc in range(8):
                    g_, gr_ = gt[gc[0] % 4]
                    gslot = f"gt{gc[0] % 4}"
                    gc[0] += 1
                    S.dma("sp", gslot, lambda e, g_=g_, n=n, dc=dc, t0=t0: e.dma_start(
                        out=g_, in_=self.P["gate"][n * 8 + dc, :, t0:t0 + TT]), writes=[gr_])
                    pt, pr = self.ps()
                    for k in range(4):
                        S.op("pe", lambda e, pt=pt, wv=wv, k=k, dc=dc, bt=bt: e.matmul(
                            pt[:], lhsT=wv[:, k, dc * 128:(dc + 1) * 128], rhs=bt[:, k, :], start=(k == 0), stop=(k == 3)),
                            reads=[wr, btr], writes=[pr], inc=(k == 3))
                    if n == 0:
                        S.op("dve", lambda e, pt=pt, g_=g_, dc=dc: e.tensor_tensor(out=mo[:, dc, :], in0=pt[:], in1=g_, op=ALU.mult),
                             reads=[pr, gr_], writes=[(mor, dc)])
                    else:
                        t_, tr_ = tmp[tc[0] % 2]
                        tc[0] += 1
                        S.op("dve", lambda e, pt=pt, g_=g_, t_=t_: e.tensor_tensor(out=t_, in0=pt[:], in1=g_, op=ALU.mult),
                             reads=[pr, gr_], writes=[tr_])
                        if n < 3:
                            S.op("dve", lambda e, t_=t_, dc=dc: e.tensor_tensor(out=mo[:, dc, :], in0=mo[:, dc, :], in1=t_, op=ALU.add),
                                 reads=[tr_, (mor, dc)], writes=[(mor, dc)])
                        else:
                            S.op("dve", lambda e, t_=t_, dc=dc: e.tensor_tensor(out=a16[:, dc, :], in0=mo[:, dc, :], in1=t_, op=ALU.add),
                                 reads=[tr_, (mor, dc)], writes=[(a16r, dc)])
            areads = [(a16r, k) for k in range(8)]
            for j in range(2):
                w, wr = wload("out", j, 4096)
                wv = w[:, 0:4096].rearrange("p (k n) -> p k n", k=8)
                for cc in range(4):
                    ec = j * 4 + cc
                    pt, pr = self.ps()
                    for k in range(8):
                        S.op("pe", lambda e, pt=pt, wv=wv, k=k, cc=cc: e.matmul(
                            pt[:], lhsT=wv[:, k, cc * 128:(cc + 1) * 128], rhs=a16[:, k, :], start=(k == 0), stop=(k == 7)),
                            reads=[wr] + areads, writes=[pr], inc=(k == 7))
                    S.op("dve", lambda e, pt=pt, ec=ec: e.tensor_copy(out=mo[:, ec, :], in_=pt[:]), reads=[pr], writes=[(mor, ec)])
                    S.op("act", lambda e, ec=ec: e.activation(out=sq[:, ec, :], in_=mo[:, ec, :], func=AF.Square),
                         reads=[(mor, ec)], writes=[(sqr, ec)])
            post_norm_residual(xt, xr, PP_GPOST)
            for kc in range(8):
                S.op("act", lambda e, kc=kc, xt=xt: e.activation(out=sq[:, kc, :], in_=xt[:, kc, :], func=AF.Square),
                     reads=[(xr, kc)], writes=[(sqr, kc)])
            self.rstd_from_sq(sq, sqr, 8, rstd, rstdr, D)
            for kc in range(8):
                S.op("dve", lambda e, kc=kc, xt=xt: e.scalar_tensor_tensor(
                    out=a16[:, kc, :], in0=xt[:, kc, :], scalar=pp[:, PP_GF1 + kc:PP_GF1 + kc + 1], in1=rstd,
                    op0=ALU.mult, op1=ALU.mult), reads=[(xr, kc), rstdr, "pp"], writes=[(a16r, kc)])
            for j in range(8):
                w, wr = wload("ff1", j, 4096)
                wv = w[:, 0:4096].rearrange("p (k n) -> p k n", k=8)
                for cc in range(4):
                    fc = j * 4 + cc
                    pt, pr = self.ps()
                    for k in range(8):
                        S.op("pe", lambda e, pt=pt, wv=wv, k=k, cc=cc: e.matmul(
                            pt[:], lhsT=wv[:, k, cc * 128:(cc + 1) * 128], rhs=a16[:, k, :], start=(k == 0), stop=(k == 7)),
                            reads=[wr] + areads, writes=[pr], inc=(k == 7))
                    r_, rr_ = rl[fc % 2]
                    S.op("act", lambda e, pt=pt, r_=r_: e.activation(out=r_, in_=pt[:], func=AF.Relu), reads=[pr], writes=[rr_])
                    S.op("dve", lambda e, r_=r_, fc=fc: e.tensor_tensor(out=hid[:, fc, :], in0=r_, in1=r_, op=ALU.mult),
                         reads=[rr_], writes=[(hidr, fc)])
            hreads = [(hidr, f) for f in range(32)]
            for j in range(4):
                w, wr = wload("ff2", j, 8192)
                wv = w[:, 0:8192].rearrange("p (k n) -> p k n", k=32)
                for cc in range(2):
                    ec = j * 2 + cc
                    pt, pr = self.ps()
                    for k in range(32):
                        S.op("pe", lambda e, pt=pt, wv=wv, k=k, cc=cc: e.matmul(
                            pt[:], lhsT=wv[:, k, cc * 128:(cc + 1) * 128], rhs=hid[:, k, :], start=(k == 0), stop=(k == 31)),
                            reads=[wr] + hreads, writes=[pr], inc=(k == 31))
                    S.op("dve", lambda e, pt=pt, ec=ec: e.tensor_copy(out=mo[:, ec, :], in_=pt[:]), reads=[pr], writes=[(mor, ec)])
                    S.op("act", lambda e, ec=ec: e.activation(out=sq[:, ec, :], in_=mo[:, ec, :], func=AF.Square),
                         reads=[(mor, ec)], writes=[(sqr, ec)])
            post_norm_residual(xt, xr, PP_GF2)
            S.dma("pool", f"xTs{ti % 2}", lambda e, xt=xt, t0=t0: e.dma_start(
                out=self.XT[:, :, t0:t0 + TT].rearrange("k p t -> p k t"), in_=xt),
                reads=[(xr, k) for k in range(8)], writes=[("XT", ti)])

    def phase_final(self):
        S = self.S
        self.arena_reset()
        xin = [self.arena(f"fx{i}", [8, 512]) for i in range(2)]
        xo = [self.arena(f"fo{i}", [4, D]) for i in range(2)]
        for ti in range(self.T // 512):
            xi, xir = xin[ti % 2]
            xot, xor_ = xo[ti % 2]
            t0 = ti * 512
            S.dma("sp", f"fx{ti % 2}", lambda e, xi=xi, t0=t0: e.dma_start(
                out=xi, in_=self.XT[:, :, t0:t0 + 512].rearrange("k p t -> p k t")), reads=[("XT", ti)], writes=[xir])
            for j in range(4):
                for half in range(2):
                    pt, pr = self.ps()
                    for q in range(4):
                        kc = half * 4 + q
                        S.op("pe", lambda e, pt=pt, xi=xi, j=j, kc=kc, q=q: e.transpose(
                            pt[:, q * 128:(q + 1) * 128], xi[:, kc, j * 128:(j + 1) * 128], self.identf),
                            reads=[xir, "cst"], writes=[pr], inc=(q == 3))
                    if half == 0:
                        S.op("dve", lambda e, pt=pt, xot=xot, j=j: e.tensor_copy(out=xot[:, j, 0:512], in_=pt[:]),
                             reads=[pr], writes=[(xor_, j, 0)])
                    else:
                        S.op("act", lambda e, pt=pt, xot=xot, j=j: e.copy(out=xot[:, j, 512:1024], in_=pt[:]),
                             reads=[pr], writes=[(xor_, j, 1)])
            S.dma("pool", f"fo{ti % 2}", lambda e, xot=xot, t0=t0: e.dma_start(
                out=self.y_out[t0:t0 + 512, :].rearrange("(j p) d -> p j d", p=128), in_=xot),
                reads=[(xor_, j, h) for j in range(4) for h in range(2)], writes=[("Y", ti)])


def make_consts():
    r = np.arange(128)
    ident = np.eye(128, dtype=np.float32)
    ones = np.ones((128, 128), np.float32)
    U = (r[:, None] <= r[None, :]).astype(np.float32)
    Lm = (r[:, None] >= r[None, :]).astype(np.float32)
    Uneg = U * (-1.0 / 16.0)
    Lneg = Lm * (-1.0 / 16.0)
    m01f, m01b = U.copy(), Lm.copy()
    negf = np.tile((1.0 - U) * -30000.0, (1, 4)).astype(np.float32)
    negb = np.tile((1.0 - Lm) * -30000.0, (1, 4)).astype(np.float32)
    corr = np.ones((2, 4, 8), np.float32)
    for g, w in enumerate((2, 4, 8, 16)):
        for t in range(8):
            if t < w // 2:
                corr[0, g, t] = w / (t + w // 2)
            tq = 7 - t
            if tq < w // 2 - 1:
                corr[1, g, t] = w / (tq + 1 + w // 2)
    corr = np.broadcast_to(corr.reshape(1, 64), (128, 64))
    half = np.stack([(r < 64), (r >= 64)], axis=1).astype(np.float32)
    cst = np.concatenate([ident, ones, U, Lm, Uneg, Lneg, negf, negb, corr, half], axis=1)
    pad = np.zeros((128, 1860 - cst.shape[1]), np.float32)
    return np.ascontiguousarray(np.concatenate([cst, pad], axis=1))


def pack_params(inputs, depth):
    L = depth
    f = lambda k: np.asarray(inputs[k], np.float32)
    pp = np.zeros((L, 128, PP_LEN), np.float32)
    for l in range(L):
        pp[l, :, PP_G1:PP_G1 + 8] = f("norm_mix_pre")[l].reshape(8, 128).T
        pp[l, :, PP_GPOST:PP_GPOST + 8] = f("norm_mix_post")[l].reshape(8, 128).T
        pp[l, :, PP_GF1:PP_GF1 + 8] = f("norm_ffn_pre")[l].reshape(8, 128).T
        pp[l, :, PP_GF2:PP_GF2 + 8] = f("norm_ffn_post")[l].reshape(8, 128).T
        pp[l, :, PP_PSC:PP_PSC + 4] = f("pool_scale")[l].reshape(4, 128).T
        pp[l, :, PP_GLAN:PP_GLAN + 4] = f("gla_norm")[l].reshape(4, 128).T
    rb = np.zeros((L, RB_LEN), np.float32)
    for l in range(L):
        rb[l, RB_CONVW:RB_CONVW + 3840] = f("ssd_conv_w")[l].reshape(-1)
        rb[l, RB_CONVB:RB_CONVB + 768] = f("ssd_conv_b")[l]
        rb[l, RB_DTB:RB_DTB + 16] = f("ssd_dt_bias")[l].reshape(-1)
        rb[l, RB_ALOG:RB_ALOG + 16] = f("ssd_a_log")[l].reshape(-1)
        rb[l, RB_D:RB_D + 8] = f("ssd_d")[l]
        rb[l, RB_SSDN:RB_SSDN + 512] = f("ssd_norm")[l]
        rb[l, RB_SGUN:RB_SGUN + 512] = f("sgu_norm")[l]
        rb[l, RB_SGUB:RB_SGUB + 512] = f("sgu_b")[l].reshape(-1)
    sguwT = np.ascontiguousarray(f("sgu_w")[:L].transpose(0, 3, 1, 2).reshape(L, 128, 512))
    poolw = np.ascontiguousarray(f("pool_w")[:L].transpose(0, 2, 1, 3).reshape(L, 128, 512))
    w2 = np.zeros((L, 33, 512), np.float32)
    w2[:, 0:16, 0:256] = f("gla_gate_w2")[:L, 0]
    w2[:, 16:32, 256:512] = f("gla_gate_w2")[:L, 1]
    w2[:, 32, :] = f("gla_gate_b")[:L].reshape(L, 512)
    return pp, rb, sguwT, poolw, w2


_CACHE = {}


def run(inputs, seqs_per_core, x_cores, depth, debug=()):
    key = (tuple(seqs_per_core), depth, tuple(debug))
    if key not in _CACHE:
        _CACHE[key] = Builder(seqs_per_core, depth, debug).build()
    nc = _CACHE[key]
    pp, rb, sguwT, poolw, w2 = pack_params(inputs, depth)
    f = lambda k: np.ascontiguousarray(np.asarray(inputs[k], np.float32)[:depth])
    shared = {"w_in": f("w_in"), "w_branch": f("w_branch"), "w_out": f("w_out"), "w_ff1": f("w_ff1"), "w_ff2": f("w_ff2"),
              "pp": pp, "rb": rb, "sguwT": sguwT, "poolw": poolw, "w2blk": w2, "cst": make_consts()}
    in_maps = [dict(shared, x=np.ascontiguousarray(xc)) for xc in x_cores]
    res = run_bass_kernel_spmd(nc, in_maps, core_ids=list(range(len(x_cores))))
    return res.results


def kernel(**inputs):
    xp = np.asarray(inputs["x_prompt"], np.float32)
    xs = np.asarray(inputs["x_sample"], np.float32)
    x_cores = []
    for c in range(NCORES):
        x_cores.append(np.concatenate([xp[2 * c], xp[2 * c + 1], xs[c]], axis=0))
    res = run(inputs, [2048, 2048, 8192], x_cores, DEPTH)
    yp = np.zeros_like(xp)
    ys = np.zeros_like(xs)
    for c in range(NCORES):
        y = res[c]["y"]
        yp[2 * c] = y[0:2048]
        yp[2 * c + 1] = y[2048:4096]
        ys[c] = y[4096:]
    return (yp, ys)
```

```python
import numpy as np
from contextlib import ExitStack
import concourse.bass as bass
import concourse.mybir as mybir
from concourse.bass_utils import run_bass_kernel_spmd

F32 = mybir.dt.float32
BF16 = mybir.dt.bfloat16
AF = mybir.ActivationFunctionType
ALU = mybir.AluOpType

D = 1024
DEPTH = 4
PAD = 8
EPS = 1e-6
NCORES = 8


class _Rec:
    def __init__(self):
        self.call = None

    def __getattr__(self, name):
        def f(*a, **k):
            self.call = (name, a, k)
            return self
        return f


def _record(fn):
    r = _Rec()
    fn(r)
    return r.call


class Sched:
    SEM_MAX = 60000

    def __init__(self, nc, sems):
        self.nc = nc
        self.free_sems = list(sems)
        self.eng_names = ["pe", "act", "dve", "pool", "sp"]
        self.prog = {e: [] for e in self.eng_names}
        self.esem = {}
        self.ecnt = {}
        for e in self.eng_names:
            self.esem[e] = self.free_sems.pop()
            self.ecnt[e] = 0
        self.seen = {e: {} for e in self.eng_names}
        self.last_w = {}
        self.readers = {}
        self.dma_sem = {}
        self.all_sems = {}
        self.n_instr = 0
        self.pend_r = {e: [] for e in self.eng_names}
        self.pend_w = {e: [] for e in self.eng_names}
        self.yield_hook = None

    def _new_sem(self):
        return self.free_sems.pop()

    def _emit_wait(self, eng, tok):
        if tok is None:
            return
        sem, val, src = tok
        if src == eng and eng == "pe":
            return
        k = id(sem)
        if self.seen[eng].get(k, 0) >= val:
            return
        self.seen[eng][k] = val
        self.prog[eng].append(("wait", sem, val))

    def _deps(self, eng, reads, writes):
        for r in reads:
            self._emit_wait(eng, self.last_w.get(r))
        for w in writes:
            self._emit_wait(eng, self.last_w.get(w))
            for t in self.readers.get(w, ()):
                self._emit_wait(eng, t)

    def _commit(self, tok, reads, writes):
        for r in reads:
            self.readers.setdefault(r, []).append(tok)
        for w in writes:
            self.last_w[w] = tok
            self.readers[w] = []

    def op(self, eng, fn, reads=(), writes=(), inc=True):
        self._deps(eng, reads, writes)
        if not inc:
            self.pend_r[eng].extend(reads)
            self.pend_w[eng].extend(writes)
            self.prog[eng].append(("op", _record(fn), None, 0))
            self.n_instr += 1
            return None
        if self.ecnt[eng] >= self.SEM_MAX:
            self.esem[eng] = self._new_sem()
            self.ecnt[eng] = 0
        self.ecnt[eng] += 1
        tok = (self.esem[eng], self.ecnt[eng], eng)
        self.prog[eng].append(("op", _record(fn), self.esem[eng], 1))
        self.all_sems[id(tok[0])] = (tok[0], tok[1])
        reads = list(reads) + self.pend_r[eng]
        writes = list(writes) + self.pend_w[eng]
        self.pend_r[eng] = []
        self.pend_w[eng] = []
        self._commit(tok, reads, writes)
        self.n_instr += 1
        if self.yield_hook is not None:
            self.yield_hook()
        return tok

    def dma(self, eng, slot, fn, reads=(), writes=()):
        self._deps(eng, reads, writes)
        if slot not in self.dma_sem:
            self.dma_sem[slot] = [self._new_sem(), 0]
        st = self.dma_sem[slot]
        if st[1] + 16 > self.SEM_MAX:
            st[0] = self._new_sem()
            st[1] = 0
        st[1] += 16
        tok = (st[0], st[1], "dma")
        self.all_sems[id(st[0])] = (st[0], st[1])
        self.prog[eng].append(("op", _record(fn), st[0], 16))
        self._commit(tok, reads, writes)
        self.n_instr += 1
        return tok

    def barrier(self, engs=None):
        for e in (engs or self.eng_names):
            for sem, val in list(self.all_sems.values()):
                if val > 0:
                    self._emit_wait(e, (sem, val, "any"))

    def emit(self):
        nc = self.nc
        objs = {"pe": "tensor", "act": "scalar", "dve": "vector", "pool": "gpsimd", "sp": "sync"}
        with nc.Block() as block:
            for e in self.eng_names:
                prog = self.prog[e]

                def body(eobj, prog=prog):
                    for it in prog:
                        if it[0] == "wait":
                            eobj.wait_ge(it[1], it[2])
                        else:
                            name, a, k = it[1]
                            ins = getattr(eobj, name)(*a, **k)
                            if it[2] is not None:
                                ins.then_inc(it[2], it[3])
                getattr(block, objs[e])(body)


def interleave(S, fns):
    import threading
    n = len(fns)
    if n == 1:
        fns[0]()
        return
    sems = [threading.Semaphore(0) for _ in range(n)]
    fin = threading.Semaphore(0)
    done = [False] * n
    errs = []
    idx = {}

    def nxt(i):
        for k in range(1, n + 1):
            j = (i + k) % n
            if not done[j]:
                return j
        return None

    def worker(i):
        sems[i].acquire()
        idx[threading.get_ident()] = i
        try:
            fns[i]()
        except BaseException as ex:
            errs.append(ex)
        finally:
            done[i] = True
            j = nxt(i)
            if j is None:
                fin.release()
            else:
                sems[j].release()

    def hook():
        i = idx.get(threading.get_ident())
        if i is None:
            return
        j = nxt(i)
        if j is not None and j != i:
            sems[j].release()
            sems[i].acquire()

    ths = [threading.Thread(target=worker, args=(i,)) for i in range(n)]
    for t in ths:
        t.start()
    S.yield_hook = hook
    sems[0].release()
    fin.acquire()
    S.yield_hook = None
    for t in ths:
        t.join()
    if errs:
        raise errs[0]


WIN_GROUPS = [
    ("z", 0, 512, "tm", 512),
    ("xbc", 512, 768, "tm", 384),
    ("dt", 1280, 16, "tm", 16),
    ("p", 1296, 512, "fm", 512),
    ("u", 1808, 512, "fm", 512),
    ("v", 2320, 512, "tm", 512),
    ("q", 2832, 256, "fm", 256),
    ("k", 3088, 256, "fm", 256),
    ("gv", 3344, 512, "tm", 512),
    ("r", 3856, 512, "fm", 512),
    ("glr", 4368, 32, "fm", 32),
    ("gate", 4400, 4096, "fm", 512),
]
RB_CONVW = 0
RB_CONVB = RB_CONVW + 5 * 768
RB_DTB = RB_CONVB + 768
RB_ALOG = RB_DTB + 16
RB_D = RB_ALOG + 16
RB_SSDN = RB_D + 8
RB_SGUN = RB_SSDN + 512
RB_SGUB = RB_SGUN + 512
RB_LEN = RB_SGUB + 512
PP_G1, PP_GPOST, PP_GF1, PP_GF2, PP_PSC, PP_GLAN = 0, 8, 16, 24, 32, 36
PP_LEN = 40


class Builder:
    def __init__(self, seqs, depth, debug=(), TT=512):
        self.seqs = list(seqs)
        self.depth = depth
        self.TT = TT
        self.T = sum(seqs)
        self.soff = [int(x) for x in np.cumsum([0] + self.seqs[:-1])]
        self.poff = [int(x) for x in (np.cumsum([0] + [L + 2 * PAD for L in self.seqs[:-1]]) + PAD)]
        self.Tp = sum(L + 2 * PAD for L in self.seqs)
        self.debug = set(debug)
        self.nc = bass.Bass("TRN2", target_bir_lowering=False)
        self.es = ExitStack()
        self.psn = 0

    def dram(self, name, shape, dt, kind="Internal"):
        if name in self.debug:
            kind = "ExternalOutput"
        return self.nc.dram_tensor(name, list(shape), dt, kind=kind).ap()

    def sb(self, name, shape, dt=F32):
        return self.es.enter_context(self.nc.sbuf_tensor("sb_" + name, list(shape), dt))

    def arena_reset(self):
        self.S.barrier()
        self.aoff = 0
        self.aphase += 1

    def arena(self, name, free_shape, dt=F32):
        n = int(np.prod(free_shape))
        words = n if dt == F32 else (n + 1) // 2
        a = self.aoff
        self.aoff += words
        assert self.aoff <= self.AWORDS, (name, self.aoff)
        v = self.arena_t[:, a:a + words]
        if dt != F32:
            v = v.bitcast(dt)[:, 0:n]
        if len(free_shape) == 2:
            v = v.rearrange("p (a b) -> p a b", a=free_shape[0])
        elif len(free_shape) == 3:
            v = v.rearrange("p (a b c) -> p a b c", a=free_shape[0], b=free_shape[1])
        return v, f"A{self.aphase}:{name}"

    def ps(self):
        i = self.psn % 8
        self.psn += 1
        return self.psum[i], f"ps{i}"

    def build(self):
        nc, es = self.nc, self.es
        T, Tp, L = self.T, self.Tp, self.depth
        dr = self.dram
        inp = lambda name, shape: nc.dram_tensor(name, list(shape), F32, kind="ExternalInput").ap()
        self.x_in = inp("x", [T, D])
        self.y_out = nc.dram_tensor("y", [T, D], F32, kind="ExternalOutput").ap()
        self.w_in = inp("w_in", [L, D, 8496])
        self.w_branch = inp("w_branch", [L, 4, 512, D])
        self.w_out = inp("w_out", [L, D, D])
        self.w_ff1 = inp("w_ff1", [L, D, 4096])
        self.w_ff2 = inp("w_ff2", [L, 4096, D])
        self.pp_in = inp("pp", [L, 128, PP_LEN])
        self.rb_in = inp("rb", [L, RB_LEN])
        self.sguw_in = inp("sguwT", [L, 128, 512])
        self.poolw_in = inp("poolw", [L, 128, 512])
        self.w2_in = inp("w2blk", [L, 33, 512])
        self.cst_in = inp("cst", [128, 1860])
        self.XT = dr("XT", [8, 128, T], F32)
        self.wsc = {}
        for l in range(L):
            for (g, off, n, ori, nb) in WIN_GROUPS:
                self.wsc[(l, g)] = dr(f"w{l}_{g}", [n // nb, 128, 8 * nb], BF16)
            self.wsc[(l, "br")] = dr(f"w{l}_br", [4, 128, 4 * 1024], BF16)
            self.wsc[(l, "out")] = dr(f"w{l}_out", [2, 128, 8 * 512], BF16)
            self.wsc[(l, "ff1")] = dr(f"w{l}_ff1", [8, 128, 8 * 512], BF16)
            self.wsc[(l, "ff2")] = dr(f"w{l}_ff2", [4, 128, 32 * 256], BF16)
        self.P = {}
        self.P["z"] = dr("P_z", [T, 512], BF16)
        self.P["xbc"] = dr("P_xbc", [Tp, 768], BF16)
        self.P["dt"] = dr("P_dt", [T, 16], F32)
        self.P["p"] = dr("P_p", [4, 128, Tp], BF16)
        self.P["u"] = dr("P_u", [4, 128, T], BF16)
        self.P["v"] = dr("P_v", [T, 512], BF16)
        self.P["q"] = dr("P_q", [2, 128, T], BF16)
        self.P["k"] = dr("P_k", [2, 128, T], BF16)
        self.P["gv"] = dr("P_gv", [T, 512], BF16)
        self.P["r"] = dr("P_r", [4, 128, T], BF16)
        self.P["glr"] = dr("P_glr", [32, T], BF16)
        self.P["gate"] = dr("P_gate", [32, 128, T], BF16)
        self.BR = dr("BR", [4, 4, 128, T], BF16)

        sems = [es.enter_context(nc.semaphore(f"s{i}")) for i in range(100)]
        self.S = Sched(nc, sems)
        self.psum = [es.enter_context(nc.psum_tensor(f"psb{i}", [128, 512], F32)) for i in range(8)]
        self.cst = self.sb("cst", [128, 1860])
        self.cstb = self.sb("cstb", [128, 4 * 128 + 2 * 512], BF16)
        self.AWORDS = 42000
        self.arena_t = self.sb("arena", [128, self.AWORDS], F32)
        self.aoff = 0
        self.aphase = 0
        self.pp = self.sb("pp", [128, PP_LEN])
        self.rb = self.sb("rbt", [128, RB_LEN])
        self.sguw = self.sb("sguw", [128, 512], BF16)
        self.poolw = self.sb("poolw", [128, 512], BF16)
        self.w2b = self.sb("w2b", [33, 512], BF16)
        self.anegt = self.sb("anegt", [128, 16])
        self.zero_t = self.sb("zero_t", [128, 768], BF16)

        S = self.S
        cst = self.cst
        S.dma("sp", "cst", lambda e: e.dma_start(out=cst[:], in_=self.cst_in[:, :]), writes=["cst"])
        self.identf = cst[:, 0:128]
        self.onesf = cst[:, 128:256]
        self.Uf = cst[:, 256:384]
        self.Lf = cst[:, 384:512]
        self.Uneg = cst[:, 512:640]
        self.Lneg = cst[:, 640:768]
        cb = self.cstb
        S.op("dve", lambda e: e.tensor_copy(out=cb[:, 0:128], in_=cst[:, 0:128]), reads=["cst"], writes=["cstb"])
        S.op("dve", lambda e: e.tensor_copy(out=cb[:, 128:256], in_=cst[:, 128:256]), reads=["cst"], writes=["cstb"])
        S.op("dve", lambda e: e.tensor_copy(out=cb[:, 256:512], in_=cst[:, 256:512]), reads=["cst"], writes=["cstb"])
        S.op("dve", lambda e: e.tensor_copy(out=cb[:, 512:1536], in_=cst[:, 768:1792]), reads=["cst"], writes=["cstb"])
        self.identb = cb[:, 0:128]
        self.onesb = cb[:, 128:256]
        self.m01 = {"f": cb[:, 256:384], "b": cb[:, 384:512]}
        self.negm = {"f": cb[:, 512:1024], "b": cb[:, 1024:1536]}
        self.tri = {"f": self.Uf, "b": self.Lf}
        self.trineg = {"f": self.Uneg, "b": self.Lneg}
        zt = self.zero_t
        S.op("dve", lambda e: e.memset(zt[:], 0.0), writes=["zero_t"])
        self.zero_pads()
        self.phase0_x()
        for l in range(L):
            self.prep_weights(l)
        for l in range(L):
            self.layer(l)
        self.phase_final()
        S.barrier(["sp"])
        S.emit()
        return nc

    def zero_pads(self):
        S, zt = self.S, self.zero_t
        for s, Ls in enumerate(self.seqs):
            for (r0) in (self.poff[s] - PAD, self.poff[s] + Ls):
                S.dma("pool", "zp", lambda e, r0=r0: e.dma_start(out=self.P["xbc"][r0:r0 + PAD, :], in_=zt[0:PAD, :]),
                      reads=["zero_t"], writes=[("Pxbc_pad", r0)])
                S.dma("pool", "zp", lambda e, r0=r0: e.dma_start(
                    out=self.P["p"][:, :, r0:r0 + PAD].rearrange("g p t -> p g t"),
                    in_=zt[:, 0:4 * PAD].rearrange("p (g t) -> p g t", g=4)),
                    reads=["zero_t"], writes=[("Pp_pad", r0)])

    def phase0_x(self):
        S = self.S
        self.arena_reset()
        xin = [self.arena(f"xin{i}", [4, D]) for i in range(2)]
        xo = [self.arena(f"xo{i}", [8, 512]) for i in range(2)]
        for ti in range(self.T // 512):
            xi, xir = xin[ti % 2]
            xot, xor_ = xo[ti % 2]
            t0 = ti * 512
            S.dma("sp", f"xin{ti % 2}", lambda e, xi=xi, t0=t0: e.dma_start(
                out=xi, in_=self.x_in[t0:t0 + 512, :].rearrange("(j p) d -> p j d", p=128)), writes=[xir])
            for kc in range(8):
                pt, pr = self.ps()
                for j in range(4):
                    S.op("pe", lambda e, pt=pt, xi=xi, j=j, kc=kc: e.transpose(
                        pt[:, j * 128:(j + 1) * 128], xi[:, j, kc * 128:(kc + 1) * 128], self.identf),
                        reads=[xir, "cst"], writes=[pr], inc=(j == 3))
                eng = "dve" if kc % 2 == 0 else "act"
                if eng == "dve":
                    S.op("dve", lambda e, pt=pt, xot=xot, kc=kc: e.tensor_copy(out=xot[:, kc, :], in_=pt[:]),
                         reads=[pr], writes=[(xor_, kc)])
                else:
                    S.op("act", lambda e, pt=pt, xot=xot, kc=kc: e.copy(out=xot[:, kc, :], in_=pt[:]),
                         reads=[pr], writes=[(xor_, kc)])
            S.dma("pool", f"xo{ti % 2}", lambda e, xot=xot, t0=t0: e.dma_start(
                out=self.XT[:, :, t0:t0 + 512].rearrange("k p t -> p k t"), in_=xot),
                reads=[(xor_, kc) for kc in range(8)], writes=[("XT", ti)])

    def prep_weights(self, l):
        S = self.S
        self.arena_reset()
        st32 = [self.arena(f"ws32_{i}", [8192]) for i in range(2)]
        st16 = [self.arena(f"ws16_{i}", [8192], BF16) for i in range(2)]
        cnt = [0]

        def block(src2d, kc_n, c0, nb, dst_blk):
            i = cnt[0] % 2
            cnt[0] += 1
            s32, r32 = st32[i]
            s16, r16 = st16[i]
            n = kc_n * nb
            S.dma("sp", f"ws32_{i}", lambda e: e.dma_start(
                out=s32[:, 0:n].rearrange("p (k n) -> p k n", k=kc_n),
                in_=src2d[:, c0:c0 + nb].rearrange("(k p) n -> p k n", p=128)), writes=[r32])
            eng = ["dve", "act"][cnt[0] % 2]
            if eng == "act":
                S.op("act", lambda e: e.copy(out=s16[:, 0:n], in_=s32[:, 0:n]), reads=[r32], writes=[r16])
            else:
                S.op("dve", lambda e: e.tensor_copy(out=s16[:, 0:n], in_=s32[:, 0:n]), reads=[r32], writes=[r16])
            S.dma("pool", f"ws16_{i}", lambda e: e.dma_start(out=dst_blk, in_=s16[:, 0:n]), reads=[r16],
                  writes=[("wsc", l)])

        for (g, off, n, ori, nb) in WIN_GROUPS:
            for j in range(n // nb):
                block(self.w_in[l], 8, off + j * nb, nb, self.wsc[(l, g)][j])
        for n_ in range(4):
            block(self.w_branch[l, n_], 4, 0, 1024, self.wsc[(l, "br")][n_])
        for j in range(2):
            block(self.w_out[l], 8, j * 512, 512, self.wsc[(l, "out")][j])
        for j in range(8):
            block(self.w_ff1[l], 8, j * 512, 512, self.wsc[(l, "ff1")][j])
        for j in range(4):
            block(self.w_ff2[l], 32, j * 256, 256, self.wsc[(l, "ff2")][j])

    def layer(self, l):
        self.load_layer_params(l)
        self.phase1(l)
        if "stop1" in self.debug:
            return
        self.phase2(l)
        if "stop2" in self.debug:
            return
        self.phase3(l)

    def load_layer_params(self, l):
        S = self.S
        self.arena_reset()
        S.dma("sp", "pp", lambda e: e.dma_start(out=self.pp[:], in_=self.pp_in[l]), writes=["pp"])
        S.dma("sp", "rb", lambda e: e.dma_start(out=self.rb[:], in_=self.rb_in[l:l + 1, :].partition_broadcast(128)),
              writes=["rb"])
        t32, r32 = self.arena("lp32", [512])
        for (src, dst, name, rows) in ((self.sguw_in, self.sguw, "sguw", 128), (self.poolw_in, self.poolw, "poolw", 128),
                                       (self.w2_in, self.w2b, "w2b", 33)):
            S.dma("sp", "lp32", lambda e, src=src, rows=rows: e.dma_start(out=t32[0:rows, :], in_=src[l]), writes=[r32])
            S.op("dve", lambda e, dst=dst, rows=rows: e.tensor_copy(out=dst[0:rows, :], in_=t32[0:rows, :]),
                 reads=[r32], writes=[name])
        S.op("act", lambda e: e.activation(out=self.anegt[:], in_=self.rb[:, RB_ALOG:RB_ALOG + 16], func=AF.Exp),
             reads=["rb"], writes=["anegt"])
        S.op("dve", lambda e: e.tensor_scalar(out=self.anegt[:], in0=self.anegt[:], scalar1=-1.0, scalar2=None,
                                              op0=ALU.mult), reads=["anegt"], writes=["anegt"])

    def rstd_from_sq(self, sq, sqr, nk, rstd, rstdr, dim):
        S = self.S
        pt, pr = self.ps()
        for kc in range(nk):
            S.op("pe", lambda e, kc=kc: e.matmul(pt[:], lhsT=self.onesb, rhs=sq[:, kc, :], start=(kc == 0),
                                                 stop=(kc == nk - 1)),
                 reads=[(sqr, k_) for k_ in range(nk)] + ["cstb"], writes=[pr], inc=(kc == nk - 1))
        S.op("dve", lambda e: e.tensor_scalar(out=rstd, in0=pt[:], scalar1=1.0 / dim, scalar2=EPS, op0=ALU.mult,
                                              op1=ALU.add), reads=[pr], writes=[rstdr])
        S.op("act", lambda e: e.activation(out=rstd, in_=rstd, func=AF.Sqrt), reads=[rstdr], writes=[rstdr])
        S.op("dve", lambda e: e.reciprocal(out=rstd, in_=rstd), reads=[rstdr], writes=[rstdr])

    def phase1(self, l):
        S = self.S
        self.arena_reset()
        TT = 1024
        NH = TT // 512
        xT = [self.arena(f"xT{i}", [8, TT]) for i in range(1)]
        sq, sqr = self.arena("sq", [8, TT], BF16)
        hT, hTr = self.arena("hT", [8, TT], BF16)
        rstd, rstdr = self.arena("rstd", [TT])
        wb = [self.arena(f"wb{i}", [8192], BF16) for i in range(4)]
        NWB = len(wb)
        stg = [self.arena(f"stg{i}", [512], BF16) for i in range(4)]
        stg32 = [self.arena(f"stgf{i}", [16]) for i in range(2)]
        wcnt = [0]
        scnt = [0]
        pp = self.pp

        def pad_off(t0):
            for s in range(len(self.seqs)):
                if self.soff[s] <= t0 < self.soff[s] + self.seqs[s]:
                    return self.poff[s] + (t0 - self.soff[s])
            raise AssertionError
        evac_funcs = {"u": AF.Gelu_apprx_tanh, "v": AF.Gelu_apprx_tanh, "r": AF.Silu, "z": AF.Silu, "gate": AF.Sigmoid}
        assert self.T % TT == 0
        for ti in range(self.T // TT):
            t0 = ti * TT
            tp0 = pad_off(t0)
            xt, xr = xT[0]
            S.dma("sp", "xT0", lambda e: e.dma_start(
                out=xt, in_=self.XT[:, :, t0:t0 + TT].rearrange("k p t -> p k t")), writes=[xr])
            for hf in range(NH):
                hs = slice(hf * 512, (hf + 1) * 512)
                for kc in range(8):
                    S.op("act", lambda e: e.activation(out=sq[:, kc, hs], in_=xt[:, kc, hs], func=AF.Square),
                         reads=[xr], writes=[(sqr, kc, hf)])
                pt, pr = self.ps()
                for kc in range(8):
                    S.op("pe", lambda e: e.matmul(pt[:], lhsT=self.onesb, rhs=sq[:, kc, hs], start=(kc == 0), stop=(kc == 7)),
                         reads=[(sqr, k_, hf) for k_ in range(8)] + ["cstb"], writes=[pr], inc=(kc == 7))
                rr = (rstdr, hf)
                S.op("dve", lambda e: e.tensor_scalar(out=rstd[:, hs], in0=pt[:], scalar1=1.0 / D, scalar2=EPS, op0=ALU.mult, op1=ALU.add),
                     reads=[pr], writes=[rr])
                S.op("act", lambda e: e.activation(out=rstd[:, hs], in_=rstd[:, hs], func=AF.Sqrt), reads=[rr], writes=[rr])
                S.op("dve", lambda e: e.reciprocal(out=rstd[:, hs], in_=rstd[:, hs]), reads=[rr], writes=[rr])
                for kc in range(8):
                    S.op("dve", lambda e: e.scalar_tensor_tensor(
                        out=hT[:, kc, hs], in0=xt[:, kc, hs], scalar=pp[:, PP_G1 + kc:PP_G1 + kc + 1], in1=rstd[:, hs],
                        op0=ALU.mult, op1=ALU.mult), reads=[xr, rr, "pp"], writes=[(hTr, kc, hf)])
            for (g, off, n, ori, nb) in WIN_GROUPS:
                for j in range(n // nb):
                    w, wr = wb[wcnt[0] % NWB]
                    slot = f"wb{wcnt[0] % NWB}"
                    wcnt[0] += 1
                    S.dma("sp", slot, lambda e: e.dma_start(out=w[:, 0:8 * nb], in_=self.wsc[(l, g)][j]), writes=[wr])
                    wv = w[:, 0:8 * nb].rearrange("p (k n) -> p k n", k=8)
                    for hf in range(NH):
                        hs = slice(hf * 512, (hf + 1) * 512)
                        hreads = [(hTr, kc, hf) for kc in range(8)]
                        th0 = t0 + hf * 512
                        tph0 = tp0 + hf * 512
                        if ori == "fm":
                            for cc in range(max(1, nb // 128)):
                                m = min(128, nb)
                                pt, pr = self.ps()
                                for kc in range(8):
                                    S.op("pe", lambda e: e.matmul(
                                        pt[0:m, :], lhsT=wv[:, kc, cc * 128:cc * 128 + m], rhs=hT[:, kc, hs],
                                        start=(kc == 0), stop=(kc == 7)), reads=[wr] + hreads, writes=[pr], inc=(kc == 7))
                                st, sr = stg[scnt[0] % 4]
                                sslot = f"stg{scnt[0] % 4}"
                                scnt[0] += 1
                                cidx = j * (nb // 128) + cc if nb >= 128 else 0
                                fn = evac_funcs.get(g)
                                if fn is not None:
                                    S.op("act", lambda e: e.activation(out=st[0:m, :], in_=pt[0:m, :], func=fn), reads=[pr], writes=[sr])
                                elif g == "q":
                                    S.op("act", lambda e: e.mul(out=st[0:m, :], in_=pt[0:m, :], mul=0.125), reads=[pr], writes=[sr])
                                else:
                                    S.op("dve", lambda e: e.tensor_copy(out=st[0:m, :], in_=pt[0:m, :]), reads=[pr], writes=[sr])
                                if g == "p":
                                    dst = self.P["p"][cidx, :, tph0:tph0 + 512]
                                elif g == "glr":
                                    dst = self.P["glr"][:, th0:th0 + 512]
                                else:
                                    dst = self.P[g][cidx, :, th0:th0 + 512]
                                S.dma("pool", sslot, lambda e: e.dma_start(out=dst, in_=st[0:m, :]), reads=[sr], writes=[("P" + g, ti)])
                        else:
                            for i in range(4):
                                pt, pr = self.ps()
                                ts_ = slice(hf * 512 + i * 128, hf * 512 + (i + 1) * 128)
                                for kc in range(8):
                                    S.op("pe", lambda e: e.matmul(
                                        pt[:, 0:nb], lhsT=hT[:, kc, ts_], rhs=wv[:, kc, :],
                                        start=(kc == 0), stop=(kc == 7)), reads=[wr] + hreads, writes=[pr], inc=(kc == 7))
                                if g == "dt":
                                    st, sr = stg32[scnt[0] % 2]
                                    sslot = f"stgf{scnt[0] % 2}"
                                else:
                                    st, sr = stg[scnt[0] % 4]
                                    sslot = f"stg{scnt[0] % 4}"
                                scnt[0] += 1
                                fn = evac_funcs.get(g)
                                if fn is not None:
                                    S.op("act", lambda e: e.activation(out=st[:, 0:nb], in_=pt[:, 0:nb], func=fn), reads=[pr], writes=[sr])
                                else:
                                    S.op("dve", lambda e: e.tensor_copy(out=st[:, 0:nb], in_=pt[:, 0:nb]), reads=[pr], writes=[sr])
                                if g == "xbc":
                                    dst = self.P["xbc"][tph0 + i * 128:tph0 + (i + 1) * 128, j * nb:(j + 1) * nb]
                                else:
                                    dst = self.P[g][th0 + i * 128:th0 + (i + 1) * 128, j * nb:(j + 1) * nb]
                                S.dma("pool", sslot, lambda e: e.dma_start(out=dst, in_=st[:, 0:nb]), reads=[sr], writes=[("P" + g, ti)])

    def phase2(self, l):
        S = self.S
        self.arena_reset()
        self.psn = 0
        ar = self.arena
        rb, pp = self.rb, self.pp
        NCMAX = max(self.seqs) // 128
        Sst_f, Sst_fr = ar("Sst_f", [NCMAX, 256], BF16)
        Sgl_f, Sgl_fr = ar("Sgl_f", [NCMAX, 256], BF16)
        Sssd, Sssdr = ar("Sssd", [256])
        Sgla, Sglar = ar("Sgla", [2, 128])
        Sin_b, Sin_br = ar("Sin_b", [256], BF16)
        Gin_b, Gin_br = ar("Gin_b", [2, 128], BF16)
        xs0 = [ar(f"xs_{k}", [768], BF16) for k in range(5)]
        xs = [xs0, xs0]
        acc, accr = ar("acc", [768])
        ctmp, ctmpr = ar("ctmp", [768])
        xa, xar = ar("xa", [768], BF16)
        dtr = [ar(f"dtr{i}", [16]) for i in range(2)]
        dtt, dttr = ar("dtt", [16])
        dta, dtar = ar("dta", [16])
        ndta, ndtar = ar("ndta", [16])
        sc, scr = ar("sc", [32])
        d1, d1r = ar("d1", [16])
        dte, dter = ar("dte", [16])
        wst, wstr = ar("wst", [16])
        etot, etotr = ar("etot", [16])
        etsel, etselr = ar("etsel", [4])
        eacs, eacsr = ar("eacs", [16])
        xw, xwr = ar("xw", [512], BF16)
        xdt = {X: ar(f"xdt{X}", [512], BF16) for X in "fb"}
        dtaU = {X: ar(f"dtaU{X}", [8, 128]) for X in "fb"}
        BCT, BCTr = ar("BCT", [2, 128], BF16)
        CBT, CBTr = ar("CBT", [2, 128], BF16)
        Cblk, Cblkr = ar("Cblk", [2, 128], BF16)
        Sblk = {X: ar(f"Sblk{X}", [2, 256], BF16) for X in "fb"}
        Qblk, Qblkr = ar("Qblk", [4, 2, 128], BF16)
        Et = [ar(f"E{i}", [4, 128], BF16) for i in range(2)]
        Mt = {(X, g): ar(f"M{X}{g}", [4, 128], BF16) for X in "fb" for g in range(2)}
        y1, y1r = ar("y1", [512])
        y2, y2r = ar("y2", [512])
        zt = [ar(f"zt{i}", [512], BF16) for i in range(2)]
        junk, junkr = ar("junk", [512])
        ss2, ss2r = ar("ss2", [2])
        ya, yar = ar("ya", [512], BF16)
        brs = [ar(f"brs{i}", [4, 128], BF16) for i in range(4)]
        glr = [ar(f"glr{i}", [128], BF16) for i in range(2)]
        e1, e1r = ar("e1", [512])
        sp_, spr = ar("sp", [512])
        ebt, ebtr = ar("ebt", [4])
        ek, ekr = ar("ek", [4, 128])
        eq, eqr = ar("eq", [4, 128])
        kT = [ar(f"kT{i}", [2, 128], BF16) for i in range(2)]
        qT = [ar(f"qT{i}", [2, 128], BF16) for i in range(2)]
        vtm = [ar(f"vtm{i}", [512], BF16) for i in range(2)]
        rT = [ar(f"rT{i}", [4, 128], BF16) for i in range(2)]
        kinv, kinvr = ar("kinv", [4, 128], BF16)
        kend, kendr = ar("kend", [4, 128], BF16)
        qdec, qdecr = ar("qdec", [4, 128], BF16)
        kendtm, kendtmr = ar("kendtm", [4, 128], BF16)
        attm = {X: ar(f"attm{X}", [4, 128], BF16) for X in "fb"}
        osq, osqr = ar("osq", [512], BF16)
        orstd, orstdr = ar("orstd", [512])
        o1, o1r = ar("o1", [512])
        rg, rgr = ar("rg", [4, 128])
        vt = [ar(f"vt{i}", [512], BF16) for i in range(2)]
        uT = [ar(f"uT{i}", [4, 128], BF16) for i in range(2)]
        ss1, ss1r = ar("ss1", [2])
        vf, vfr = ar("vf", [512], BF16)
        sg1, sg1r = ar("sg1", [4, 128])
        ptl = [ar(f"ptl{i}", [4, 144], BF16) for i in range(2)]
        a1, a1r = ar("a1", [3, 144])
        a2, a2r = ar("a2", [2, 144])
        a3, a3r = ar("a3", [144])
        wsum, wsumr = ar("wsum", [4, 128])
        pooled, pooledr = ar("pooled", [4, 128], BF16)
        ldc = [0]
        brc = [0]
        V = lambda fn, r=(), w=(): S.op("dve", fn, reads=r, writes=w)
        A = lambda fn, r=(), w=(): S.op("act", fn, reads=r, writes=w)
        PE = lambda fn, r=(), w=(), inc=True: S.op("pe", fn, reads=r, writes=w, inc=inc)
        cw = lambda k: rb[:, RB_CONVW + k * 768:RB_CONVW + (k + 1) * 768]
        for i in range(2):
            g_, gr_ = glr[i]
            V(lambda e, g_=g_: e.memset(g_[32:33, :], 1.0), w=[gr_ + "one"])

        def bc(ap2, n):
            return ap2.unsqueeze(2).to_broadcast([ap2.shape[0], ap2.shape[1], n])

        def store_branch(n, src, srcr, tok0):
            S.dma("pool", "brs" + srcr, lambda e: e.dma_start(
                out=self.BR[n, :, :, tok0:tok0 + 128].rearrange("w p t -> p w t"), in_=src), reads=[srcr], writes=[("BR", n, tok0)])

        def next_brs():
            b = brs[brc[0] % 4]
            brc[0] += 1
            return b

        def ssd_common(tp, tk, i):
            for k in range(5):
                x_, xr_ = xs[i][k]
                S.dma("sp", f"xs{i}_{k}", lambda e, x_=x_, k=k: e.dma_start(out=x_, in_=self.P["xbc"][tp + k - 2:tp + k - 2 + 128, :]), writes=[xr_])
            d_, dr_ = dtr[i]
            S.dma("sp", f"dtr{i}", lambda e: e.dma_start(out=d_, in_=self.P["dt"][tk:tk + 128, :]), writes=[dr_])
            V(lambda e: e.tensor_tensor(out=acc, in0=xs[i][0][0], in1=cw(0), op=ALU.mult), [xs[i][0][1], "rb"], [accr])
            for k in range(1, 5):
                V(lambda e, k=k: e.tensor_tensor(out=ctmp, in0=xs[i][k][0], in1=cw(k), op=ALU.mult), [xs[i][k][1], "rb"], [ctmpr])
                V(lambda e: e.tensor_tensor(out=acc, in0=acc, in1=ctmp, op=ALU.add), [accr, ctmpr], [accr])
            V(lambda e: e.tensor_tensor(out=acc, in0=acc, in1=rb[:, RB_CONVB:RB_CONVB + 768], op=ALU.add), [accr, "rb"], [accr])
            A(lambda e: e.activation(out=xa, in_=acc, func=AF.Silu), [accr], [xar])
            if "s1" in self.debug:
                return
            V(lambda e: e.tensor_tensor(out=dtt, in0=d_, in1=rb[:, RB_DTB:RB_DTB + 16], op=ALU.add), [dr_, "rb"], [dttr])
            A(lambda e: e.activation(out=dtt, in_=dtt, func=AF.Exp), [dttr], [dttr])
            A(lambda e: e.activation(out=dtt, in_=dtt, func=AF.Ln, bias=1.0), [dttr], [dttr])
            V(lambda e: e.tensor_tensor(out=dta, in0=dtt, in1=self.anegt[:], op=ALU.mult), [dttr, "anegt"], [dtar])
            if "s2" in self.debug:
                return
            pt, pr = self.ps()
            PE(lambda e: e.matmul(pt[:, 0:8], lhsT=self.Uf, rhs=dta[:, 0:8], start=True, stop=True), [dtar, "cst"], [pr], inc=False)
            PE(lambda e: e.matmul(pt[:, 8:16], lhsT=self.Lf, rhs=dta[:, 8:16], start=True, stop=True), [dtar, "cst"], [pr], inc=False)
            PE(lambda e: e.matmul(pt[:, 16:32], lhsT=self.onesf, rhs=dta[:, 0:16], start=True, stop=True), [dtar, "cst"], [pr])
            V(lambda e: e.tensor_copy(out=sc, in_=pt[:, 0:32]), [pr], [scr])
            V(lambda e: e.tensor_tensor(out=d1, in0=sc[:, 16:32], in1=sc[:, 0:16], op=ALU.subtract), [scr], [d1r])
            A(lambda e: e.activation(out=dte, in_=d1, func=AF.Exp), [d1r], [dter])
            A(lambda e: e.activation(out=etot, in_=sc[:, 16:32], func=AF.Exp), [scr], [etotr])
            V(lambda e: e.tensor_tensor(out=wst, in0=dtt, in1=dte, op=ALU.mult), [dttr, dter], [wstr])

        def ssd_state_update(X, c, store):
            if "s1" in self.debug or "s2" in self.debug or "s3" in self.debug:
                return
            xo = 0 if X == "f" else 8
            V(lambda e: e.tensor_tensor(out=xw.rearrange("p (h d) -> p h d", h=8), in0=xa[:, 0:512].rearrange("p (h d) -> p h d", h=8),
                                        in1=bc(wst[:, xo:xo + 8], 64), op=ALU.mult), [xar, wstr], [xwr])
            pt, pr = self.ps()
            PE(lambda e: e.matmul(pt[:], lhsT=xa[:, 512:640], rhs=xw, start=True, stop=True), [xar, xwr], [pr])
            if store:
                V(lambda e: e.tensor_copy(out=Sst_f[:, c, :], in_=Sssd), [Sssdr], [(Sst_fr, c)])
            V(lambda e: e.tensor_scalar(out=etsel, in0=etot[:, xo:xo + 4], scalar1=self.cst[:, 1856:1857], scalar2=None, op0=ALU.mult), [etotr, "cst"], [etselr])
            V(lambda e: e.scalar_tensor_tensor(out=etsel, in0=etot[:, xo + 4:xo + 8], scalar=self.cst[:, 1857:1858], in1=etsel, op0=ALU.mult, op1=ALU.add),
              [etotr, etselr, "cst"], [etselr])
            V(lambda e: e.tensor_tensor(out=Sssd.rearrange("p (h d) -> p h d", h=4), in0=Sssd.rearrange("p (h d) -> p h d", h=4), in1=bc(etsel, 64), op=ALU.mult),
              [Sssdr, etselr], [Sssdr])
            for g in range(2):
                V(lambda e, g=g, pt=pt: e.scalar_tensor_tensor(out=Sssd, in0=pt[:, g * 256:(g + 1) * 256], scalar=self.cst[:, 1856 + g:1857 + g], in1=Sssd,
                                                               op0=ALU.mult, op1=ALU.add), [Sssdr, pr, "cst"], [Sssdr])

        def gla_common(tk, dirs, i):
            g_, gr_ = glr[i]
            S.dma("sp", f"glr{i}", lambda e: e.dma_start(out=g_[0:32, :], in_=self.P["glr"][:, tk:tk + 128]), writes=[gr_])
            k_, kr_ = kT[i]
            S.dma("sp", f"kT{i}", lambda e: e.dma_start(out=k_, in_=self.P["k"][:, :, tk:tk + 128].rearrange("h p t -> p h t")), writes=[kr_])
            v_, vr_ = vtm[i]
            S.dma("sp", f"vtm{i}", lambda e: e.dma_start(out=v_, in_=self.P["gv"][tk:tk + 128, :]), writes=[vr_])
            pg, pgr = self.ps()
            PE(lambda e: e.matmul(pg[:], lhsT=g_[0:33, :], rhs=self.w2b[0:33, :], start=True, stop=True), [gr_, gr_ + "one", "w2b"], [pgr])
            A(lambda e: e.activation(out=e1, in_=pg[:], func=AF.Exp, scale=-1.0), [pgr], [e1r])
            A(lambda e: e.activation(out=sp_, in_=e1, func=AF.Ln, bias=1.0), [e1r], [spr])
            pb, pbr = self.ps()
            pb3 = pb[:].rearrange("p (a t) -> p a t", a=4)
            combos = [(X, hp) for X in dirs for hp in range(2)]
            for n_, (X, hp) in enumerate(combos):
                xi = 0 if X == "f" else 1
                PE(lambda e, X=X, hp=hp, xi=xi: e.matmul(pb3[:, xi * 2 + hp, :], lhsT=sp_[:, xi * 256 + hp * 128:xi * 256 + (hp + 1) * 128],
                                                         rhs=self.trineg[X], start=True, stop=True), [spr, "cst"], [pbr], inc=(n_ == len(combos) - 1))
            for X in dirs:
                xi = 0 if X == "f" else 1
                col = 127 if X == "f" else 0
                A(lambda e, xi=xi, col=col: e.activation(out=ebt[:, xi * 2:xi * 2 + 2], in_=pb3[:, xi * 2:xi * 2 + 2, col], func=AF.Exp), [pbr], [ebtr + X])
            lo, hi = (0, 2) if dirs == "f" else (0, 4)
            A(lambda e: e.activation(out=ek[:, lo:hi, :], in_=pb3[:, lo:hi, :], func=AF.Exp, scale=-1.0), [pbr], [ekr])
            if dirs == "fb":
                A(lambda e: e.activation(out=eq, in_=pb3, func=AF.Exp), [pbr], [eqr])
            nx = hi // 2
            V(lambda e: e.tensor_tensor(out=kinv[:, lo:hi, :].rearrange("p (x h) t -> p x h t", x=nx),
                                        in0=ek[:, lo:hi, :].rearrange("p (x h) t -> p x h t", x=nx),
                                        in1=k_.unsqueeze(1).to_broadcast([128, nx, 2, 128]), op=ALU.mult), [ekr, kr_], [kinvr])
            V(lambda e: e.tensor_tensor(out=kend[:, lo:hi, :], in0=kinv[:, lo:hi, :], in1=bc(ebt[:, lo:hi], 128), op=ALU.mult),
              [kinvr] + [ebtr + X for X in dirs], [kendr])
            ptb, ptr_ = self.ps()
            ptb3 = ptb[:].bitcast(BF16)[:, 0:512].rearrange("p (a t) -> p a t", a=4)
            for a_ in range(lo, hi):
                PE(lambda e, a_=a_: e.transpose(ptb3[:, a_, :], kend[:, a_, :], self.identb), [kendr, "cstb"], [ptr_], inc=(a_ == hi - 1))
            V(lambda e: e.tensor_copy(out=kendtm[:, lo:hi, :], in_=ptb3[:, lo:hi, :]), [ptr_], [kendtmr])
            return i

        def gla_state_update(X, c, i, store):
            xi = 0 if X == "f" else 1
            v_, vr_ = vtm[i]
            pt, pr = self.ps()
            for hp in range(2):
                PE(lambda e, hp=hp: e.matmul(pt[:, hp * 256:(hp + 1) * 256], lhsT=kendtm[:, xi * 2 + hp, :], rhs=v_[:, hp * 256:(hp + 1) * 256],
                                             start=True, stop=True), [kendtmr, vr_], [pr], inc=(hp == 1))
            if store:
                V(lambda e: e.tensor_copy(out=Sgl_f[:, c, :].rearrange("p (h v) -> p h v", h=2), in_=Sgla), [Sglar], [(Sgl_fr, c)])
            V(lambda e: e.tensor_tensor(out=Sgla, in0=Sgla, in1=bc(ebt[:, xi * 2:xi * 2 + 2], 128), op=ALU.mult), [Sglar, ebtr + X], [Sglar])
            p4 = pt[:].rearrange("p (a b v) -> p a b v", a=2, b=2)
            for hh in range(2):
                V(lambda e, hh=hh: e.scalar_tensor_tensor(out=Sgla, in0=p4[:, :, hh, :], scalar=self.cst[:, 1856 + hh:1857 + hh], in1=Sgla,
                                                          op0=ALU.mult, op1=ALU.add), [Sglar, pr, "cst"], [Sglar])

        for s, Ls in enumerate(self.seqs):
            nch = Ls // 128
            V(lambda e: e.memset(Sssd, 0.0), w=[Sssdr])
            V(lambda e: e.memset(Sgla, 0.0), w=[Sglar])
            for c in range(nch):
                tp = self.poff[s] + c * 128
                tk = self.soff[s] + c * 128
                i = c % 2

                def fA_ssd():
                    ssd_common(tp, tk, i)
                    ssd_state_update("f", c, True)

                def fA_gla():
                    gla_common(tk, "f", i)
                    gla_state_update("f", c, i, True)
                interleave(S, [fA_ssd, fA_gla])
            V(lambda e: e.memset(Sssd, 0.0), w=[Sssdr])
            V(lambda e: e.memset(Sgla, 0.0), w=[Sglar])
            for c in range(nch - 1, -1, -1):
                tp = self.poff[s] + c * 128
                tk = self.soff[s] + c * 128
                i = c % 2

                def fB_ssd():
                    ssd_common(tp, tk, i)
                    li = i
                    z_, zr_ = zt[li]
                    S.dma("sp", f"zt{li}", lambda e, z_=z_, tk=tk: e.dma_start(out=z_, in_=self.P["z"][tk:tk + 128, :]), writes=[zr_])
                    A(lambda e: e.activation(out=eacs, in_=sc[:, 0:16], func=AF.Exp), [scr], [eacsr])
                    V(lambda e: e.tensor_scalar(out=ndta, in0=dta, scalar1=-1.0, scalar2=None, op0=ALU.mult), [dtar], [ndtar])
                    for X in "fb":
                        xo = 0 if X == "f" else 8
                        xd, xdr = xdt[X]
                        V(lambda e, xd=xd, xo=xo: e.tensor_tensor(out=xd.rearrange("p (h d) -> p h d", h=8), in0=xa[:, 0:512].rearrange("p (h d) -> p h d", h=8),
                                                                  in1=bc(dtt[:, xo:xo + 8], 64), op=ALU.mult), [xar, dttr], [xdr])
                        du, dur = dtaU[X]
                        V(lambda e, du=du, xo=xo, X=X: e.tensor_tensor(out=du, in0=self.tri[X].unsqueeze(1).to_broadcast([128, 8, 128]),
                                                                       in1=bc(dta[:, xo:xo + 8], 128), op=ALU.mult), [dtar, "cst"], [dur])
                    ptb, ptr_ = self.ps()
                    ptb3 = ptb[:].bitcast(BF16)[:, 0:256].rearrange("p (a t) -> p a t", a=2)
                    PE(lambda e: e.transpose(ptb3[:, 0, :], xa[:, 512:640], self.identb), [xar, "cstb"], [ptr_], inc=False)
                    PE(lambda e: e.transpose(ptb3[:, 1, :], xa[:, 640:768], self.identb), [xar, "cstb"], [ptr_])
                    V(lambda e: e.tensor_copy(out=BCT, in_=ptb3), [ptr_], [BCTr])
                    pcb, pcbr = self.ps()
                    for g in range(2):
                        V(lambda e, g=g: e.tensor_scalar(out=Cblk[:, g, :], in0=BCT[:, 1, :], scalar1=self.cst[:, 1856 + g:1857 + g], scalar2=None, op0=ALU.mult),
                          [BCTr, "cst"], [Cblkr])
                    PE(lambda e: e.matmul(pcb[:, 0:256], lhsT=BCT[:, 0, :], rhs=Cblk.rearrange("p a t -> p (a t)"), start=True, stop=True), [BCTr, Cblkr], [pcbr])
                    V(lambda e: e.tensor_copy(out=CBT, in_=pcb[:, 0:256].rearrange("p (a t) -> p a t", a=2)), [pcbr], [CBTr])
                    ei = 0
                    for X in "fb":
                        xo = 0 if X == "f" else 8
                        du, dur = dtaU[X]
                        for g in range(2):
                            pseg, psegr = self.ps()
                            pseg3 = pseg[:].rearrange("p (h t) -> p h t", h=4)
                            PE(lambda e, g=g, du=du, pseg3=pseg3: e.matmul(pseg3, lhsT=self.onesf, rhs=du[:, g * 4:(g + 1) * 4, :], start=True, stop=False),
                               [dur, "cst"], [psegr], inc=False)
                            PE(lambda e, g=g, X=X, xo=xo, pseg3=pseg3: e.matmul(pseg3, lhsT=self.tri[X], rhs=bc(ndta[:, xo + g * 4:xo + g * 4 + 4], 128),
                                                                               start=False, stop=False), [ndtar, "cst"], [psegr], inc=False)
                            PE(lambda e, X=X, pseg=pseg: e.matmul(pseg[:], lhsT=self.identb, rhs=self.negm[X], start=False, stop=True), ["cstb"], [psegr])
                            E_, Er_ = Et[ei % 2]
                            ei += 1
                            A(lambda e, E_=E_, pseg3=pseg3: e.activation(out=E_, in_=pseg3, func=AF.Exp), [psegr], [Er_])
                            M_, Mr_ = Mt[(X, g)]
                            V(lambda e, M_=M_, E_=E_, g=g: e.tensor_tensor(out=M_, in0=E_, in1=CBT[:, g, :].unsqueeze(1).to_broadcast([128, 4, 128]), op=ALU.mult),
                              [Er_, CBTr], [Mr_])
                    pyd, pydr = self.ps()
                    for h in range(8):
                        g, h4 = h // 4, h % 4
                        for X in "fb":
                            PE(lambda e, h=h, g=g, h4=h4, X=X: e.matmul(pyd[:, h * 64:(h + 1) * 64], lhsT=Mt[(X, g)][0][:, h4, :], rhs=xdt[X][0][:, h * 64:(h + 1) * 64],
                                                                        start=(X == "f"), stop=(X == "b")), [Mt[(X, g)][1], xdt[X][1]], [pydr], inc=(h == 7 and X == "b"))
                    for g in range(2):
                        V(lambda e, g=g: e.tensor_scalar(out=Sblk["f"][0][:, g, :], in0=Sst_f[:, c, :], scalar1=self.cst[:, 1856 + g:1857 + g], scalar2=None, op0=ALU.mult),
                          [(Sst_fr, c), "cst"], [Sblk["f"][1]])
                        V(lambda e, g=g: e.tensor_scalar(out=Sblk["b"][0][:, g, :], in0=Sssd, scalar1=self.cst[:, 1856 + g:1857 + g], scalar2=None, op0=ALU.mult),
                          [Sssdr, "cst"], [Sblk["b"][1]])
                    pyo = {}
                    for X in "fb":
                        p_, pr_ = self.ps()
                        pyo[X] = (p_, pr_)
                        PE(lambda e, p_=p_, X=X: e.matmul(p_[:], lhsT=BCT[:, 1, :], rhs=Sblk[X][0].rearrange("p a t -> p (a t)"), start=True, stop=True),
                           [BCTr, Sblk[X][1]], [pr_])
                    h8 = lambda ap: ap.rearrange("p (h d) -> p h d", h=8)
                    V(lambda e: e.tensor_tensor(out=h8(y1), in0=h8(pyo["f"][0][:]), in1=bc(eacs[:, 0:8], 64), op=ALU.mult), [pyo["f"][1], eacsr], [y1r])
                    V(lambda e: e.tensor_tensor(out=h8(y2), in0=h8(pyo["b"][0][:]), in1=bc(eacs[:, 8:16], 64), op=ALU.mult), [pyo["b"][1], eacsr], [y2r])
                    V(lambda e: e.tensor_tensor(out=y1, in0=y1, in1=y2, op=ALU.add), [y1r, y2r], [y1r])
                    V(lambda e: e.tensor_tensor(out=h8(y2), in0=h8(xa[:, 0:512]), in1=bc(rb[:, RB_D:RB_D + 8], 64), op=ALU.mult), [xar, "rb"], [y2r])
                    V(lambda e: e.tensor_tensor(out=y1, in0=y1, in1=y2, op=ALU.add), [y1r, y2r], [y1r])
                    V(lambda e: e.tensor_tensor(out=y1, in0=y1, in1=pyd[:], op=ALU.add), [y1r, pydr], [y1r])
                    V(lambda e: e.tensor_tensor(out=y1, in0=y1, in1=z_, op=ALU.mult), [y1r, zr_], [y1r])
                    for g in range(2):
                        A(lambda e, g=g: e.activation(out=junk[:, g * 256:(g + 1) * 256], in_=y1[:, g * 256:(g + 1) * 256], func=AF.Square,
                                                      accum_out=ss2[:, g:g + 1]), [y1r], [junkr, ss2r])
                    V(lambda e: e.tensor_scalar(out=ss2, in0=ss2, scalar1=1.0 / 256, scalar2=EPS, op0=ALU.mult, op1=ALU.add), [ss2r], [ss2r])
                    A(lambda e: e.activation(out=ss2, in_=ss2, func=AF.Sqrt), [ss2r], [ss2r])
                    V(lambda e: e.reciprocal(out=ss2, in_=ss2), [ss2r], [ss2r])
                    for g in range(2):
                        V(lambda e, g=g: e.scalar_tensor_tensor(out=ya[:, g * 256:(g + 1) * 256], in0=y1[:, g * 256:(g + 1) * 256], scalar=ss2[:, g:g + 1],
                                                                in1=rb[:, RB_SSDN + g * 256:RB_SSDN + (g + 1) * 256], op0=ALU.mult, op1=ALU.mult),
                          [y1r, ss2r, "rb"], [yar])
                    ptb, ptr_ = self.ps()
                    ptb3 = ptb[:].bitcast(BF16)[:, 0:512].rearrange("p (a t) -> p a t", a=4)
                    for a_ in range(4):
                        PE(lambda e, a_=a_, ptb3=ptb3: e.transpose(ptb3[:, a_, :], ya[:, a_ * 128:(a_ + 1) * 128], self.identb), [yar, "cstb"], [ptr_], inc=(a_ == 3))
                    b_, br_ = next_brs()
                    V(lambda e, b_=b_, ptb3=ptb3: e.tensor_copy(out=b_, in_=ptb3), [ptr_], [br_])
                    store_branch(0, b_, br_, tk)
                    ssd_state_update("b", c, False)

                def fB_gla():
                    gla_common(tk, "fb", i)
                    q_, qr_ = qT[i]
                    S.dma("sp", f"qT{i}", lambda e, q_=q_, tk=tk: e.dma_start(out=q_, in_=self.P["q"][:, :, tk:tk + 128].rearrange("h p t -> p h t")), writes=[qr_])
                    r_, rr_ = rT[i]
                    S.dma("sp", f"rT{i}", lambda e, r_=r_, tk=tk: e.dma_start(out=r_, in_=self.P["r"][:, :, tk:tk + 128].rearrange("h p t -> p h t")), writes=[rr_])
                    v_, vr_ = vtm[i]
                    V(lambda e, q_=q_: e.tensor_tensor(out=qdec.rearrange("p (x h) t -> p x h t", x=2), in0=eq.rearrange("p (x h) t -> p x h t", x=2),
                                                       in1=q_.unsqueeze(1).to_broadcast([128, 2, 2, 128]), op=ALU.mult), [eqr, qr_], [qdecr])
                    for hh in range(2):
                        V(lambda e, hh=hh: e.tensor_scalar(out=Qblk[:, :, hh, :], in0=qdec, scalar1=self.cst[:, 1856 + hh:1857 + hh], scalar2=None, op0=ALU.mult),
                          [qdecr, "cst"], [Qblkr])
                    for X in "fb":
                        xi = 0 if X == "f" else 1
                        pa, par = self.ps()
                        pa3 = pa[:].rearrange("p (h t) -> p h t", h=4)
                        for hp in range(2):
                            PE(lambda e, hp=hp, xi=xi, pa3=pa3: e.matmul(pa3[:, hp * 2:(hp + 1) * 2, :], lhsT=kinv[:, xi * 2 + hp, :], rhs=Qblk[:, xi * 2 + hp, :, :],
                                                                         start=True, stop=True), [kinvr, Qblkr], [par], inc=(hp == 1))
                        am, amr = attm[X]
                        V(lambda e, am=am, pa3=pa3, X=X: e.tensor_tensor(out=am, in0=pa3, in1=self.m01[X].unsqueeze(1).to_broadcast([128, 4, 128]), op=ALU.mult),
                          [par, "cstb"], [amr])
                    V(lambda e: e.tensor_copy(out=Gin_b, in_=Sgla), [Sglar], [Gin_br])
                    po, por = self.ps()
                    po3 = po[:].rearrange("p (h t) -> p h t", h=4)
                    for h in range(4):
                        hp, hh = h // 2, h % 2
                        rs = slice(hh * 64, (hh + 1) * 64)
                        PE(lambda e, h=h: e.matmul(po3[:, h, :], lhsT=v_[:, h * 128:(h + 1) * 128], rhs=attm["f"][0][:, h, :], start=True, stop=False),
                           [vr_, attm["f"][1]], [por], inc=False)
                        PE(lambda e, h=h: e.matmul(po3[:, h, :], lhsT=v_[:, h * 128:(h + 1) * 128], rhs=attm["b"][0][:, h, :], start=False, stop=False),
                           [vr_, attm["b"][1]], [por], inc=False)
                        PE(lambda e, h=h, hp=hp, hh=hh: e.matmul(po3[:, h, :], lhsT=Sgl_f[:, c, hp * 128:(hp + 1) * 128], rhs=Qblk[:, hp, hh, :], start=False, stop=False),
                           [(Sgl_fr, c), Qblkr], [por], inc=False)
                        PE(lambda e, h=h, hp=hp, hh=hh: e.matmul(po3[:, h, :], lhsT=Gin_b[:, hp, :], rhs=Qblk[:, 2 + hp, hh, :], start=False, stop=True),
                           [Gin_br, Qblkr], [por], inc=(h == 3))
                    A(lambda e: e.activation(out=osq, in_=po[:], func=AF.Square), [por], [osqr])
                    pss, pssr = self.ps()
                    PE(lambda e: e.matmul(pss[:], lhsT=self.onesb, rhs=osq, start=True, stop=True), [osqr, "cstb"], [pssr])
                    V(lambda e: e.tensor_scalar(out=orstd, in0=pss[:], scalar1=1.0 / 128, scalar2=EPS, op0=ALU.mult, op1=ALU.add), [pssr], [orstdr])
                    A(lambda e: e.activation(out=orstd, in_=orstd, func=AF.Sqrt), [orstdr], [orstdr])
                    V(lambda e: e.reciprocal(out=orstd, in_=orstd), [orstdr], [orstdr])
                    V(lambda e: e.tensor_tensor(out=o1, in0=po[:], in1=orstd, op=ALU.mult), [por, orstdr], [o1r])
                    V(lambda e, r_=r_: e.tensor_tensor(out=rg, in0=r_, in1=bc(pp[:, PP_GLAN:PP_GLAN + 4], 128), op=ALU.mult), [rr_, "pp"], [rgr])
                    b_, br_ = next_brs()
                    V(lambda e, b_=b_: e.tensor_tensor(out=b_, in0=o1.rearrange("p (h t) -> p h t", h=4), in1=rg, op=ALU.mult), [o1r, rgr], [br_])
                    store_branch(3, b_, br_, tk)
                    gla_state_update("b", c, i, False)

                def fB_sgu():
                    vv, vvr = vt[i]
                    S.dma("sp", f"vt{i}", lambda e, vv=vv, tk=tk: e.dma_start(out=vv, in_=self.P["v"][tk:tk + 128, :]), writes=[vvr])
                    uu, uur = uT[i]
                    S.dma("sp", f"uT{i}", lambda e, uu=uu, tk=tk: e.dma_start(out=uu, in_=self.P["u"][:, :, tk:tk + 128].rearrange("h p t -> p h t")), writes=[uur])
                    A(lambda e, vv=vv: e.activation(out=junk, in_=vv, func=AF.Square, accum_out=ss1[:, 0:1]), [vvr], [junkr, ss1r])
                    V(lambda e: e.tensor_scalar(out=ss1[:, 0:1], in0=ss1[:, 0:1], scalar1=1.0 / 512, scalar2=EPS, op0=ALU.mult, op1=ALU.add), [ss1r], [ss1r])
                    A(lambda e: e.activation(out=ss1[:, 0:1], in_=ss1[:, 0:1], func=AF.Sqrt), [ss1r], [ss1r])
                    V(lambda e: e.reciprocal(out=ss1[:, 0:1], in_=ss1[:, 0:1]), [ss1r], [ss1r])
                    V(lambda e, vv=vv: e.scalar_tensor_tensor(out=vf, in0=vv, scalar=ss1[:, 0:1], in1=rb[:, RB_SGUN:RB_SGUN + 512], op0=ALU.mult, op1=ALU.mult),
                      [vvr, ss1r, "rb"], [vfr])
                    psg, psgr = self.ps()
                    for g in range(4):
                        PE(lambda e, g=g, psg=psg: e.matmul(psg[:, g * 128:(g + 1) * 128], lhsT=vf[:, g * 128:(g + 1) * 128], rhs=self.sguw[:, g * 128:(g + 1) * 128],
                                                            start=True, stop=True), [vfr, "sguw"], [psgr], inc=(g == 3))
                    V(lambda e, psg=psg: e.tensor_tensor(out=sg1.rearrange("p h t -> p (h t)"), in0=psg[:], in1=rb[:, RB_SGUB:RB_SGUB + 512], op=ALU.add), [psgr, "rb"], [sg1r])
                    b_, br_ = next_brs()
                    V(lambda e, b_=b_, uu=uu: e.tensor_tensor(out=b_, in0=sg1, in1=uu, op=ALU.mult), [sg1r, uur], [br_])
                    store_branch(2, b_, br_, tk)

                def fB_pool():
                    pl, plr = ptl[i]
                    S.dma("sp", f"ptl{i}", lambda e, pl=pl, tp=tp: e.dma_start(out=pl, in_=self.P["p"][:, :, tp - 8:tp + 136].rearrange("g p t -> p g t")), writes=[plr])
                    V(lambda e, pl=pl: e.tensor_tensor(out=wsum[:, 0, :], in0=pl[:, 0, 7:135], in1=pl[:, 0, 8:136], op=ALU.add), [plr], [wsumr + "0"])
                    V(lambda e, pl=pl: e.tensor_tensor(out=a1[:, :, 1:144], in0=pl[:, 1:4, 0:143], in1=pl[:, 1:4, 1:144], op=ALU.add), [plr], [a1r])
                    V(lambda e: e.tensor_tensor(out=wsum[:, 1, :], in0=a1[:, 0, 7:135], in1=a1[:, 0, 9:137], op=ALU.add), [a1r], [wsumr + "1"])
                    V(lambda e: e.tensor_tensor(out=a2[:, :, 2:143], in0=a1[:, 1:3, 1:142], in1=a1[:, 1:3, 3:144], op=ALU.add), [a1r], [a2r])
                    V(lambda e: e.tensor_tensor(out=wsum[:, 2, :], in0=a2[:, 0, 6:134], in1=a2[:, 0, 10:138], op=ALU.add), [a2r], [wsumr + "2"])
                    V(lambda e: e.tensor_tensor(out=a3[:, 4:141], in0=a2[:, 1, 2:139], in1=a2[:, 1, 6:143], op=ALU.add), [a2r], [a3r])
                    V(lambda e: e.tensor_tensor(out=wsum[:, 3, :], in0=a3[:, 4:132], in1=a3[:, 12:140], op=ALU.add), [a3r], [wsumr + "3"])
                    wregs = [wsumr + str(g) for g in range(4)]
                    corr = self.cst[:, 1792:1856].rearrange("p (e g t) -> p e g t", e=2, g=4)
                    if c == 0:
                        V(lambda e: e.tensor_tensor(out=wsum[:, :, 0:8], in0=wsum[:, :, 0:8], in1=corr[:, 0], op=ALU.mult), wregs + ["cst"], wregs)
                    if c == nch - 1:
                        V(lambda e: e.tensor_tensor(out=wsum[:, :, 120:128], in0=wsum[:, :, 120:128], in1=corr[:, 1], op=ALU.mult), wregs + ["cst"], wregs)
                    for g, w_ in enumerate((2, 4, 8, 16)):
                        V(lambda e, g=g, w_=w_, pl=pl: e.scalar_tensor_tensor(out=pooled[:, g, :], in0=wsum[:, g, :], scalar=1.0 / w_, in1=pl[:, g, 8:136],
                                                                             op0=ALU.mult, op1=ALU.subtract), [wsumr + str(g), plr], [pooledr + str(g)])
                    ppl, pplr = self.ps()
                    for g in range(4):
                        PE(lambda e, g=g, ppl=ppl: e.matmul(ppl[:, g * 128:(g + 1) * 128], lhsT=self.poolw[:, g * 128:(g + 1) * 128], rhs=pooled[:, g, :],
                                                            start=True, stop=True), [pooledr + str(g), "poolw"], [pplr], inc=(g == 3))
                    b_, br_ = next_brs()
                    for g in range(4):
                        V(lambda e, g=g, b_=b_, ppl=ppl: e.tensor_scalar(out=b_[:, g, :], in0=ppl[:, g * 128:(g + 1) * 128], scalar1=pp[:, PP_PSC + g:PP_PSC + g + 1],
                                                                         scalar2=None, op0=ALU.mult), [pplr, "pp"], [br_])
                    S.dma("pool", "brs" + br_, lambda e, b_=b_, tk=tk: e.dma_start(
                        out=self.BR[1, :, :, tk:tk + 128].rearrange("w p t -> p w t"), in_=b_), reads=[br_], writes=[("BR", 1, tk)])
                interleave(S, [fB_ssd, fB_gla, fB_sgu, fB_pool])

    def phase3(self, l):
        S = self.S
        self.arena_reset()
        TT = 512
        pp = self.pp
        xT = [self.arena(f"xT{i}", [8, TT]) for i in range(2)]
        a16, a16r = self.arena("a16", [8, TT], BF16)
        mo, mor = self.arena("mo", [8, TT])
        sq, sqr = self.arena("sq", [8, TT], BF16)
        rstd, rstdr = self.arena("rstd", [TT])
        hid, hidr = self.arena("hid", [32, TT], BF16)
        wb = [self.arena(f"wb{i}", [8192], BF16) for i in range(3)]
        brt = [self.arena(f"brt{i}", [4, TT], BF16) for i in range(2)]
        gt = [self.arena(f"gt{i}", [TT], BF16) for i in range(4)]
        tmp = [self.arena(f"tmp{i}", [TT]) for i in range(2)]
        rl = [self.arena(f"rl{i}", [TT], BF16) for i in range(2)]
        wc = [0]
        gc = [0]
        tc = [0]

        def wload(key, j, n):
            w, wr = wb[wc[0] % 3]
            slot = f"wb{wc[0] % 3}"
            wc[0] += 1
            S.dma("sp", slot, lambda e: e.dma_start(out=w[:, 0:n], in_=self.wsc[(l, key)][j]), writes=[wr])
            return w, wr

        def post_norm_residual(xt, xr, gcol):
            self.rstd_from_sq(sq, sqr, 8, rstd, rstdr, D)
            for ec in range(8):
                t_, tr_ = tmp[tc[0] % 2]
                tc[0] += 1
                S.op("dve", lambda e, ec=ec, t_=t_: e.scalar_tensor_tensor(
                    out=t_, in0=mo[:, ec, :], scalar=pp[:, gcol + ec:gcol + ec + 1], in1=rstd, op0=ALU.mult, op1=ALU.mult),
                    reads=[(mor, ec), rstdr, "pp"], writes=[tr_])
                S.op("dve", lambda e, ec=ec, t_=t_: e.tensor_tensor(out=xt[:, ec, :], in0=xt[:, ec, :], in1=t_, op=ALU.add),
                     reads=[tr_, (xr, ec)], writes=[(xr, ec)])

        for ti in range(self.T // TT):
            t0 = ti * TT
            xt, xr = xT[ti % 2]
            S.dma("sp", f"xT{ti % 2}", lambda e, xt=xt, t0=t0: e.dma_start(
                out=xt, in_=self.XT[:, :, t0:t0 + TT].rearrange("k p t -> p k t")),
                writes=[(xr, k) for k in range(8)])
            for n in range(4):
                w, wr = wload("br", n, 4096)
                wv = w[:, 0:4096].rearrange("p (k n) -> p k n", k=4)
                bt, btr = brt[n % 2]
                S.dma("sp", f"brt{n % 2}", lambda e, bt=bt, n=n, t0=t0: e.dma_start(
                    out=bt, in_=self.BR[n, :, :, t0:t0 + TT].rearrange("w p t -> p w t")), writes=[btr])
                for dc in range(8):
                    g_, gr_ = gt[gc[0] % 4]
                    gslot = f"gt{gc[0] % 4}"
                    gc[0] += 1
                    S.dma("sp", gslot, lambda e, g_=g_, n=n, dc=dc, t0=t0: e.dma_start(
                        out=g_, in_=self.P["gate"][n * 8 + dc, :, t0:t0 + TT]), writes=[gr_])
                    pt, pr = self.ps()
                    for k in range(4):
                        S.op("pe", lambda e, pt=pt, wv=wv, k=k, dc=dc, bt=bt: e.matmul(
                            pt[:], lhsT=wv[:, k, dc * 128:(dc + 1) * 128], rhs=bt[:, k, :], start=(k == 0), stop=(k == 3)),
                            reads=[wr, btr], writes=[pr], inc=(k == 3))
                    if n == 0:
                        S.op("dve", lambda e, pt=pt, g_=g_, dc=dc: e.tensor_tensor(out=mo[:, dc, :], in0=pt[:], in1=g_, op=ALU.mult),
                             reads=[pr, gr_], writes=[(mor, dc)])
                    else:
                        t_, tr_ = tmp[tc[0] % 2]
                        tc[0] += 1
                        S.op("dve", lambda e, pt=pt, g_=g_, t_=t_: e.tensor_tensor(out=t_, in0=pt[:], in1=g_, op=ALU.mult),
                             reads=[pr, gr_], writes=[tr_])
                        if n < 3:
                            S.op("dve", lambda e, t_=t_, dc=dc: e.tensor_tensor(out=mo[:, dc, :], in0=mo[:, dc, :], in1=t_, op=ALU.add),
                                 reads=[tr_, (mor, dc)], writes=[(mor, dc)])
                        else:
                            S.op("dve", lambda e, t_=t_, dc=dc: e.tensor_tensor(out=a16[:, dc, :], in0=mo[:, dc, :], in1=t_, op=ALU.add),
                                 reads=[tr_, (mor, dc)], writes=[(a16r, dc)])
            areads = [(a16r, k) for k in range(8)]
            for j in range(2):
                w, wr = wload("out", j, 4096)
                wv = w[:, 0:4096].rearrange("p (k n) -> p k n", k=8)
                for cc in range(4):
                    ec = j * 4 + cc
                    pt, pr = self.ps()
                    for k in range(8):
                        S.op("pe", lambda e, pt=pt, wv=wv, k=k, cc=cc: e.matmul(
                            pt[:], lhsT=wv[:, k, cc * 128:(cc + 1) * 128], rhs=a16[:, k, :], start=(k == 0), stop=(k == 7)),
                            reads=[wr] + areads, writes=[pr], inc=(k == 7))
                    S.op("dve", lambda e, pt=pt, ec=ec: e.tensor_copy(out=mo[:, ec, :], in_=pt[:]), reads=[pr], writes=[(mor, ec)])
                    S.op("act", lambda e, ec=ec: e.activation(out=sq[:, ec, :], in_=mo[:, ec, :], func=AF.Square),
                         reads=[(mor, ec)], writes=[(sqr, ec)])
            post_norm_residual(xt, xr, PP_GPOST)
            for kc in range(8):
                S.op("act", lambda e, kc=kc, xt=xt: e.activation(out=sq[:, kc, :], in_=xt[:, kc, :], func=AF.Square),
                     reads=[(xr, kc)], writes=[(sqr, kc)])
            self.rstd_from_sq(sq, sqr, 8, rstd, rstdr, D)
            for kc in range(8):
                S.op("dve", lambda e, kc=kc, xt=xt: e.scalar_tensor_tensor(
                    out=a16[:, kc, :], in0=xt[:, kc, :], scalar=pp[:, PP_GF1 + kc:PP_GF1 + kc + 1], in1=rstd,
                    op0=ALU.mult, op1=ALU.mult), reads=[(xr, kc), rstdr, "pp"], writes=[(a16r, kc)])
            for j in range(8):
                w, wr = wload("ff1", j, 4096)
                wv = w[:, 0:4096].rearrange("p (k n) -> p k n", k=8)
                for cc in range(4):
                    fc = j * 4 + cc
                    pt, pr = self.ps()
                    for k in range(8):
                        S.op("pe", lambda e, pt=pt, wv=wv, k=k, cc=cc: e.matmul(
                            pt[:], lhsT=wv[:, k, cc * 128:(cc + 1) * 128], rhs=a16[:, k, :], start=(k == 0), stop=(k == 7)),
                            reads=[wr] + areads, writes=[pr], inc=(k == 7))
                    r_, rr_ = rl[fc % 2]
                    S.op("act", lambda e, pt=pt, r_=r_: e.activation(out=r_, in_=pt[:], func=AF.Relu), reads=[pr], writes=[rr_])
                    S.op("dve", lambda e, r_=r_, fc=fc: e.tensor_tensor(out=hid[:, fc, :], in0=r_, in1=r_, op=ALU.mult),
                         reads=[rr_], writes=[(hidr, fc)])
            hreads = [(hidr, f) for f in range(32)]
            for j in range(4):
                w, wr = wload("ff2", j, 8192)
                wv = w[:, 0:8192].rearrange("p (k n) -> p k n", k=32)
                for cc in range(2):
                    ec = j * 2 + cc
                    pt, pr = self.ps()
                    for k in range(32):
                        S.op("pe", lambda e, pt=pt, wv=wv, k=k, cc=cc: e.matmul(
                            pt[:], lhsT=wv[:, k, cc * 128:(cc + 1) * 128], rhs=hid[:, k, :], start=(k == 0), stop=(k == 31)),
                            reads=[wr] + hreads, writes=[pr], inc=(k == 31))
                    S.op("dve", lambda e, pt=pt, ec=ec: e.tensor_copy(out=mo[:, ec, :], in_=pt[:]), reads=[pr], writes=[(mor, ec)])
                    S.op("act", lambda e, ec=ec: e.activation(out=sq[:, ec, :], in_=mo[:, ec, :], func=AF.Square),
                         reads=[(mor, ec)], writes=[(sqr, ec)])
            post_norm_residual(xt, xr, PP_GF2)
            S.dma("pool", f"xTs{ti % 2}", lambda e, xt=xt, t0=t0: e.dma_start(
                out=self.XT[:, :, t0:t0 + TT].rearrange("k p t -> p k t"), in_=xt),
                reads=[(xr, k) for k in range(8)], writes=[("XT", ti)])

    def phase_final(self):
        S = self.S
        self.arena_reset()
        xin = [self.arena(f"fx{i}", [8, 512]) for i in range(2)]
        xo = [self.arena(f"fo{i}", [4, D]) for i in range(2)]
        for ti in range(self.T // 512):
            xi, xir = xin[ti % 2]
            xot, xor_ = xo[ti % 2]
            t0 = ti * 512
            S.dma("sp", f"fx{ti % 2}", lambda e, xi=xi, t0=t0: e.dma_start(
                out=xi, in_=self.XT[:, :, t0:t0 + 512].rearrange("k p t -> p k t")), reads=[("XT", ti)], writes=[xir])
            for j in range(4):
                for half in range(2):
                    pt, pr = self.ps()
                    for q in range(4):
                        kc = half * 4 + q
                        S.op("pe", lambda e, pt=pt, xi=xi, j=j, kc=kc, q=q: e.transpose(
                            pt[:, q * 128:(q + 1) * 128], xi[:, kc, j * 128:(j + 1) * 128], self.identf),
                            reads=[xir, "cst"], writes=[pr], inc=(q == 3))
                    if half == 0:
                        S.op("dve", lambda e, pt=pt, xot=xot, j=j: e.tensor_copy(out=xot[:, j, 0:512], in_=pt[:]),
                             reads=[pr], writes=[(xor_, j, 0)])
                    else:
                        S.op("act", lambda e, pt=pt, xot=xot, j=j: e.copy(out=xot[:, j, 512:1024], in_=pt[:]),
                             reads=[pr], writes=[(xor_, j, 1)])
            S.dma("pool", f"fo{ti % 2}", lambda e, xot=xot, t0=t0: e.dma_start(
                out=self.y_out[t0:t0 + 512, :].rearrange("(j p) d -> p j d", p=128), in_=xot),
                reads=[(xor_, j, h) for j in range(4) for h in range(2)], writes=[("Y", ti)])


def make_consts():
    r = np.arange(128)
    ident = np.eye(128, dtype=np.float32)
    ones = np.ones((128, 128), np.float32)
    U = (r[:, None] <= r[None, :]).astype(np.float32)
    Lm = (r[:, None] >= r[None, :]).astype(np.float32)
    Uneg = U * (-1.0 / 16.0)
    Lneg = Lm * (-1.0 / 16.0)
    m01f, m01b = U.copy(), Lm.copy()
    negf = np.tile((1.0 - U) * -30000.0, (1, 4)).astype(np.float32)
    negb = np.tile((1.0 - Lm) * -30000.0, (1, 4)).astype(np.float32)
    corr = np.ones((2, 4, 8), np.float32)
    for g, w in enumerate((2, 4, 8, 16)):
        for t in range(8):
            if t < w // 2:
                corr[0, g, t] = w / (t + w // 2)
            tq = 7 - t
            if tq < w // 2 - 1:
                corr[1, g, t] = w / (tq + 1 + w // 2)
    corr = np.broadcast_to(corr.reshape(1, 64), (128, 64))
    half = np.stack([(r < 64), (r >= 64)], axis=1).astype(np.float32)
    cst = np.concatenate([ident, ones, U, Lm, Uneg, Lneg, negf, negb, corr, half], axis=1)
    pad = np.zeros((128, 1860 - cst.shape[1]), np.float32)
    return np.ascontiguousarray(np.concatenate([cst, pad], axis=1))


def pack_params(inputs, depth):
    L = depth
    f = lambda k: np.asarray(inputs[k], np.float32)
    pp = np.zeros((L, 128, PP_LEN), np.float32)
    for l in range(L):
        pp[l, :, PP_G1:PP_G1 + 8] = f("norm_mix_pre")[l].reshape(8, 128).T
        pp[l, :, PP_GPOST:PP_GPOST + 8] = f("norm_mix_post")[l].reshape(8, 128).T
        pp[l, :, PP_GF1:PP_GF1 + 8] = f("norm_ffn_pre")[l].reshape(8, 128).T
        pp[l, :, PP_GF2:PP_GF2 + 8] = f("norm_ffn_post")[l].reshape(8, 128).T
        pp[l, :, PP_PSC:PP_PSC + 4] = f("pool_scale")[l].reshape(4, 128).T
        pp[l, :, PP_GLAN:PP_GLAN + 4] = f("gla_norm")[l].reshape(4, 128).T
    rb = np.zeros((L, RB_LEN), np.float32)
    for l in range(L):
        rb[l, RB_CONVW:RB_CONVW + 3840] = f("ssd_conv_w")[l].reshape(-1)
        rb[l, RB_CONVB:RB_CONVB + 768] = f("ssd_conv_b")[l]
        rb[l, RB_DTB:RB_DTB + 16] = f("ssd_dt_bias")[l].reshape(-1)
        rb[l, RB_ALOG:RB_ALOG + 16] = f("ssd_a_log")[l].reshape(-1)
        rb[l, RB_D:RB_D + 8] = f("ssd_d")[l]
        rb[l, RB_SSDN:RB_SSDN + 512] = f("ssd_norm")[l]
        rb[l, RB_SGUN:RB_SGUN + 512] = f("sgu_norm")[l]
        rb[l, RB_SGUB:RB_SGUB + 512] = f("sgu_b")[l].reshape(-1)
    sguwT = np.ascontiguousarray(f("sgu_w")[:L].transpose(0, 3, 1, 2).reshape(L, 128, 512))
    poolw = np.ascontiguousarray(f("pool_w")[:L].transpose(0, 2, 1, 3).reshape(L, 128, 512))
    w2 = np.zeros((L, 33, 512), np.float32)
    w2[:, 0:16, 0:256] = f("gla_gate_w2")[:L, 0]
    w2[:, 16:32, 256:512] = f("gla_gate_w2")[:L, 1]
    w2[:, 32, :] = f("gla_gate_b")[:L].reshape(L, 512)
    return pp, rb, sguwT, poolw, w2


_CACHE = {}


def run(inputs, seqs_per_core, x_cores, depth, debug=()):
    key = (tuple(seqs_per_core), depth, tuple(debug))
    if key not in _CACHE:
        _CACHE[key] = Builder(seqs_per_core, depth, debug).build()
    nc = _CACHE[key]
    pp, rb, sguwT, poolw, w2 = pack_params(inputs, depth)
    f = lambda k: np.ascontiguousarray(np.asarray(inputs[k], np.float32)[:depth])
    shared = {"w_in": f("w_in"), "w_branch": f("w_branch"), "w_out": f("w_out"), "w_ff1": f("w_ff1"), "w_ff2": f("w_ff2"),
              "pp": pp, "rb": rb, "sguwT": sguwT, "poolw": poolw, "w2blk": w2, "cst": make_consts()}
    in_maps = [dict(shared, x=np.ascontiguousarray(xc)) for xc in x_cores]
    res = run_bass_kernel_spmd(nc, in_maps, core_ids=list(range(len(x_cores))))
    return res.results


def kernel(**inputs):
    xp = np.asarray(inputs["x_prompt"], np.float32)
    xs = np.asarray(inputs["x_sample"], np.float32)
    x_cores = []
    for c in range(NCORES):
        x_cores.append(np.concatenate([xp[2 * c], xp[2 * c + 1], xs[c]], axis=0))
    res = run(inputs, [2048, 2048, 8192], x_cores, DEPTH)
    yp = np.zeros_like(xp)
    ys = np.zeros_like(xs)
    for c in range(NCORES):
        y = res[c]["y"]
        yp[2 * c] = y[0:2048]
        yp[2 * c + 1] = y[2048:4096]
        ys[c] = y[4096:]
    return (yp, ys)
```

```python
import numpy as np
from contextlib import ExitStack
import concourse.bass as bass
import concourse.mybir as mybir
from concourse.bass_utils import run_bass_kernel_spmd

F32 = mybir.dt.float32
BF16 = mybir.dt.bfloat16
AF = mybir.ActivationFunctionType
ALU = mybir.AluOpType

D = 1024
DEPTH = 4
PAD = 8
EPS = 1e-6
NCORES = 8


class _Rec:
    def __init__(self):
        self.call = None

    def __getattr__(self, name):
        def f(*a, **k):
            self.call = (name, a, k)
            return self
        return f


def _record(fn):
    r = _Rec()
    fn(r)
    return r.call


class Sched:
    SEM_MAX = 60000

    def __init__(self, nc, sems):
        self.nc = nc
        self.free_sems = list(sems)
        self.eng_names = ["pe", "act", "dve", "pool", "sp"]
        self.prog = {e: [] for e in self.eng_names}
        self.esem = {}
        self.ecnt = {}
        for e in self.eng_names:
            self.esem[e] = self.free_sems.pop()
            self.ecnt[e] = 0
        self.seen = {e: {} for e in self.eng_names}
        self.last_w = {}
        self.readers = {}
        self.dma_sem = {}
        self.all_sems = {}
        self.n_instr = 0
        self.pend_r = {e: [] for e in self.eng_names}
        self.pend_w = {e: [] for e in self.eng_names}
        self.yield_hook = None

    def _new_sem(self):
        return self.free_sems.pop()

    def _emit_wait(self, eng, tok):
        if tok is None:
            return
        sem, val, src = tok
        if src == eng and eng == "pe":
            return
        k = id(sem)
        if self.seen[eng].get(k, 0) >= val:
            return
        self.seen[eng][k] = val
        self.prog[eng].append(("wait", sem, val))

    def _deps(self, eng, reads, writes):
        for r in reads:
            self._emit_wait(eng, self.last_w.get(r))
        for w in writes:
            self._emit_wait(eng, self.last_w.get(w))
            for t in self.readers.get(w, ()):
                self._emit_wait(eng, t)

    def _commit(self, tok, reads, writes):
        for r in reads:
            self.readers.setdefault(r, []).append(tok)
        for w in writes:
            self.last_w[w] = tok
            self.readers[w] = []

    def op(self, eng, fn, reads=(), writes=(), inc=True):
        self._deps(eng, reads, writes)
        if not inc:
            self.pend_r[eng].extend(reads)
            self.pend_w[eng].extend(writes)
            self.prog[eng].append(("op", _record(fn), None, 0))
            self.n_instr += 1
            return None
        if self.ecnt[eng] >= self.SEM_MAX:
            self.esem[eng] = self._new_sem()
            self.ecnt[eng] = 0
        self.ecnt[eng] += 1
        tok = (self.esem[eng], self.ecnt[eng], eng)
        self.prog[eng].append(("op", _record(fn), self.esem[eng], 1))
        self.all_sems[id(tok[0])] = (tok[0], tok[1])
        reads = list(reads) + self.pend_r[eng]
        writes = list(writes) + self.pend_w[eng]
        self.pend_r[eng] = []
        self.pend_w[eng] = []
        self._commit(tok, reads, writes)
        self.n_instr += 1
        if self.yield_hook is not None:
            self.yield_hook()
        return tok

    def dma(self, eng, slot, fn, reads=(), writes=()):
        self._deps(eng, reads, writes)
        if slot not in self.dma_sem:
            self.dma_sem[slot] = [self._new_sem(), 0]
        st = self.dma_sem[slot]
        if st[1] + 16 > self.SEM_MAX:
            st[0] = self._new_sem()
            st[1] = 0
        st[1] += 16
        tok = (st[0], st[1], "dma")
        self.all_sems[id(st[0])] = (st[0], st[1])
        self.prog[eng].append(("op", _record(fn), st[0], 16))
        self._commit(tok, reads, writes)
        self.n_instr += 1
        return tok

    def barrier(self, engs=None):
        for e in (engs or self.eng_names):
            for sem, val in list(self.all_sems.values()):
                if val > 0:
                    self._emit_wait(e, (sem, val, "any"))

    def emit(self):
        nc = self.nc
        objs = {"pe": "tensor", "act": "scalar", "dve": "vector", "pool": "gpsimd", "sp": "sync"}
        with nc.Block() as block:
            for e in self.eng_names:
                prog = self.prog[e]

                def body(eobj, prog=prog):
                    for it in prog:
                        if it[0] == "wait":
                            eobj.wait_ge(it[1], it[2])
                        else:
                            name, a, k = it[1]
                            ins = getattr(eobj, name)(*a, **k)
                            if it[2] is not None:
                                ins.then_inc(it[2], it[3])
                getattr(block, objs[e])(body)


def interleave(S, fns):
    import threading
    n = len(fns)
    if n == 1:
        fns[0]()
        return
    sems = [threading.Semaphore(0) for _ in range(n)]
    fin = threading.Semaphore(0)
    done = [False] * n
    errs = []
    idx = {}

    def nxt(i):
        for k in range(1, n + 1):
            j = (i + k) % n
            if not done[j]:
                return j
        return None

    def worker(i):
        sems[i].acquire()
        idx[threading.get_ident()] = i
        try:
            fns[i]()
        except BaseException as ex:
            errs.append(ex)
        finally:
            done[i] = True
            j = nxt(i)
            if j is None:
                fin.release()
            else:
                sems[j].release()

    def hook():
        i = idx.get(threading.get_ident())
        if i is None:
            return
        j = nxt(i)
        if j is not None and j != i:
            sems[j].release()
            sems[i].acquire()

    ths = [threading.Thread(target=worker, args=(i,)) for i in range(n)]
    for t in ths:
        t.start()
    S.yield_hook = hook
    sems[0].release()
    fin.acquire()
    S.yield_hook = None
    for t in ths:
        t.join()
    if errs:
        raise errs[0]


WIN_GROUPS = [
    ("z", 0, 512, "tm", 512),
    ("xbc", 512, 768, "tm", 384),
    ("dt", 1280, 16, "tm", 16),
    ("p", 1296, 512, "fm", 512),
    ("u", 1808, 512, "fm", 512),
    ("v", 2320, 512, "tm", 512),
    ("q", 2832, 256, "fm", 256),
    ("k", 3088, 256, "fm", 256),
    ("gv", 3344, 512, "tm", 512),
    ("r", 3856, 512, "fm", 512),
    ("glr", 4368, 32, "fm", 32),
    ("gate", 4400, 4096, "fm", 512),
]
RB_CONVW = 0
RB_CONVB = RB_CONVW + 5 * 768
RB_DTB = RB_CONVB + 768
RB_ALOG = RB_DTB + 16
RB_D = RB_ALOG + 16
RB_SSDN = RB_D + 8
RB_SGUN = RB_SSDN + 512
RB_SGUB = RB_SGUN + 512
RB_LEN = RB_SGUB + 512
PP_G1, PP_GPOST, PP_GF1, PP_GF2, PP_PSC, PP_GLAN = 0, 8, 16, 24, 32, 36
PP_LEN = 40


class Builder:
    def __init__(self, seqs, depth, debug=(), TT=512):
        self.seqs = list(seqs)
        self.depth = depth
        self.TT = TT
        self.T = sum(seqs)
        self.soff = [int(x) for x in np.cumsum([0] + self.seqs[:-1])]
        self.poff = [int(x) for x in (np.cumsum([0] + [L + 2 * PAD for L in self.seqs[:-1]]) + PAD)]
        self.Tp = sum(L + 2 * PAD for L in self.seqs)
        self.debug = set(debug)
        self.nc = bass.Bass("TRN2", target_bir_lowering=False)
        self.es = ExitStack()
        self.psn = 0

    def dram(self, name, shape, dt, kind="Internal"):
        if name in self.debug:
            kind = "ExternalOutput"
        return self.nc.dram_tensor(name, list(shape), dt, kind=kind).ap()

    def sb(self, name, shape, dt=F32):
        return self.es.enter_context(self.nc.sbuf_tensor("sb_" + name, list(shape), dt))

    def arena_reset(self):
        self.S.barrier()
        self.aoff = 0
        self.aphase += 1

    def arena(self, name, free_shape, dt=F32):
        n = int(np.prod(free_shape))
        words = n if dt == F32 else (n + 1) // 2
        a = self.aoff
        self.aoff += words
        assert self.aoff <= self.AWORDS, (name, self.aoff)
        v = self.arena_t[:, a:a + words]
        if dt != F32:
            v = v.bitcast(dt)[:, 0:n]
        if len(free_shape) == 2:
            v = v.rearrange("p (a b) -> p a b", a=free_shape[0])
        elif len(free_shape) == 3:
            v = v.rearrange("p (a b c) -> p a b c", a=free_shape[0], b=free_shape[1])
        return v, f"A{self.aphase}:{name}"

    def ps(self):
        i = self.psn % 8
        self.psn += 1
        return self.psum[i], f"ps{i}"

    def build(self):
        nc, es = self.nc, self.es
        T, Tp, L = self.T, self.Tp, self.depth
        dr = self.dram
        inp = lambda name, shape: nc.dram_tensor(name, list(shape), F32, kind="ExternalInput").ap()
        self.x_in = inp("x", [T, D])
        self.y_out = nc.dram_tensor("y", [T, D], F32, kind="ExternalOutput").ap()
        self.w_in = inp("w_in", [L, D, 8496])
        self.w_branch = inp("w_branch", [L, 4, 512, D])
        self.w_out = inp("w_out", [L, D, D])
        self.w_ff1 = inp("w_ff1", [L, D, 4096])
        self.w_ff2 = inp("w_ff2", [L, 4096, D])
        self.pp_in = inp("pp", [L, 128, PP_LEN])
        self.rb_in = inp("rb", [L, RB_LEN])
        self.sguw_in = inp("sguwT", [L, 128, 512])
        self.poolw_in = inp("poolw", [L, 128, 512])
        self.w2_in = inp("w2blk", [L, 33, 512])
        self.cst_in = inp("cst", [128, 1860])
        self.XT = dr("XT", [8, 128, T], F32)
        self.wsc = {}
        for l in range(L):
            for (g, off, n, ori, nb) in WIN_GROUPS:
                self.wsc[(l, g)] = dr(f"w{l}_{g}", [n // nb, 128, 8 * nb], BF16)
            self.wsc[(l, "br")] = dr(f"w{l}_br", [4, 128, 4 * 1024], BF16)
            self.wsc[(l, "out")] = dr(f"w{l}_out", [2, 128, 8 * 512], BF16)
            self.wsc[(l, "ff1")] = dr(f"w{l}_ff1", [8, 128, 8 * 512], BF16)
            self.wsc[(l, "ff2")] = dr(f"w{l}_ff2", [4, 128, 32 * 256], BF16)
        self.P = {}
        self.P["z"] = dr("P_z", [T, 512], BF16)
        self.P["xbc"] = dr("P_xbc", [Tp, 768], BF16)
        self.P["dt"] = dr("P_dt", [T, 16], F32)
        self.P["p"] = dr("P_p", [4, 128, Tp], BF16)
        self.P["u"] = dr("P_u", [4, 128, T], BF16)
        self.P["v"] = dr("P_v", [T, 512], BF16)
        self.P["q"] = dr("P_q", [2, 128, T], BF16)
        self.P["k"] = dr("P_k", [2, 128, T], BF16)
        self.P["gv"] = dr("P_gv", [T, 512], BF16)
        self.P["r"] = dr("P_r", [4, 128, T], BF16)
        self.P["glr"] = dr("P_glr", [32, T], BF16)
        self.P["gate"] = dr("P_gate", [32, 128, T], BF16)
        self.BR = dr("BR", [4, 4, 128, T], BF16)

        sems = [es.enter_context(nc.semaphore(f"s{i}")) for i in range(100)]
        self.S = Sched(nc, sems)
        self.psum = [es.enter_context(nc.psum_tensor(f"psb{i}", [128, 512], F32)) for i in range(8)]
        self.cst = self.sb("cst", [128, 1860])
        self.cstb = self.sb("cstb", [128, 4 * 128 + 2 * 512], BF16)
        self.AWORDS = 42000
        self.arena_t = self.sb("arena", [128, self.AWORDS], F32)
        self.aoff = 0
        self.aphase = 0
        self.pp = self.sb("pp", [128, PP_LEN])
        self.rb = self.sb("rbt", [128, RB_LEN])
        self.sguw = self.sb("sguw", [128, 512], BF16)
        self.poolw = self.sb("poolw", [128, 512], BF16)
        self.w2b = self.sb("w2b", [33, 512], BF16)
        self.anegt = self.sb("anegt", [128, 16])
        self.zero_t = self.sb("zero_t", [128, 768], BF16)

        S = self.S
        cst = self.cst
        S.dma("sp", "cst", lambda e: e.dma_start(out=cst[:], in_=self.cst_in[:, :]), writes=["cst"])
        self.identf = cst[:, 0:128]
        self.onesf = cst[:, 128:256]
        self.Uf = cst[:, 256:384]
        self.Lf = cst[:, 384:512]
        self.Uneg = cst[:, 512:640]
        self.Lneg = cst[:, 640:768]
        cb = self.cstb
        S.op("dve", lambda e: e.tensor_copy(out=cb[:, 0:128], in_=cst[:, 0:128]), reads=["cst"], writes=["cstb"])
        S.op("dve", lambda e: e.tensor_copy(out=cb[:, 128:256], in_=cst[:, 128:256]), reads=["cst"], writes=["cstb"])
        S.op("dve", lambda e: e.tensor_copy(out=cb[:, 256:512], in_=cst[:, 256:512]), reads=["cst"], writes=["cstb"])
        S.op("dve", lambda e: e.tensor_copy(out=cb[:, 512:1536], in_=cst[:, 768:1792]), reads=["cst"], writes=["cstb"])
        self.identb = cb[:, 0:128]
        self.onesb = cb[:, 128:256]
        self.m01 = {"f": cb[:, 256:384], "b": cb[:, 384:512]}
        self.negm = {"f": cb[:, 512:1024], "b": cb[:, 1024:1536]}
        self.tri = {"f": self.Uf, "b": self.Lf}
        self.trineg = {"f": self.Uneg, "b": self.Lneg}
        zt = self.zero_t
        S.op("dve", lambda e: e.memset(zt[:], 0.0), writes=["zero_t"])
        self.zero_pads()
        self.phase0_x()
        for l in range(L):
            self.prep_weights(l)
        for l in range(L):
            self.layer(l)
        self.phase_final()
        S.barrier(["sp"])
        S.emit()
        return nc

    def zero_pads(self):
        S, zt = self.S, self.zero_t
        for s, Ls in enumerate(self.seqs):
            for (r0) in (self.poff[s] - PAD, self.poff[s] + Ls):
                S.dma("pool", "zp", lambda e, r0=r0: e.dma_start(out=self.P["xbc"][r0:r0 + PAD, :], in_=zt[0:PAD, :]),
                      reads=["zero_t"], writes=[("Pxbc_pad", r0)])
                S.dma("pool", "zp", lambda e, r0=r0: e.dma_start(
                    out=self.P["p"][:, :, r0:r0 + PAD].rearrange("g p t -> p g t"),
                    in_=zt[:, 0:4 * PAD].rearrange("p (g t) -> p g t", g=4)),
                    reads=["zero_t"], writes=[("Pp_pad", r0)])

    def phase0_x(self):
        S = self.S
        self.arena_reset()
        xin = [self.arena(f"xin{i}", [4, D]) for i in range(2)]
        xo = [self.arena(f"xo{i}", [8, 512]) for i in range(2)]
        for ti in range(self.T // 512):
            xi, xir = xin[ti % 2]
            xot, xor_ = xo[ti % 2]
            t0 = ti * 512
            S.dma("sp", f"xin{ti % 2}", lambda e, xi=xi, t0=t0: e.dma_start(
                out=xi, in_=self.x_in[t0:t0 + 512, :].rearrange("(j p) d -> p j d", p=128)), writes=[xir])
            for kc in range(8):
                pt, pr = self.ps()
                for j in range(4):
                    S.op("pe", lambda e, pt=pt, xi=xi, j=j, kc=kc: e.transpose(
                        pt[:, j * 128:(j + 1) * 128], xi[:, j, kc * 128:(kc + 1) * 128], self.identf),
                        reads=[xir, "cst"], writes=[pr], inc=(j == 3))
                eng = "dve" if kc % 2 == 0 else "act"
                if eng == "dve":
                    S.op("dve", lambda e, pt=pt, xot=xot, kc=kc: e.tensor_copy(out=xot[:, kc, :], in_=pt[:]),
                         reads=[pr], writes=[(xor_, kc)])
                else:
                    S.op("act", lambda e, pt=pt, xot=xot, kc=kc: e.copy(out=xot[:, kc, :], in_=pt[:]),
                         reads=[pr], writes=[(xor_, kc)])
            S.dma("pool", f"xo{ti % 2}", lambda e, xot=xot, t0=t0: e.dma_start(
                out=self.XT[:, :, t0:t0 + 512].rearrange("k p t -> p k t"), in_=xot),
                reads=[(xor_, kc) for kc in range(8)], writes=[("XT", ti)])

    def prep_weights(self, l):
        S = self.S
        self.arena_reset()
        st32 = [self.arena(f"ws32_{i}", [8192]) for i in range(2)]
        st16 = [self.arena(f"ws16_{i}", [8192], BF16) for i in range(2)]
        cnt = [0]

        def block(src2d, kc_n, c0, nb, dst_blk):
            i = cnt[0] % 2
            cnt[0] += 1
            s32, r32 = st32[i]
            s16, r16 = st16[i]
            n = kc_n * nb
            S.dma("sp", f"ws32_{i}", lambda e: e.dma_start(
                out=s32[:, 0:n].rearrange("p (k n) -> p k n", k=kc_n),
                in_=src2d[:, c0:c0 + nb].rearrange("(k p) n -> p k n", p=128)), writes=[r32])
            eng = ["dve", "act"][cnt[0] % 2]
            if eng == "act":
                S.op("act", lambda e: e.copy(out=s16[:, 0:n], in_=s32[:, 0:n]), reads=[r32], writes=[r16])
            else:
                S.op("dve", lambda e: e.tensor_copy(out=s16[:, 0:n], in_=s32[:, 0:n]), reads=[r32], writes=[r16])
            S.dma("pool", f"ws16_{i}", lambda e: e.dma_start(out=dst_blk, in_=s16[:, 0:n]), reads=[r16],
                  writes=[("wsc", l)])

        for (g, off, n, ori, nb) in WIN_GROUPS:
            for j in range(n // nb):
                block(self.w_in[l], 8, off + j * nb, nb, self.wsc[(l, g)][j])
        for n_ in range(4):
            block(self.w_branch[l, n_], 4, 0, 1024, self.wsc[(l, "br")][n_])
        for j in range(2):
            block(self.w_out[l], 8, j * 512, 512, self.wsc[(l, "out")][j])
        for j in range(8):
            block(self.w_ff1[l], 8, j * 512, 512, self.wsc[(l, "ff1")][j])
        for j in range(4):
            block(self.w_ff2[l], 32, j * 256, 256, self.wsc[(l, "ff2")][j])

    def layer(self, l):
        self.load_layer_params(l)
        self.phase1(l)
        if "stop1" in self.debug:
            return
        self.phase2(l)
        if "stop2" in self.debug:
            return
        self.phase3(l)

    def load_layer_params(self, l):
        S = self.S
        self.arena_reset()
        S.dma("sp", "pp", lambda e: e.dma_start(out=self.pp[:], in_=self.pp_in[l]), writes=["pp"])
        S.dma("sp", "rb", lambda e: e.dma_start(out=self.rb[:], in_=self.rb_in[l:l + 1, :].partition_broadcast(128)),
              writes=["rb"])
        t32, r32 = self.arena("lp32", [512])
        for (src, dst, name, rows) in ((self.sguw_in, self.sguw, "sguw", 128), (self.poolw_in, self.poolw, "poolw", 128),
                                       (self.w2_in, self.w2b, "w2b", 33)):
            S.dma("sp", "lp32", lambda e, src=src, rows=rows: e.dma_start(out=t32[0:rows, :], in_=src[l]), writes=[r32])
            S.op("dve", lambda e, dst=dst, rows=rows: e.tensor_copy(out=dst[0:rows, :], in_=t32[0:rows, :]),
                 reads=[r32], writes=[name])
        S.op("act", lambda e: e.activation(out=self.anegt[:], in_=self.rb[:, RB_ALOG:RB_ALOG + 16], func=AF.Exp),
             reads=["rb"], writes=["anegt"])
        S.op("dve", lambda e: e.tensor_scalar(out=self.anegt[:], in0=self.anegt[:], scalar1=-1.0, scalar2=None,
                                              op0=ALU.mult), reads=["anegt"], writes=["anegt"])

    def rstd_from_sq(self, sq, sqr, nk, rstd, rstdr, dim):
        S = self.S
        pt, pr = self.ps()
        for kc in range(nk):
            S.op("pe", lambda e, kc=kc: e.matmul(pt[:], lhsT=self.onesb, rhs=sq[:, kc, :], start=(kc == 0),
                                                 stop=(kc == nk - 1)),
                 reads=[(sqr, k_) for k_ in range(nk)] + ["cstb"], writes=[pr], inc=(kc == nk - 1))
        S.op("dve", lambda e: e.tensor_scalar(out=rstd, in0=pt[:], scalar1=1.0 / dim, scalar2=EPS, op0=ALU.mult,
                                              op1=ALU.add), reads=[pr], writes=[rstdr])
        S.op("act", lambda e: e.activation(out=rstd, in_=rstd, func=AF.Sqrt), reads=[rstdr], writes=[rstdr])
        S.op("dve", lambda e: e.reciprocal(out=rstd, in_=rstd), reads=[rstdr], writes=[rstdr])

    def phase1(self, l):
        S = self.S
        self.arena_reset()
        TT = 1024
        NH = TT // 512
        xT = [self.arena(f"xT{i}", [8, TT]) for i in range(1)]
        sq, sqr = self.arena("sq", [8, TT], BF16)
        hT, hTr = self.arena("hT", [8, TT], BF16)
        rstd, rstdr = self.arena("rstd", [TT])
        wb = [self.arena(f"wb{i}", [8192], BF16) for i in range(4)]
        NWB = len(wb)
        stg = [self.arena(f"stg{i}", [4, 512], BF16) for i in range(4)]
        stg32 = [self.arena(f"stgf{i}", [4, 16]) for i in range(2)]
        wcnt = [0]
        scnt = [0]
        pp = self.pp

        def pad_off(t0):
            for s in range(len(self.seqs)):
                if self.soff[s] <= t0 < self.soff[s] + self.seqs[s]:
                    return self.poff[s] + (t0 - self.soff[s])
            raise AssertionError
        evac_funcs = {"u": AF.Gelu_apprx_tanh, "v": AF.Gelu_apprx_tanh, "r": AF.Silu, "z": AF.Silu, "gate": AF.Sigmoid}
        assert self.T % TT == 0
        for ti in range(self.T // TT):
            t0 = ti * TT
            tp0 = pad_off(t0)
            xt, xr = xT[0]
            S.dma("sp", "xT0", lambda e: e.dma_start(
                out=xt, in_=self.XT[:, :, t0:t0 + TT].rearrange("k p t -> p k t")), writes=[xr])
            for hf in range(NH):
                hs = slice(hf * 512, (hf + 1) * 512)
                for kc in range(8):
                    S.op("act", lambda e: e.activation(out=sq[:, kc, hs], in_=xt[:, kc, hs], func=AF.Square),
                         reads=[xr], writes=[(sqr, kc, hf)])
                pt, pr = self.ps()
                for kc in range(8):
                    S.op("pe", lambda e: e.matmul(pt[:], lhsT=self.onesb, rhs=sq[:, kc, hs], start=(kc == 0), stop=(kc == 7)),
                         reads=[(sqr, k_, hf) for k_ in range(8)] + ["cstb"], writes=[pr], inc=(kc == 7))
                rr = (rstdr, hf)
                S.op("dve", lambda e: e.tensor_scalar(out=rstd[:, hs], in0=pt[:], scalar1=1.0 / D, scalar2=EPS, op0=ALU.mult, op1=ALU.add),
                     reads=[pr], writes=[rr])
                S.op("act", lambda e: e.activation(out=rstd[:, hs], in_=rstd[:, hs], func=AF.Sqrt), reads=[rr], writes=[rr])
                S.op("dve", lambda e: e.reciprocal(out=rstd[:, hs], in_=rstd[:, hs]), reads=[rr], writes=[rr])
                for kc in range(8):
                    S.op("dve", lambda e: e.scalar_tensor_tensor(
                        out=hT[:, kc, hs], in0=xt[:, kc, hs], scalar=pp[:, PP_G1 + kc:PP_G1 + kc + 1], in1=rstd[:, hs],
                        op0=ALU.mult, op1=ALU.mult), reads=[xr, rr, "pp"], writes=[(hTr, kc, hf)])
            for (g, off, n, ori, nb) in WIN_GROUPS:
                for j in range(n // nb):
                    w, wr = wb[wcnt[0] % NWB]
                    slot = f"wb{wcnt[0] % NWB}"
                    wcnt[0] += 1
                    S.dma("sp", slot, lambda e: e.dma_start(out=w[:, 0:8 * nb], in_=self.wsc[(l, g)][j]), writes=[wr])
                    wv = w[:, 0:8 * nb].rearrange("p (k n) -> p k n", k=8)
                    for hf in range(NH):
                        hs = slice(hf * 512, (hf + 1) * 512)
                        hreads = [(hTr, kc, hf) for kc in range(8)]
                        th0 = t0 + hf * 512
                        tph0 = tp0 + hf * 512
                        if ori == "fm":
                            ncc = max(1, nb // 128)
                            m = min(128, nb)
                            st, sr = stg[scnt[0] % 4]
                            sslot = f"stg{scnt[0] % 4}"
                            scnt[0] += 1
                            fn = evac_funcs.get(g)
                            for cc in range(ncc):
                                pt, pr = self.ps()
                                for kc in range(8):
                                    S.op("pe", lambda e: e.matmul(
                                        pt[0:m, :], lhsT=wv[:, kc, cc * 128:cc * 128 + m], rhs=hT[:, kc, hs],
                                        start=(kc == 0), stop=(kc == 7)), reads=[wr] + hreads, writes=[pr], inc=(kc == 7))
                                if fn is not None:
                                    S.op("act", lambda e: e.activation(out=st[0:m, cc, :], in_=pt[0:m, :], func=fn), reads=[pr], writes=[(sr, cc)])
                                elif g == "q":
                                    S.op("act", lambda e: e.mul(out=st[0:m, cc, :], in_=pt[0:m, :], mul=0.125), reads=[pr], writes=[(sr, cc)])
                                else:
                                    S.op("dve", lambda e: e.tensor_copy(out=st[0:m, cc, :], in_=pt[0:m, :]), reads=[pr], writes=[(sr, cc)])
                            c0 = j * ncc
                            if g == "p":
                                dst = self.P["p"][c0:c0 + ncc, :, tph0:tph0 + 512].rearrange("c p t -> p c t")
                                src = st[:, 0:ncc, :]
                            elif g == "glr":
                                dst = self.P["glr"][:, th0:th0 + 512]
                                src = st[0:m, 0, :]
                            else:
                                dst = self.P[g][c0:c0 + ncc, :, th0:th0 + 512].rearrange("c p t -> p c t")
                                src = st[:, 0:ncc, :]
                            S.dma("pool", sslot, lambda e: e.dma_start(out=dst, in_=src), reads=[(sr, cc) for cc in range(ncc)],
                                  writes=[("P" + g, ti)])
                        else:
                            if g == "dt":
                                st, sr = stg32[scnt[0] % 2]
                                sslot = f"stgf{scnt[0] % 2}"
                            else:
                                st, sr = stg[scnt[0] % 4]
                                sslot = f"stg{scnt[0] % 4}"
                            scnt[0] += 1
                            fn = evac_funcs.get(g)
                            for i in range(4):
                                pt, pr = self.ps()
                                ts_ = slice(hf * 512 + i * 128, hf * 512 + (i + 1) * 128)
                                for kc in range(8):
                                    S.op("pe", lambda e: e.matmul(
                                        pt[:, 0:nb], lhsT=hT[:, kc, ts_], rhs=wv[:, kc, :],
                                        start=(kc == 0), stop=(kc == 7)), reads=[wr] + hreads, writes=[pr], inc=(kc == 7))
                                if fn is not None:
                                    S.op("act", lambda e: e.activation(out=st[:, i, 0:nb], in_=pt[:, 0:nb], func=fn), reads=[pr], writes=[(sr, i)])
                                else:
                                    S.op("dve", lambda e: e.tensor_copy(out=st[:, i, 0:nb], in_=pt[:, 0:nb]), reads=[pr], writes=[(sr, i)])
                            if g == "xbc":
                                dst = self.P["xbc"][tph0:tph0 + 512, j * nb:(j + 1) * nb].rearrange("(i p) n -> p i n", p=128)
                            else:
                                dst = self.P[g][th0:th0 + 512, j * nb:(j + 1) * nb].rearrange("(i p) n -> p i n", p=128)
                            S.dma("pool", sslot, lambda e: e.dma_start(out=dst, in_=st[:, :, 0:nb]), reads=[(sr, i) for i in range(4)],
                                  writes=[("P" + g, ti)])

    def phase2(self, l):
        S = self.S
        self.arena_reset()
        self.psn = 0
        ar = self.arena
        rb, pp = self.rb, self.pp
        NCMAX = max(self.seqs) // 128
        Sst_f, Sst_fr = ar("Sst_f", [NCMAX, 256], BF16)
        Sgl_f, Sgl_fr = ar("Sgl_f", [NCMAX, 256], BF16)
        Sssd, Sssdr = ar("Sssd", [256])
        Sgla, Sglar = ar("Sgla", [2, 128])
        Sin_b, Sin_br = ar("Sin_b", [256], BF16)
        Gin_b, Gin_br = ar("Gin_b", [2, 128], BF16)
        xs0 = [ar(f"xs_{k}", [768], BF16) for k in range(5)]
        xs = [xs0, xs0]
        acc, accr = ar("acc", [768])
        ctmp, ctmpr = ar("ctmp", [768])
        xa, xar = ar("xa", [768], BF16)
        dtr = [ar(f"dtr{i}", [16]) for i in range(2)]
        dtt, dttr = ar("dtt", [16])
        dta, dtar = ar("dta", [16])
        ndta, ndtar = ar("ndta", [16])
        sc, scr = ar("sc", [32])
        d1, d1r = ar("d1", [16])
        dte, dter = ar("dte", [16])
        wst, wstr = ar("wst", [16])
        etot, etotr = ar("etot", [16])
        etsel, etselr = ar("etsel", [4])
        eacs, eacsr = ar("eacs", [16])
        xw, xwr = ar("xw", [512], BF16)
        xdt = {X: ar(f"xdt{X}", [512], BF16) for X in "fb"}
        dtaU = {X: ar(f"dtaU{X}", [8, 128]) for X in "fb"}
        BCT, BCTr = ar("BCT", [2, 128], BF16)
        CBT, CBTr = ar("CBT", [2, 128], BF16)
        Cblk, Cblkr = ar("Cblk", [2, 128], BF16)
        Sblk = {X: ar(f"Sblk{X}", [2, 256], BF16) for X in "fb"}
        Qblk, Qblkr = ar("Qblk", [4, 2, 128], BF16)
        Et = [ar(f"E{i}", [4, 128], BF16) for i in range(2)]
        Mt = {(X, g): ar(f"M{X}{g}", [4, 128], BF16) for X in "fb" for g in range(2)}
        y1, y1r = ar("y1", [512])
        y2, y2r = ar("y2", [512])
        zt = [ar(f"zt{i}", [512], BF16) for i in range(2)]
        junk, junkr = ar("junk", [512])
        ss2, ss2r = ar("ss2", [2])
        ya, yar = ar("ya", [512], BF16)
        brs = [ar(f"brs{i}", [4, 128], BF16) for i in range(4)]
        glr = [ar(f"glr{i}", [128], BF16) for i in range(2)]
        e1, e1r = ar("e1", [512])
        sp_, spr = ar("sp", [512])
        ebt, ebtr = ar("ebt", [4])
        ek, ekr = ar("ek", [4, 128])
        eq, eqr = ar("eq", [4, 128])
        kT = [ar(f"kT{i}", [2, 128], BF16) for i in range(2)]
        qT = [ar(f"qT{i}", [2, 128], BF16) for i in range(2)]
        vtm = [ar(f"vtm{i}", [512], BF16) for i in range(2)]
        rT = [ar(f"rT{i}", [4, 128], BF16) for i in range(2)]
        kinv, kinvr = ar("kinv", [4, 128], BF16)
        kend, kendr = ar("kend", [4, 128], BF16)
        qdec, qdecr = ar("qdec", [4, 128], BF16)
        kendtm, kendtmr = ar("kendtm", [4, 128], BF16)
        attm = {X: ar(f"attm{X}", [4, 128], BF16) for X in "fb"}
        osq, osqr = ar("osq", [512], BF16)
        orstd, orstdr = ar("orstd", [512])
        o1, o1r = ar("o1", [512])
        rg, rgr = ar("rg", [4, 128])
        vt = [ar(f"vt{i}", [512], BF16) for i in range(2)]
        uT = [ar(f"uT{i}", [4, 128], BF16) for i in range(2)]
        ss1, ss1r = ar("ss1", [2])
        vf, vfr = ar("vf", [512], BF16)
        sg1, sg1r = ar("sg1", [4, 128])
        ptl = [ar(f"ptl{i}", [4, 144], BF16) for i in range(2)]
        a1, a1r = ar("a1", [3, 144])
        a2, a2r = ar("a2", [2, 144])
        a3, a3r = ar("a3", [144])
        wsum, wsumr = ar("wsum", [4, 128])
        pooled, pooledr = ar("pooled", [4, 128], BF16)
        ldc = [0]
        brc = [0]
        V = lambda fn, r=(), w=(): S.op("dve", fn, reads=r, writes=w)
        A = lambda fn, r=(), w=(): S.op("act", fn, reads=r, writes=w)
        PE = lambda fn, r=(), w=(), inc=True: S.op("pe", fn, reads=r, writes=w, inc=inc)
        cw = lambda k: rb[:, RB_CONVW + k * 768:RB_CONVW + (k + 1) * 768]
        for i in range(2):
            g_, gr_ = glr[i]
            V(lambda e, g_=g_: e.memset(g_[32:33, :], 1.0), w=[gr_ + "one"])

        def bc(ap2, n):
            return ap2.unsqueeze(2).to_broadcast([ap2.shape[0], ap2.shape[1], n])

        def store_branch(n, src, srcr, tok0):
            S.dma("pool", "brs" + srcr, lambda e: e.dma_start(
                out=self.BR[n, :, :, tok0:tok0 + 128].rearrange("w p t -> p w t"), in_=src), reads=[srcr], writes=[("BR", n, tok0)])

        def next_brs():
            b = brs[brc[0] % 4]
            brc[0] += 1
            return b

        def ssd_common(tp, tk, i):
            for k in range(5):
                x_, xr_ = xs[i][k]
                S.dma("sp", f"xs{i}_{k}", lambda e, x_=x_, k=k: e.dma_start(out=x_, in_=self.P["xbc"][tp + k - 2:tp + k - 2 + 128, :]), writes=[xr_])
            d_, dr_ = dtr[i]
            S.dma("sp", f"dtr{i}", lambda e: e.dma_start(out=d_, in_=self.P["dt"][tk:tk + 128, :]), writes=[dr_])
            V(lambda e: e.tensor_tensor(out=acc, in0=xs[i][0][0], in1=cw(0), op=ALU.mult), [xs[i][0][1], "rb"], [accr])
            for k in range(1, 5):
                V(lambda e, k=k: e.tensor_tensor(out=ctmp, in0=xs[i][k][0], in1=cw(k), op=ALU.mult), [xs[i][k][1], "rb"], [ctmpr])
                V(lambda e: e.tensor_tensor(out=acc, in0=acc, in1=ctmp, op=ALU.add), [accr, ctmpr], [accr])
            V(lambda e: e.tensor_tensor(out=acc, in0=acc, in1=rb[:, RB_CONVB:RB_CONVB + 768], op=ALU.add), [accr, "rb"], [accr])
            A(lambda e: e.activation(out=xa, in_=acc, func=AF.Silu), [accr], [xar])
            if "s1" in self.debug:
                return
            V(lambda e: e.tensor_tensor(out=dtt, in0=d_, in1=rb[:, RB_DTB:RB_DTB + 16], op=ALU.add), [dr_, "rb"], [dttr])
            A(lambda e: e.activation(out=dtt, in_=dtt, func=AF.Exp), [dttr], [dttr])
            A(lambda e: e.activation(out=dtt, in_=dtt, func=AF.Ln, bias=1.0), [dttr], [dttr])
            V(lambda e: e.tensor_tensor(out=dta, in0=dtt, in1=self.anegt[:], op=ALU.mult), [dttr, "anegt"], [dtar])
            if "s2" in self.debug:
                return
            pt, pr = self.ps()
            PE(lambda e: e.matmul(pt[:, 0:8], lhsT=self.Uf, rhs=dta[:, 0:8], start=True, stop=True), [dtar, "cst"], [pr], inc=False)
            PE(lambda e: e.matmul(pt[:, 8:16], lhsT=self.Lf, rhs=dta[:, 8:16], start=True, stop=True), [dtar, "cst"], [pr], inc=False)
            PE(lambda e: e.matmul(pt[:, 16:32], lhsT=self.onesf, rhs=dta[:, 0:16], start=True, stop=True), [dtar, "cst"], [pr])
            V(lambda e: e.tensor_copy(out=sc, in_=pt[:, 0:32]), [pr], [scr])
            V(lambda e: e.tensor_tensor(out=d1, in0=sc[:, 16:32], in1=sc[:, 0:16], op=ALU.subtract), [scr], [d1r])
            A(lambda e: e.activation(out=dte, in_=d1, func=AF.Exp), [d1r], [dter])
            A(lambda e: e.activation(out=etot, in_=sc[:, 16:32], func=AF.Exp), [scr], [etotr])
            V(lambda e: e.tensor_tensor(out=wst, in0=dtt, in1=dte, op=ALU.mult), [dttr, dter], [wstr])

        def ssd_state_update(X, c, store):
            if "s1" in self.debug or "s2" in self.debug or "s3" in self.debug:
                return
            xo = 0 if X == "f" else 8
            V(lambda e: e.tensor_tensor(out=xw.rearrange("p (h d) -> p h d", h=8), in0=xa[:, 0:512].rearrange("p (h d) -> p h d", h=8),
                                        in1=bc(wst[:, xo:xo + 8], 64), op=ALU.mult), [xar, wstr], [xwr])
            pt, pr = self.ps()
            PE(lambda e: e.matmul(pt[:], lhsT=xa[:, 512:640], rhs=xw, start=True, stop=True), [xar, xwr], [pr])
            if store:
                V(lambda e: e.tensor_copy(out=Sst_f[:, c, :], in_=Sssd), [Sssdr], [(Sst_fr, c)])
            V(lambda e: e.tensor_scalar(out=etsel, in0=etot[:, xo:xo + 4], scalar1=self.cst[:, 1856:1857], scalar2=None, op0=ALU.mult), [etotr, "cst"], [etselr])
            V(lambda e: e.scalar_tensor_tensor(out=etsel, in0=etot[:, xo + 4:xo + 8], scalar=self.cst[:, 1857:1858], in1=etsel, op0=ALU.mult, op1=ALU.add),
              [etotr, etselr, "cst"], [etselr])
            V(lambda e: e.tensor_tensor(out=Sssd.rearrange("p (h d) -> p h d", h=4), in0=Sssd.rearrange("p (h d) -> p h d", h=4), in1=bc(etsel, 64), op=ALU.mult),
              [Sssdr, etselr], [Sssdr])
            for g in range(2):
                V(lambda e, g=g, pt=pt: e.scalar_tensor_tensor(out=Sssd, in0=pt[:, g * 256:(g + 1) * 256], scalar=self.cst[:, 1856 + g:1857 + g], in1=Sssd,
                                                               op0=ALU.mult, op1=ALU.add), [Sssdr, pr, "cst"], [Sssdr])

        def gla_common(tk, dirs, i):
            g_, gr_ = glr[i]
            S.dma("sp", f"glr{i}", lambda e: e.dma_start(out=g_[0:32, :], in_=self.P["glr"][:, tk:tk + 128]), writes=[gr_])
            k_, kr_ = kT[i]
            S.dma("sp", f"kT{i}", lambda e: e.dma_start(out=k_, in_=self.P["k"][:, :, tk:tk + 128].rearrange("h p t -> p h t")), writes=[kr_])
            v_, vr_ = vtm[i]
            S.dma("sp", f"vtm{i}", lambda e: e.dma_start(out=v_, in_=self.P["gv"][tk:tk + 128, :]), writes=[vr_])
            pg, pgr = self.ps()
            PE(lambda e: e.matmul(pg[:], lhsT=g_[0:33, :], rhs=self.w2b[0:33, :], start=True, stop=True), [gr_, gr_ + "one", "w2b"], [pgr])
            A(lambda e: e.activation(out=e1, in_=pg[:], func=AF.Exp, scale=-1.0), [pgr], [e1r])
            A(lambda e: e.activation(out=sp_, in_=e1, func=AF.Ln, bias=1.0), [e1r], [spr])
            pb, pbr = self.ps()
            pb3 = pb[:].rearrange("p (a t) -> p a t", a=4)
            combos = [(X, hp) for X in dirs for hp in range(2)]
            for n_, (X, hp) in enumerate(combos):
                xi = 0 if X == "f" else 1
                PE(lambda e, X=X, hp=hp, xi=xi: e.matmul(pb3[:, xi * 2 + hp, :], lhsT=sp_[:, xi * 256 + hp * 128:xi * 256 + (hp + 1) * 128],
                                                         rhs=self.trineg[X], start=True, stop=True), [spr, "cst"], [pbr], inc=(n_ == len(combos) - 1))
            for X in dirs:
                xi = 0 if X == "f" else 1
                col = 127 if X == "f" else 0
                A(lambda e, xi=xi, col=col: e.activation(out=ebt[:, xi * 2:xi * 2 + 2], in_=pb3[:, xi * 2:xi * 2 + 2, col], func=AF.Exp), [pbr], [ebtr + X])
            lo, hi = (0, 2) if dirs == "f" else (0, 4)
            A(lambda e: e.activation(out=ek[:, lo:hi, :], in_=pb3[:, lo:hi, :], func=AF.Exp, scale=-1.0), [pbr], [ekr])
            if dirs == "fb":
                A(lambda e: e.activation(out=eq, in_=pb3, func=AF.Exp), [pbr], [eqr])
            nx = hi // 2
            V(lambda e: e.tensor_tensor(out=kinv[:, lo:hi, :].rearrange("p (x h) t -> p x h t", x=nx),
                                        in0=ek[:, lo:hi, :].rearrange("p (x h) t -> p x h t", x=nx),
                                        in1=k_.unsqueeze(1).to_broadcast([128, nx, 2, 128]), op=ALU.mult), [ekr, kr_], [kinvr])
            V(lambda e: e.tensor_tensor(out=kend[:, lo:hi, :], in0=kinv[:, lo:hi, :], in1=bc(ebt[:, lo:hi], 128), op=ALU.mult),
              [kinvr] + [ebtr + X for X in dirs], [kendr])
            ptb, ptr_ = self.ps()
            ptb3 = ptb[:].bitcast(BF16)[:, 0:512].rearrange("p (a t) -> p a t", a=4)
            for a_ in range(lo, hi):
                PE(lambda e, a_=a_: e.transpose(ptb3[:, a_, :], kend[:, a_, :], self.identb), [kendr, "cstb"], [ptr_], inc=(a_ == hi - 1))
            V(lambda e: e.tensor_copy(out=kendtm[:, lo:hi, :], in_=ptb3[:, lo:hi, :]), [ptr_], [kendtmr])
            return i

        def gla_state_update(X, c, i, store):
            xi = 0 if X == "f" else 1
            v_, vr_ = vtm[i]
            pt, pr = self.ps()
            for hp in range(2):
                PE(lambda e, hp=hp: e.matmul(pt[:, hp * 256:(hp + 1) * 256], lhsT=kendtm[:, xi * 2 + hp, :], rhs=v_[:, hp * 256:(hp + 1) * 256],
                                             start=True, stop=True), [kendtmr, vr_], [pr], inc=(hp == 1))
            if store:
                V(lambda e: e.tensor_copy(out=Sgl_f[:, c, :].rearrange("p (h v) -> p h v", h=2), in_=Sgla), [Sglar], [(Sgl_fr, c)])
            V(lambda e: e.tensor_tensor(out=Sgla, in0=Sgla, in1=bc(ebt[:, xi * 2:xi * 2 + 2], 128), op=ALU.mult), [Sglar, ebtr + X], [Sglar])
            p4 = pt[:].rearrange("p (a b v) -> p a b v", a=2, b=2)
            for hh in range(2):
                V(lambda e, hh=hh: e.scalar_tensor_tensor(out=Sgla, in0=p4[:, :, hh, :], scalar=self.cst[:, 1856 + hh:1857 + hh], in1=Sgla,
                                                          op0=ALU.mult, op1=ALU.add), [Sglar, pr, "cst"], [Sglar])

        for s, Ls in enumerate(self.seqs):
            nch = Ls // 128
            V(lambda e: e.memset(Sssd, 0.0), w=[Sssdr])
            V(lambda e: e.memset(Sgla, 0.0), w=[Sglar])
            for c in range(nch):
                tp = self.poff[s] + c * 128
                tk = self.soff[s] + c * 128
                i = c % 2

                def fA_ssd():
                    ssd_common(tp, tk, i)
                    ssd_state_update("f", c, True)

                def fA_gla():
                    gla_common(tk, "f", i)
                    gla_state_update("f", c, i, True)
                interleave(S, [fA_ssd, fA_gla])
            V(lambda e: e.memset(Sssd, 0.0), w=[Sssdr])
            V(lambda e: e.memset(Sgla, 0.0), w=[Sglar])
            for c in range(nch - 1, -1, -1):
                tp = self.poff[s] + c * 128
                tk = self.soff[s] + c * 128
                i = c % 2

                def fB_ssd():
                    ssd_common(tp, tk, i)
                    li = i
                    z_, zr_ = zt[li]
                    S.dma("sp", f"zt{li}", lambda e, z_=z_, tk=tk: e.dma_start(out=z_, in_=self.P["z"][tk:tk + 128, :]), writes=[zr_])
                    A(lambda e: e.activation(out=eacs, in_=sc[:, 0:16], func=AF.Exp), [scr], [eacsr])
                    V(lambda e: e.tensor_scalar(out=ndta, in0=dta, scalar1=-1.0, scalar2=None, op0=ALU.mult), [dtar], [ndtar])
                    for X in "fb":
                        xo = 0 if X == "f" else 8
                        xd, xdr = xdt[X]
                        V(lambda e, xd=xd, xo=xo: e.tensor_tensor(out=xd.rearrange("p (h d) -> p h d", h=8), in0=xa[:, 0:512].rearrange("p (h d) -> p h d", h=8),
                                                                  in1=bc(dtt[:, xo:xo + 8], 64), op=ALU.mult), [xar, dttr], [xdr])
                        du, dur = dtaU[X]
                        V(lambda e, du=du, xo=xo, X=X: e.tensor_tensor(out=du, in0=self.tri[X].unsqueeze(1).to_broadcast([128, 8, 128]),
                                                                       in1=bc(dta[:, xo:xo + 8], 128), op=ALU.mult), [dtar, "cst"], [dur])
                    ptb, ptr_ = self.ps()
                    ptb3 = ptb[:].bitcast(BF16)[:, 0:256].rearrange("p (a t) -> p a t", a=2)
                    PE(lambda e: e.transpose(ptb3[:, 0, :], xa[:, 512:640], self.identb), [xar, "cstb"], [ptr_], inc=False)
                    PE(lambda e: e.transpose(ptb3[:, 1, :], xa[:, 640:768], self.identb), [xar, "cstb"], [ptr_])
                    V(lambda e: e.tensor_copy(out=BCT, in_=ptb3), [ptr_], [BCTr])
                    pcb, pcbr = self.ps()
                    for g in range(2):
                        V(lambda e, g=g: e.tensor_scalar(out=Cblk[:, g, :], in0=BCT[:, 1, :], scalar1=self.cst[:, 1856 + g:1857 + g], scalar2=None, op0=ALU.mult),
                          [BCTr, "cst"], [Cblkr])
                    PE(lambda e: e.matmul(pcb[:, 0:256], lhsT=BCT[:, 0, :], rhs=Cblk.rearrange("p a t -> p (a t)"), start=True, stop=True), [BCTr, Cblkr], [pcbr])
                    V(lambda e: e.tensor_copy(out=CBT, in_=pcb[:, 0:256].rearrange("p (a t) -> p a t", a=2)), [pcbr], [CBTr])
                    ei = 0
                    for X in "fb":
                        xo = 0 if X == "f" else 8
                        du, dur = dtaU[X]
                        for g in range(2):
                            pseg, psegr = self.ps()
                            pseg3 = pseg[:].rearrange("p (h t) -> p h t", h=4)
                            PE(lambda e, g=g, du=du, pseg3=pseg3: e.matmul(pseg3, lhsT=self.onesf, rhs=du[:, g * 4:(g + 1) * 4, :], start=True, stop=False),
                               [dur, "cst"], [psegr], inc=False)
                            PE(lambda e, g=g, X=X, xo=xo, pseg3=pseg3: e.matmul(pseg3, lhsT=self.tri[X], rhs=bc(ndta[:, xo + g * 4:xo + g * 4 + 4], 128),
                                                                               start=False, stop=False), [ndtar, "cst"], [psegr], inc=False)
                            PE(lambda e, X=X, pseg=pseg: e.matmul(pseg[:], lhsT=self.identb, rhs=self.negm[X], start=False, stop=True), ["cstb"], [psegr])
                            E_, Er_ = Et[ei % 2]
                            ei += 1
                            A(lambda e, E_=E_, pseg3=pseg3: e.activation(out=E_, in_=pseg3, func=AF.Exp), [psegr], [Er_])
                            M_, Mr_ = Mt[(X, g)]
                            V(lambda e, M_=M_, E_=E_, g=g: e.tensor_tensor(out=M_, in0=E_, in1=CBT[:, g, :].unsqueeze(1).to_broadcast([128, 4, 128]), op=ALU.mult),
                              [Er_, CBTr], [Mr_])
                    pyd, pydr = self.ps()
                    for h in range(8):
                        g, h4 = h // 4, h % 4
                        for X in "fb":
                            PE(lambda e, h=h, g=g, h4=h4, X=X: e.matmul(pyd[:, h * 64:(h + 1) * 64], lhsT=Mt[(X, g)][0][:, h4, :], rhs=xdt[X][0][:, h * 64:(h + 1) * 64],
                                                                        start=(X == "f"), stop=(X == "b")), [Mt[(X, g)][1], xdt[X][1]], [pydr], inc=(h == 7 and X == "b"))
                    for g in range(2):
                        V(lambda e, g=g: e.tensor_scalar(out=Sblk["f"][0][:, g, :], in0=Sst_f[:, c, :], scalar1=self.cst[:, 1856 + g:1857 + g], scalar2=None, op0=ALU.mult),
                          [(Sst_fr, c), "cst"], [Sblk["f"][1]])
                        V(lambda e, g=g: e.tensor_scalar(out=Sblk["b"][0][:, g, :], in0=Sssd, scalar1=self.cst[:, 1856 + g:1857 + g], scalar2=None, op0=ALU.mult),
                          [Sssdr, "cst"], [Sblk["b"][1]])
                    pyo = {}
                    for X in "fb":
                        p_, pr_ = self.ps()
                        pyo[X] = (p_, pr_)
                        PE(lambda e, p_=p_, X=X: e.matmul(p_[:], lhsT=BCT[:, 1, :], rhs=Sblk[X][0].rearrange("p a t -> p (a t)"), start=True, stop=True),
                           [BCTr, Sblk[X][1]], [pr_])
                    h8 = lambda ap: ap.rearrange("p (h d) -> p h d", h=8)
                    V(lambda e: e.tensor_tensor(out=h8(y1), in0=h8(pyo["f"][0][:]), in1=bc(eacs[:, 0:8], 64), op=ALU.mult), [pyo["f"][1], eacsr], [y1r])
                    V(lambda e: e.tensor_tensor(out=h8(y2), in0=h8(pyo["b"][0][:]), in1=bc(eacs[:, 8:16], 64), op=ALU.mult), [pyo["b"][1], eacsr], [y2r])
                    V(lambda e: e.tensor_tensor(out=y1, in0=y1, in1=y2, op=ALU.add), [y1r, y2r], [y1r])
                    V(lambda e: e.tensor_tensor(out=h8(y2), in0=h8(xa[:, 0:512]), in1=bc(rb[:, RB_D:RB_D + 8], 64), op=ALU.mult), [xar, "rb"], [y2r])
                    V(lambda e: e.tensor_tensor(out=y1, in0=y1, in1=y2, op=ALU.add), [y1r, y2r], [y1r])
                    V(lambda e: e.tensor_tensor(out=y1, in0=y1, in1=pyd[:], op=ALU.add), [y1r, pydr], [y1r])
                    V(lambda e: e.tensor_tensor(out=y1, in0=y1, in1=z_, op=ALU.mult), [y1r, zr_], [y1r])
                    for g in range(2):
                        A(lambda e, g=g: e.activation(out=junk[:, g * 256:(g + 1) * 256], in_=y1[:, g * 256:(g + 1) * 256], func=AF.Square,
                                                      accum_out=ss2[:, g:g + 1]), [y1r], [junkr, ss2r])
                    V(lambda e: e.tensor_scalar(out=ss2, in0=ss2, scalar1=1.0 / 256, scalar2=EPS, op0=ALU.mult, op1=ALU.add), [ss2r], [ss2r])
                    A(lambda e: e.activation(out=ss2, in_=ss2, func=AF.Sqrt), [ss2r], [ss2r])
                    V(lambda e: e.reciprocal(out=ss2, in_=ss2), [ss2r], [ss2r])
                    for g in range(2):
                        V(lambda e, g=g: e.scalar_tensor_tensor(out=ya[:, g * 256:(g + 1) * 256], in0=y1[:, g * 256:(g + 1) * 256], scalar=ss2[:, g:g + 1],
                                                                in1=rb[:, RB_SSDN + g * 256:RB_SSDN + (g + 1) * 256], op0=ALU.mult, op1=ALU.mult),
                          [y1r, ss2r, "rb"], [yar])
                    ptb, ptr_ = self.ps()
                    ptb3 = ptb[:].bitcast(BF16)[:, 0:512].rearrange("p (a t) -> p a t", a=4)
                    for a_ in range(4):
                        PE(lambda e, a_=a_, ptb3=ptb3: e.transpose(ptb3[:, a_, :], ya[:, a_ * 128:(a_ + 1) * 128], self.identb), [yar, "cstb"], [ptr_], inc=(a_ == 3))
                    b_, br_ = next_brs()
                    V(lambda e, b_=b_, ptb3=ptb3: e.tensor_copy(out=b_, in_=ptb3), [ptr_], [br_])
                    store_branch(0, b_, br_, tk)
                    ssd_state_update("b", c, False)

                def fB_gla():
                    gla_common(tk, "fb", i)
                    q_, qr_ = qT[i]
                    S.dma("sp", f"qT{i}", lambda e, q_=q_, tk=tk: e.dma_start(out=q_, in_=self.P["q"][:, :, tk:tk + 128].rearrange("h p t -> p h t")), writes=[qr_])
                    r_, rr_ = rT[i]
                    S.dma("sp", f"rT{i}", lambda e, r_=r_, tk=tk: e.dma_start(out=r_, in_=self.P["r"][:, :, tk:tk + 128].rearrange("h p t -> p h t")), writes=[rr_])
                    v_, vr_ = vtm[i]
                    V(lambda e, q_=q_: e.tensor_tensor(out=qdec.rearrange("p (x h) t -> p x h t", x=2), in0=eq.rearrange("p (x h) t -> p x h t", x=2),
                                                       in1=q_.unsqueeze(1).to_broadcast([128, 2, 2, 128]), op=ALU.mult), [eqr, qr_], [qdecr])
                    for hh in range(2):
                        V(lambda e, hh=hh: e.tensor_scalar(out=Qblk[:, :, hh, :], in0=qdec, scalar1=self.cst[:, 1856 + hh:1857 + hh], scalar2=None, op0=ALU.mult),
                          [qdecr, "cst"], [Qblkr])
                    for X in "fb":
                        xi = 0 if X == "f" else 1
                        pa, par = self.ps()
                        pa3 = pa[:].rearrange("p (h t) -> p h t", h=4)
                        for hp in range(2):
                            PE(lambda e, hp=hp, xi=xi, pa3=pa3: e.matmul(pa3[:, hp * 2:(hp + 1) * 2, :], lhsT=kinv[:, xi * 2 + hp, :], rhs=Qblk[:, xi * 2 + hp, :, :],
                                                                         start=True, stop=True), [kinvr, Qblkr], [par], inc=(hp == 1))
                        am, amr = attm[X]
                        V(lambda e, am=am, pa3=pa3, X=X: e.tensor_tensor(out=am, in0=pa3, in1=self.m01[X].unsqueeze(1).to_broadcast([128, 4, 128]), op=ALU.mult),
                          [par, "cstb"], [amr])
                    V(lambda e: e.tensor_copy(out=Gin_b, in_=Sgla), [Sglar], [Gin_br])
                    po, por = self.ps()
                    po3 = po[:].rearrange("p (h t) -> p h t", h=4)
                    for h in range(4):
                        hp, hh = h // 2, h % 2
                        rs = slice(hh * 64, (hh + 1) * 64)
                        PE(lambda e, h=h: e.matmul(po3[:, h, :], lhsT=v_[:, h * 128:(h + 1) * 128], rhs=attm["f"][0][:, h, :], start=True, stop=False),
                           [vr_, attm["f"][1]], [por], inc=False)
                        PE(lambda e, h=h: e.matmul(po3[:, h, :], lhsT=v_[:, h * 128:(h + 1) * 128], rhs=attm["b"][0][:, h, :], start=False, stop=False),
                           [vr_, attm["b"][1]], [por], inc=False)
                        PE(lambda e, h=h, hp=hp, hh=hh: e.matmul(po3[:, h, :], lhsT=Sgl_f[:, c, hp * 128:(hp + 1) * 128], rhs=Qblk[:, hp, hh, :], start=False, stop=False),
                           [(Sgl_fr, c), Qblkr], [por], inc=False)
                        PE(lambda e, h=h, hp=hp, hh=hh: e.matmul(po3[:, h, :], lhsT=Gin_b[:, hp, :], rhs=Qblk[:, 2 + hp, hh, :], start=False, stop=True),
                           [Gin_br, Qblkr], [por], inc=(h == 3))
                    A(lambda e: e.activation(out=osq, in_=po[:], func=AF.Square), [por], [osqr])
                    pss, pssr = self.ps()
                    PE(lambda e: e.matmul(pss[:], lhsT=self.onesb, rhs=osq, start=True, stop=True), [osqr, "cstb"], [pssr])
                    V(lambda e: e.tensor_scalar(out=orstd, in0=pss[:], scalar1=1.0 / 128, scalar2=EPS, op0=ALU.mult, op1=ALU.add), [pssr], [orstdr])
                    A(lambda e: e.activation(out=orstd, in_=orstd, func=AF.Sqrt), [orstdr], [orstdr])
                    V(lambda e: e.reciprocal(out=orstd, in_=orstd), [orstdr], [orstdr])
                    V(lambda e: e.tensor_tensor(out=o1, in0=po[:], in1=orstd, op=ALU.mult), [por, orstdr], [o1r])
                    V(lambda e, r_=r_: e.tensor_tensor(out=rg, in0=r_, in1=bc(pp[:, PP_GLAN:PP_GLAN + 4], 128), op=ALU.mult), [rr_, "pp"], [rgr])
                    b_, br_ = next_brs()
                    V(lambda e, b_=b_: e.tensor_tensor(out=b_, in0=o1.rearrange("p (h t) -> p h t", h=4), in1=rg, op=ALU.mult), [o1r, rgr], [br_])
                    store_branch(3, b_, br_, tk)
                    gla_state_update("b", c, i, False)

                def fB_sgu():
                    vv, vvr = vt[i]
                    S.dma("sp", f"vt{i}", lambda e, vv=vv, tk=tk: e.dma_start(out=vv, in_=self.P["v"][tk:tk + 128, :]), writes=[vvr])
                    uu, uur = uT[i]
                    S.dma("sp", f"uT{i}", lambda e, uu=uu, tk=tk: e.dma_start(out=uu, in_=self.P["u"][:, :, tk:tk + 128].rearrange("h p t -> p h t")), writes=[uur])
                    A(lambda e, vv=vv: e.activation(out=junk, in_=vv, func=AF.Square, accum_out=ss1[:, 0:1]), [vvr], [junkr, ss1r])
                    V(lambda e: e.tensor_scalar(out=ss1[:, 0:1], in0=ss1[:, 0:1], scalar1=1.0 / 512, scalar2=EPS, op0=ALU.mult, op1=ALU.add), [ss1r], [ss1r])
                    A(lambda e: e.activation(out=ss1[:, 0:1], in_=ss1[:, 0:1], func=AF.Sqrt), [ss1r], [ss1r])
                    V(lambda e: e.reciprocal(out=ss1[:, 0:1], in_=ss1[:, 0:1]), [ss1r], [ss1r])
                    V(lambda e, vv=vv: e.scalar_tensor_tensor(out=vf, in0=vv, scalar=ss1[:, 0:1], in1=rb[:, RB_SGUN:RB_SGUN + 512], op0=ALU.mult, op1=ALU.mult),
                      [vvr, ss1r, "rb"], [vfr])
                    psg, psgr = self.ps()
                    for g in range(4):
                        PE(lambda e, g=g, psg=psg: e.matmul(psg[:, g * 128:(g + 1) * 128], lhsT=vf[:, g * 128:(g + 1) * 128], rhs=self.sguw[:, g * 128:(g + 1) * 128],
                                                            start=True, stop=True), [vfr, "sguw"], [psgr], inc=(g == 3))
                    V(lambda e, psg=psg: e.tensor_tensor(out=sg1.rearrange("p h t -> p (h t)"), in0=psg[:], in1=rb[:, RB_SGUB:RB_SGUB + 512], op=ALU.add), [psgr, "rb"], [sg1r])
                    b_, br_ = next_brs()
                    V(lambda e, b_=b_, uu=uu: e.tensor_tensor(out=b_, in0=sg1, in1=uu, op=ALU.mult), [sg1r, uur], [br_])
                    store_branch(2, b_, br_, tk)

                def fB_pool():
                    pl, plr = ptl[i]
                    S.dma("sp", f"ptl{i}", lambda e, pl=pl, tp=tp: e.dma_start(out=pl, in_=self.P["p"][:, :, tp - 8:tp + 136].rearrange("g p t -> p g t")), writes=[plr])
                    V(lambda e, pl=pl: e.tensor_tensor(out=wsum[:, 0, :], in0=pl[:, 0, 7:135], in1=pl[:, 0, 8:136], op=ALU.add), [plr], [wsumr + "0"])
                    V(lambda e, pl=pl: e.tensor_tensor(out=a1[:, :, 1:144], in0=pl[:, 1:4, 0:143], in1=pl[:, 1:4, 1:144], op=ALU.add), [plr], [a1r])
                    V(lambda e: e.tensor_tensor(out=wsum[:, 1, :], in0=a1[:, 0, 7:135], in1=a1[:, 0, 9:137], op=ALU.add), [a1r], [wsumr + "1"])
                    V(lambda e: e.tensor_tensor(out=a2[:, :, 2:143], in0=a1[:, 1:3, 1:142], in1=a1[:, 1:3, 3:144], op=ALU.add), [a1r], [a2r])
                    V(lambda e: e.tensor_tensor(out=wsum[:, 2, :], in0=a2[:, 0, 6:134], in1=a2[:, 0, 10:138], op=ALU.add), [a2r], [wsumr + "2"])
                    V(lambda e: e.tensor_tensor(out=a3[:, 4:141], in0=a2[:, 1, 2:139], in1=a2[:, 1, 6:143], op=ALU.add), [a2r], [a3r])
                    V(lambda e: e.tensor_tensor(out=wsum[:, 3, :], in0=a3[:, 4:132], in1=a3[:, 12:140], op=ALU.add), [a3r], [wsumr + "3"])
                    wregs = [wsumr + str(g) for g in range(4)]
                    corr = self.cst[:, 1792:1856].rearrange("p (e g t) -> p e g t", e=2, g=4)
                    if c == 0:
                        V(lambda e: e.tensor_tensor(out=wsum[:, :, 0:8], in0=wsum[:, :, 0:8], in1=corr[:, 0], op=ALU.mult), wregs + ["cst"], wregs)
                    if c == nch - 1:
                        V(lambda e: e.tensor_tensor(out=wsum[:, :, 120:128], in0=wsum[:, :, 120:128], in1=corr[:, 1], op=ALU.mult), wregs + ["cst"], wregs)
                    for g, w_ in enumerate((2, 4, 8, 16)):
                        V(lambda e, g=g, w_=w_, pl=pl: e.scalar_tensor_tensor(out=pooled[:, g, :], in0=wsum[:, g, :], scalar=1.0 / w_, in1=pl[:, g, 8:136],
                                                                             op0=ALU.mult, op1=ALU.subtract), [wsumr + str(g), plr], [pooledr + str(g)])
                    ppl, pplr = self.ps()
                    for g in range(4):
                        PE(lambda e, g=g, ppl=ppl: e.matmul(ppl[:, g * 128:(g + 1) * 128], lhsT=self.poolw[:, g * 128:(g + 1) * 128], rhs=pooled[:, g, :],
                                                            start=True, stop=True), [pooledr + str(g), "poolw"], [pplr], inc=(g == 3))
                    b_, br_ = next_brs()
                    for g in range(4):
                        V(lambda e, g=g, b_=b_, ppl=ppl: e.tensor_scalar(out=b_[:, g, :], in0=ppl[:, g * 128:(g + 1) * 128], scalar1=pp[:, PP_PSC + g:PP_PSC + g + 1],
                                                                         scalar2=None, op0=ALU.mult), [pplr, "pp"], [br_])
                    S.dma("pool", "brs" + br_, lambda e, b_=b_, tk=tk: e.dma_start(
                        out=self.BR[1, :, :, tk:tk + 128].rearrange("w p t -> p w t"), in_=b_), reads=[br_], writes=[("BR", 1, tk)])
                interleave(S, [fB_ssd, fB_gla, fB_sgu, fB_pool])

    def phase3(self, l):
        S = self.S
        self.arena_reset()
        TT = 512
        pp = self.pp
        xT = [self.arena(f"xT{i}", [8, TT]) for i in range(2)]
        a16, a16r = self.arena("a16", [8, TT], BF16)
        mo, mor = self.arena("mo", [8, TT])
        sq, sqr = self.arena("sq", [8, TT], BF16)
        rstd, rstdr = self.arena("rstd", [TT])
        hid, hidr = self.arena("hid", [32, TT], BF16)
        wb = [self.arena(f"wb{i}", [8192], BF16) for i in range(2)]
        brt = [self.arena(f"brt{i}", [4, TT], BF16) for i in range(2)]
        gt = [self.arena(f"gt{i}", [8, TT], BF16) for i in range(2)]
        tmp = [self.arena(f"tmp{i}", [TT]) for i in range(2)]
        rl = [self.arena(f"rl{i}", [TT], BF16) for i in range(2)]
        wc = [0]
        gc = [0]
        tc = [0]

        def wload(key, j, n):
            w, wr = wb[wc[0] % 2]
            slot = f"wb{wc[0] % 2}"
            wc[0] += 1
            S.dma("sp", slot, lambda e: e.dma_start(out=w[:, 0:n], in_=self.wsc[(l, key)][j]), writes=[wr])
            return w, wr

        def post_norm_residual(xt, xr, gcol):
            self.rstd_from_sq(sq, sqr, 8, rstd, rstdr, D)
            for ec in range(8):
                t_, tr_ = tmp[tc[0] % 2]
                tc[0] += 1
                S.op("dve", lambda e, ec=ec, t_=t_: e.scalar_tensor_tensor(
                    out=t_, in0=mo[:, ec, :], scalar=pp[:, gcol + ec:gcol + ec + 1], in1=rstd, op0=ALU.mult, op1=ALU.mult),
                    reads=[(mor, ec), rstdr, "pp"], writes=[tr_])
                S.op("dve", lambda e, ec=ec, t_=t_: e.tensor_tensor(out=xt[:, ec, :], in0=xt[:, ec, :], in1=t_, op=ALU.add),
                     reads=[tr_, (xr, ec)], writes=[(xr, ec)])

        for ti in range(self.T // TT):
            t0 = ti * TT
            xt, xr = xT[ti % 2]
            S.dma("sp", f"xT{ti % 2}", lambda e, xt=xt, t0=t0: e.dma_start(
                out=xt, in_=self.XT[:, :, t0:t0 + TT].rearrange("k p t -> p k t")),
                writes=[(xr, k) for k in range(8)])
            for n in range(4):
                w, wr = wload("br", n, 4096)
                wv = w[:, 0:4096].rearrange("p (k n) -> p k n", k=4)
                bt, btr = brt[n % 2]
                S.dma("sp", f"brt{n % 2}", lambda e, bt=bt, n=n, t0=t0: e.dma_start(
                    out=bt, in_=self.BR[n, :, :, t0:t0 + TT].rearrange("w p t -> p w t")), writes=[btr])
                g8, gr_ = gt[n % 2]
                S.dma("sp", f"gt{n % 2}", lambda e: e.dma_start(
                    out=g8, in_=self.P["gate"][n * 8:(n + 1) * 8, :, t0:t0 + TT].rearrange("c p t -> p c t")), writes=[gr_])
                for dc in range(8):
                    g_ = g8[:, dc, :]
                    pt, pr = self.ps()
                    for k in range(4):
                        S.op("pe", lambda e, pt=pt, wv=wv, k=k, dc=dc, bt=bt: e.matmul(
                            pt[:], lhsT=wv[:, k, dc * 128:(dc + 1) * 128], rhs=bt[:, k, :], start=(k == 0), stop=(k == 3)),
                            reads=[wr, btr], writes=[pr], inc=(k == 3))
                    if n == 0:
                        S.op("dve", lambda e, pt=pt, g_=g_, dc=dc: e.tensor_tensor(out=mo[:, dc, :], in0=pt[:], in1=g_, op=ALU.mult),
                             reads=[pr, gr_], writes=[(mor, dc)])
                    else:
                        t_, tr_ = tmp[tc[0] % 2]
                        tc[0] += 1
                        S.op("dve", lambda e, pt=pt, g_=g_, t_=t_: e.tensor_tensor(out=t_, in0=pt[:], in1=g_, op=ALU.mult),
                             reads=[pr, gr_], writes=[tr_])
                        if n < 3:
                            S.op("dve", lambda e, t_=t_, dc=dc: e.tensor_tensor(out=mo[:, dc, :], in0=mo[:, dc, :], in1=t_, op=ALU.add),
                                 reads=[tr_, (mor, dc)], writes=[(mor, dc)])
                        else:
                            S.op("dve", lambda e, t_=t_, dc=dc: e.tensor_tensor(out=a16[:, dc, :], in0=mo[:, dc, :], in1=t_, op=ALU.add),
                                 reads=[tr_, (mor, dc)], writes=[(a16r, dc)])
            areads = [(a16r, k) for k in range(8)]
            for j in range(2):
                w, wr = wload("out", j, 4096)
                wv = w[:, 0:4096].rearrange("p (k n) -> p k n", k=8)
                for cc in range(4):
                    ec = j * 4 + cc
                    pt, pr = self.ps()
                    for k in range(8):
                        S.op("pe", lambda e, pt=pt, wv=wv, k=k, cc=cc: e.matmul(
                            pt[:], lhsT=wv[:, k, cc * 128:(cc + 1) * 128], rhs=a16[:, k, :], start=(k == 0), stop=(k == 7)),
                            reads=[wr] + areads, writes=[pr], inc=(k == 7))
                    S.op("dve", lambda e, pt=pt, ec=ec: e.tensor_copy(out=mo[:, ec, :], in_=pt[:]), reads=[pr], writes=[(mor, ec)])
                    S.op("act", lambda e, ec=ec: e.activation(out=sq[:, ec, :], in_=mo[:, ec, :], func=AF.Square),
                         reads=[(mor, ec)], writes=[(sqr, ec)])
            post_norm_residual(xt, xr, PP_GPOST)
            for kc in range(8):
                S.op("act", lambda e, kc=kc, xt=xt: e.activation(out=sq[:, kc, :], in_=xt[:, kc, :], func=AF.Square),
                     reads=[(xr, kc)], writes=[(sqr, kc)])
            self.rstd_from_sq(sq, sqr, 8, rstd, rstdr, D)
            for kc in range(8):
                S.op("dve", lambda e, kc=kc, xt=xt: e.scalar_tensor_tensor(
                    out=a16[:, kc, :], in0=xt[:, kc, :], scalar=pp[:, PP_GF1 + kc:PP_GF1 + kc + 1], in1=rstd,
                    op0=ALU.mult, op1=ALU.mult), reads=[(xr, kc), rstdr, "pp"], writes=[(a16r, kc)])
            for j in range(8):
                w, wr = wload("ff1", j, 4096)
                wv = w[:, 0:4096].rearrange("p (k n) -> p k n", k=8)
                for cc in range(4):
                    fc = j * 4 + cc
                    pt, pr = self.ps()
                    for k in range(8):
                        S.op("pe", lambda e, pt=pt, wv=wv, k=k, cc=cc: e.matmul(
                            pt[:], lhsT=wv[:, k, cc * 128:(cc + 1) * 128], rhs=a16[:, k, :], start=(k == 0), stop=(k == 7)),
                            reads=[wr] + areads, writes=[pr], inc=(k == 7))
                    r_, rr_ = rl[fc % 2]
                    S.op("act", lambda e, pt=pt, r_=r_: e.activation(out=r_, in_=pt[:], func=AF.Relu), reads=[pr], writes=[rr_])
                    S.op("dve", lambda e, r_=r_, fc=fc: e.tensor_tensor(out=hid[:, fc, :], in0=r_, in1=r_, op=ALU.mult),
                         reads=[rr_], writes=[(hidr, fc)])
            hreads = [(hidr, f) for f in range(32)]
            for j in range(4):
                w, wr = wload("ff2", j, 8192)
                wv = w[:, 0:8192].rearrange("p (k n) -> p k n", k=32)
                for cc in range(2):
                    ec = j * 2 + cc
                    pt, pr = self.ps()
                    for k in range(32):
                        S.op("pe", lambda e, pt=pt, wv=wv, k=k, cc=cc: e.matmul(
                            pt[:], lhsT=wv[:, k, cc * 128:(cc + 1) * 128], rhs=hid[:, k, :], start=(k == 0), stop=(k == 31)),
                            reads=[wr] + hreads, writes=[pr], inc=(k == 31))
                    S.op("dve", lambda e, pt=pt, ec=ec: e.tensor_copy(out=mo[:, ec, :], in_=pt[:]), reads=[pr], writes=[(mor, ec)])
                    S.op("act", lambda e, ec=ec: e.activation(out=sq[:, ec, :], in_=mo[:, ec, :], func=AF.Square),
                         reads=[(mor, ec)], writes=[(sqr, ec)])
            post_norm_residual(xt, xr, PP_GF2)
            S.dma("pool", f"xTs{ti % 2}", lambda e, xt=xt, t0=t0: e.dma_start(
                out=self.XT[:, :, t0:t0 + TT].rearrange("k p t -> p k t"), in_=xt),
                reads=[(xr, k) for k in range(8)], writes=[("XT", ti)])

    def phase_final(self):
        S = self.S
        self.arena_reset()
        xin = [self.arena(f"fx{i}", [8, 512]) for i in range(2)]
        xo = [self.arena(f"fo{i}", [4, D]) for i in range(2)]
        for ti in range(self.T // 512):
            xi, xir = xin[ti % 2]
            xot, xor_ = xo[ti % 2]
            t0 = ti * 512
            S.dma("sp", f"fx{ti % 2}", lambda e, xi=xi, t0=t0: e.dma_start(
                out=xi, in_=self.XT[:, :, t0:t0 + 512].rearrange("k p t -> p k t")), reads=[("XT", ti)], writes=[xir])
            for j in range(4):
                for half in range(2):
                    pt, pr = self.ps()
                    for q in range(4):
                        kc = half * 4 + q
                        S.op("pe", lambda e, pt=pt, xi=xi, j=j, kc=kc, q=q: e.transpose(
                            pt[:, q * 128:(q + 1) * 128], xi[:, kc, j * 128:(j + 1) * 128], self.identf),
                            reads=[xir, "cst"], writes=[pr], inc=(q == 3))
                    if half == 0:
                        S.op("dve", lambda e, pt=pt, xot=xot, j=j: e.tensor_copy(out=xot[:, j, 0:512], in_=pt[:]),
                             reads=[pr], writes=[(xor_, j, 0)])
                    else:
                        S.op("act", lambda e, pt=pt, xot=xot, j=j: e.copy(out=xot[:, j, 512:1024], in_=pt[:]),
                             reads=[pr], writes=[(xor_, j, 1)])
            S.dma("pool", f"fo{ti % 2}", lambda e, xot=xot, t0=t0: e.dma_start(
                out=self.y_out[t0:t0 + 512, :].rearrange("(j p) d -> p j d", p=128), in_=xot),
                reads=[(xor_, j, h) for j in range(4) for h in range(2)], writes=[("Y", ti)])


def make_consts():
    r = np.arange(128)
    ident = np.eye(128, dtype=np.float32)
    ones = np.ones((128, 128), np.float32)
    U = (r[:, None] <= r[None, :]).astype(np.float32)
    Lm = (r[:, None] >= r[None, :]).astype(np.float32)
    Uneg = U * (-1.0 / 16.0)
    Lneg = Lm * (-1.0 / 16.0)
    m01f, m01b = U.copy(), Lm.copy()
    negf = np.tile((1.0 - U) * -30000.0, (1, 4)).astype(np.float32)
    negb = np.tile((1.0 - Lm) * -30000.0, (1, 4)).astype(np.float32)
    corr = np.ones((2, 4, 8), np.float32)
    for g, w in enumerate((2, 4, 8, 16)):
        for t in range(8):
            if t < w // 2:
                corr[0, g, t] = w / (t + w // 2)
            tq = 7 - t
            if tq < w // 2 - 1:
                corr[1, g, t] = w / (tq + 1 + w // 2)
    corr = np.broadcast_to(corr.reshape(1, 64), (128, 64))
    half = np.stack([(r < 64), (r >= 64)], axis=1).astype(np.float32)
    cst = np.concatenate([ident, ones, U, Lm, Uneg, Lneg, negf, negb, corr, half], axis=1)
    pad = np.zeros((128, 1860 - cst.shape[1]), np.float32)
    return np.ascontiguousarray(np.concatenate([cst, pad], axis=1))


def pack_params(inputs, depth):
    L = depth
    f = lambda k: np.asarray(inputs[k], np.float32)
    pp = np.zeros((L, 128, PP_LEN), np.float32)
    for l in range(L):
        pp[l, :, PP_G1:PP_G1 + 8] = f("norm_mix_pre")[l].reshape(8, 128).T
        pp[l, :, PP_GPOST:PP_GPOST + 8] = f("norm_mix_post")[l].reshape(8, 128).T
        pp[l, :, PP_GF1:PP_GF1 + 8] = f("norm_ffn_pre")[l].reshape(8, 128).T
        pp[l, :, PP_GF2:PP_GF2 + 8] = f("norm_ffn_post")[l].reshape(8, 128).T
        pp[l, :, PP_PSC:PP_PSC + 4] = f("pool_scale")[l].reshape(4, 128).T
        pp[l, :, PP_GLAN:PP_GLAN + 4] = f("gla_norm")[l].reshape(4, 128).T
    rb = np.zeros((L, RB_LEN), np.float32)
    for l in range(L):
        rb[l, RB_CONVW:RB_CONVW + 3840] = f("ssd_conv_w")[l].reshape(-1)
        rb[l, RB_CONVB:RB_CONVB + 768] = f("ssd_conv_b")[l]
        rb[l, RB_DTB:RB_DTB + 16] = f("ssd_dt_bias")[l].reshape(-1)
        rb[l, RB_ALOG:RB_ALOG + 16] = f("ssd_a_log")[l].reshape(-1)
        rb[l, RB_D:RB_D + 8] = f("ssd_d")[l]
        rb[l, RB_SSDN:RB_SSDN + 512] = f("ssd_norm")[l]
        rb[l, RB_SGUN:RB_SGUN + 512] = f("sgu_norm")[l]
        rb[l, RB_SGUB:RB_SGUB + 512] = f("sgu_b")[l].reshape(-1)
    sguwT = np.ascontiguousarray(f("sgu_w")[:L].transpose(0, 3, 1, 2).reshape(L, 128, 512))
    poolw = np.ascontiguousarray(f("pool_w")[:L].transpose(0, 2, 1, 3).reshape(L, 128, 512))
    w2 = np.zeros((L, 33, 512), np.float32)
    w2[:, 0:16, 0:256] = f("gla_gate_w2")[:L, 0]
    w2[:, 16:32, 256:512] = f("gla_gate_w2")[:L, 1]
    w2[:, 32, :] = f("gla_gate_b")[:L].reshape(L, 512)
    return pp, rb, sguwT, poolw, w2


_CACHE = {}


def run(inputs, seqs_per_core, x_cores, depth, debug=()):
    key = (tuple(seqs_per_core), depth, tuple(debug))
    if key not in _CACHE:
        _CACHE[key] = Builder(seqs_per_core, depth, debug).build()
    nc = _CACHE[key]
    pp, rb, sguwT, poolw, w2 = pack_params(inputs, depth)
    f = lambda k: np.ascontiguousarray(np.asarray(inputs[k], np.float32)[:depth])
    shared = {"w_in": f("w_in"), "w_branch": f("w_branch"), "w_out": f("w_out"), "w_ff1": f("w_ff1"), "w_ff2": f("w_ff2"),
              "pp": pp, "rb": rb, "sguwT": sguwT, "poolw": poolw, "w2blk": w2, "cst": make_consts()}
    in_maps = [dict(shared, x=np.ascontiguousarray(xc)) for xc in x_cores]
    res = run_bass_kernel_spmd(nc, in_maps, core_ids=list(range(len(x_cores))))
    return res.results


def kernel(**inputs):
    xp = np.asarray(inputs["x_prompt"], np.float32)
    xs = np.asarray(inputs["x_sample"], np.float32)
    x_cores = []
    for c in range(NCORES):
        x_cores.append(np.concatenate([xp[2 * c], xp[2 * c + 1], xs[c]], axis=0))
    res = run(inputs, [2048, 2048, 8192], x_cores, DEPTH)
    yp = np.zeros_like(xp)
    ys = np.zeros_like(xs)
    for c in range(NCORES):
        y = res[c]["y"]
        yp[2 * c] = y[0:2048]
        yp[2 * c + 1] = y[2048:4096]
        ys[c] = y[4096:]
    return (yp, ys)
```
